# Optimizing a Trainium2 kernel written in Bass

```python
import math
import jax, jax.numpy as jnp
from jax import lax
import numpy as np

D_MODEL = 1024
BATCH = 8
SEQ = 4096
DEPTH = 2

HEAD_DIM = 64
RWKV_HEADS = 6
RWKV_WIDTH = RWKV_HEADS * HEAD_DIM
SB_HEADS = 6
SB_WIDTH = SB_HEADS * HEAD_DIM
POOL_WINDOWS = (2, 4, 8, 16)
POOL_GROUPS = len(POOL_WINDOWS)
POOL_WIDTH = D_MODEL - RWKV_WIDTH - SB_WIDTH
POOL_GROUP_DIM = POOL_WIDTH // POOL_GROUPS
DECAY_LORA = 64
AAA_LORA = 64
GATE_LORA = 128
MV_LORA = 32
RWKV_COLS = 3 * RWKV_WIDTH + DECAY_LORA + AAA_LORA + GATE_LORA
SB_COLS = 3 * SB_WIDTH
IN_COLS = RWKV_COLS + SB_COLS + POOL_WIDTH
D_FF = 2816
CONV_WIDTH = 3
Q_BLOCK = 128
NORM_EPS = 1e-6
LN_X_EPS = 64e-5
L2_EPS = 1e-12

kernel_name = "hymba_rwkv7_stickbreak_pool_convffn"

F32 = jnp.float32


def rms_norm(x, g, eps=NORM_EPS):
    xf = x.astype(F32)
    y = xf * lax.rsqrt(jnp.mean(xf * xf, axis=-1, keepdims=True) + eps) * g.astype(F32)
    return y.astype(x.dtype)


def token_shift(p):
    return jnp.pad(p, ((0, 0), (1, 0), (0, 0)))[:, :-1]


def wkv7_scan(r, decay, k, v, a_vec, b_vec):
    def step(S, inp):
        r_t, w_t, k_t, v_t, a_t, b_t = inp
        Sa = jnp.einsum('bhij,bhj->bhi', S, a_t)
        S = (S * w_t[:, :, None, :] + Sa[..., :, None] * b_t[..., None, :]
             + v_t[..., :, None] * k_t[..., None, :])
        y = jnp.einsum('bhij,bhj->bhi', S, r_t)
        return S, y
    B, T, H, N = r.shape
    xs = tuple(jnp.moveaxis(t, 1, 0) for t in (r, decay, k, v, a_vec, b_vec))
    S0 = jnp.zeros((B, H, N, N), F32)
    _, ys = lax.scan(step, S0, xs)
    return jnp.moveaxis(ys, 0, 1)


def rwkv7_time_mix(p, mu, w0, w2, a0, a2, g2, k_k, k_a, r_k, lnx_w, lnx_b, v_first, vmix):
    B, T, _ = p.shape
    H, N = RWKV_HEADS, HEAD_DIM
    p = p + (token_shift(p) - p) * mu
    cuts = np.cumsum([RWKV_WIDTH, RWKV_WIDTH, RWKV_WIDTH, DECAY_LORA, AAA_LORA])
    r, k, v, xw, xa, xg = jnp.split(p, cuts, axis=-1)
    w = -jax.nn.softplus(-(w0 + jnp.tanh(xw) @ w2)) - 0.5
    decay = jnp.exp(-jnp.exp(w))
    a = jax.nn.sigmoid(a0 + xa @ a2)
    g = jax.nn.sigmoid(xg) @ g2
    if vmix is None:
        v_first = v
    else:
        v0, v1, v2 = vmix
        v = v + (v_first - v) * jax.nn.sigmoid(v0 + (v @ v1) @ v2)
    kk = (k * k_k).reshape(B, T, H, N)
    kk = kk / jnp.maximum(jnp.sqrt(jnp.sum(kk * kk, axis=-1, keepdims=True)), L2_EPS)
    k = k * (1.0 + (a - 1.0) * k_a)
    rh, kh, vh = (t.reshape(B, T, H, N) for t in (r, k, v))
    ah = a.reshape(B, T, H, N)
    y = wkv7_scan(rh, decay.reshape(B, T, H, N), kh, vh, -kk, kk * ah)
    mean = jnp.mean(y, axis=-1, keepdims=True)
    var = jnp.mean(jnp.square(y - mean), axis=-1, keepdims=True)
    y = ((y - mean) * lax.rsqrt(var + LN_X_EPS)).reshape(B, T, RWKV_WIDTH) * lnx_w + lnx_b
    bonus = jnp.sum(rh * kh * r_k, axis=-1, keepdims=True) * vh
    y = (y + bonus.reshape(B, T, RWKV_WIDTH)) * g
    return y, v_first


def stick_breaking_attention(q, k, v, q_gain, k_gain):
    q = rms_norm(q, q_gain)
    k = rms_norm(k, k_gain)
    q, k, v = (jnp.transpose(t, (0, 2, 1, 3)) for t in (q, k, v))
    T = q.shape[2]
    scale = HEAD_DIM ** -0.5
    outs = []
    for t0 in range(0, T, Q_BLOCK):
        lk = t0 + Q_BLOCK
        qb, kp, vp = q[:, :, t0:lk], k[:, :, :lk], v[:, :, :lk]
        z = jnp.einsum('bhqd,bhkd->bhqk', qb, kp) * scale
        causal = jnp.arange(lk)[None, :] < (t0 + jnp.arange(Q_BLOCK))[:, None]
        log_not = jnp.where(causal, jax.nn.log_sigmoid(-z), 0.0)
        log_remaining = lax.cumsum(log_not, axis=3, reverse=True) - log_not
        A = jnp.where(causal, jnp.exp(jax.nn.log_sigmoid(z) + log_remaining), 0.0)
        outs.append(jnp.einsum('bhqk,bhkd->bhqd', A, vp))
    o = jnp.concatenate(outs, axis=2)
    B, H, _, N = o.shape
    return jnp.transpose(o, (0, 2, 1, 3)).reshape(B, T, H * N)


def multiscale_pool(u, lin_w, lin_b, scale):
    B, T, _ = u.shape
    ug = u.reshape(B, T, POOL_GROUPS, POOL_GROUP_DIM)
    cs = jnp.pad(jnp.cumsum(ug, axis=1), ((0, 0), (1, 0), (0, 0), (0, 0)))
    pos = jnp.arange(T)
    pooled = []
    for g, w in enumerate(POOL_WINDOWS):
        start = jnp.maximum(pos + 1 - w, 0)
        win_sum = cs[:, 1:, g] - cs[:, start, g]
        count = jnp.minimum(pos + 1, w).astype(F32)[:, None]
        pooled.append(win_sum / count)
    pooled = jnp.stack(pooled, axis=2) - ug
    y = jnp.einsum('btgc,gcd->btgd', pooled, lin_w) + lin_b
    return y.reshape(B, T, POOL_WIDTH) * scale


def conv_glu_ffn(h, w_up, conv_w, conv_b, w_down):
    T = h.shape[1]
    up = h @ w_up
    padded = jnp.pad(up, ((0, 0), (CONV_WIDTH - 1, 0), (0, 0)))
    c = conv_b
    for i in range(CONV_WIDTH):
        c = c + padded[:, i:i + T] * conv_w[i]
    gate, val = jnp.split(c, 2, axis=-1)
    return (jax.nn.silu(gate) * val) @ w_down


def setup_inputs(seed: int = 0) -> dict:
    key = jax.random.key(seed)
    ks = jax.random.split(key, 32)
    nrm = lambda k, shape, s: jax.random.normal(k, shape, F32) * s
    L, Lv = DEPTH, DEPTH - 1
    return {
        "x": jax.random.normal(ks[0], (BATCH, SEQ, D_MODEL), F32),
        "ln1_g": 1.0 + nrm(ks[1], (L, D_MODEL), 0.05),
        "w_in": nrm(ks[2], (L, D_MODEL, IN_COLS), D_MODEL ** -0.5),
        "mu_shift": jax.random.uniform(ks[3], (L, RWKV_COLS), F32),
        "w0": jax.random.uniform(ks[4], (L, RWKV_WIDTH), F32, -6.0, 0.0),
        "w2": nrm(ks[5], (L, DECAY_LORA, RWKV_WIDTH), 0.1 * DECAY_LORA ** -0.5),
        "a0": nrm(ks[6], (L, RWKV_WIDTH), 0.1),
        "a2": nrm(ks[7], (L, AAA_LORA, RWKV_WIDTH), AAA_LORA ** -0.5),
        "g2": nrm(ks[8], (L, GATE_LORA, RWKV_WIDTH), GATE_LORA ** -0.5),
        "k_k": 0.85 + nrm(ks[9], (L, RWKV_WIDTH), 0.05),
        "k_a": 1.0 + nrm(ks[10], (L, RWKV_WIDTH), 0.05),
        "r_k": nrm(ks[11], (L, RWKV_HEADS, HEAD_DIM), 0.1),
        "lnx_w": 1.0 + nrm(ks[12], (L, RWKV_WIDTH), 0.05),
        "lnx_b": nrm(ks[13], (L, RWKV_WIDTH), 0.02),
        "v0": 1.0 + nrm(ks[14], (Lv, RWKV_WIDTH), 0.1),
        "v1": nrm(ks[15], (Lv, RWKV_WIDTH, MV_LORA), RWKV_WIDTH ** -0.5),
        "v2": nrm(ks[16], (Lv, MV_LORA, RWKV_WIDTH), MV_LORA ** -0.5),
        "q_gain": 1.0 + nrm(ks[17], (L, HEAD_DIM), 0.05),
        "k_gain": 1.0 + nrm(ks[18], (L, HEAD_DIM), 0.05),
        "pool_w": nrm(ks[19], (L, POOL_GROUPS, POOL_GROUP_DIM, POOL_GROUP_DIM), POOL_GROUP_DIM ** -0.5),
        "pool_b": nrm(ks[20], (L, POOL_GROUPS, POOL_GROUP_DIM), 0.02),
        "pool_scale": 1.0 + nrm(ks[21], (L, POOL_WIDTH), 0.1),
        "w_out": nrm(ks[22], (L, D_MODEL, D_MODEL), D_MODEL ** -0.5),
        "ln2_g": 1.0 + nrm(ks[23], (L, D_MODEL), 0.05),
        "w_up": nrm(ks[24], (L, D_MODEL, 2 * D_FF), D_MODEL ** -0.5),
        "conv_w": nrm(ks[25], (L, CONV_WIDTH, 2 * D_FF), CONV_WIDTH ** -0.5),
        "conv_b": nrm(ks[26], (L, 2 * D_FF), 0.02),
        "w_down": nrm(ks[27], (L, D_FF, D_MODEL), D_FF ** -0.5),
    }


def reference(x, ln1_g, w_in, mu_shift, w0, w2, a0, a2, g2, k_k, k_a, r_k, lnx_w, lnx_b,
              v0, v1, v2, q_gain, k_gain, pool_w, pool_b, pool_scale, w_out, ln2_g,
              w_up, conv_w, conv_b, w_down):
    B, T, _ = x.shape
    v_first = None
    for l in range(DEPTH):
        h = rms_norm(x, ln1_g[l])
        proj = (h @ w_in[l]).astype(F32)
        p_rwkv, p_sb, p_pool = jnp.split(proj, [RWKV_COLS, RWKV_COLS + SB_COLS], axis=-1)
        vmix = None if l == 0 else (v0[l - 1], v1[l - 1], v2[l - 1])
        y_rwkv, v_first = rwkv7_time_mix(p_rwkv, mu_shift[l], w0[l], w2[l], a0[l], a2[l], g2[l],
                                         k_k[l], k_a[l], r_k[l], lnx_w[l], lnx_b[l], v_first, vmix)
        q, k, v = (t.reshape(B, T, SB_HEADS, HEAD_DIM) for t in jnp.split(p_sb, 3, axis=-1))
        y_sb = stick_breaking_attention(q, k, v, q_gain[l], k_gain[l])
        y_pool = multiscale_pool(p_pool, pool_w[l], pool_b[l], pool_scale[l])
        mix = jnp.concatenate([y_rwkv, y_sb, y_pool], axis=-1).astype(x.dtype)
        x = x + mix @ w_out[l]
        x = x + conv_glu_ffn(rms_norm(x, ln2_g[l]), w_up[l], conv_w[l], conv_b[l], w_down[l])
    return x
```

```python
from contextlib import ExitStack
import numpy as np
import ml_dtypes
import concourse.bass as bass
import concourse.mybir as mybir
from concourse.bass_utils import run_bass_kernel_spmd

F32 = mybir.dt.float32
BF16 = mybir.dt.bfloat16
AF = mybir.ActivationFunctionType
ALU = mybir.AluOpType
AX = mybir.AxisListType

D = 1024
NIN = 2816
DFF = 2816
NUP = 5632
TT = 512
CDEC = -float(np.exp(-0.5))

COMPUTE = ("pe", "act", "dve", "pool")
ALLQ = ("pe", "act", "dve", "pool", "sp")


class Reg:
    __slots__ = ("name", "w", "rs")

    def __init__(self, name):
        self.name = name
        self.w = None
        self.rs = {}


class Chan:
    def __init__(self, prog, name):
        self.key = "dma_" + name
        prog.semkeys.append(self.key)
        self.cnt = 0


class Prog:
    def __init__(self, nc):
        self.nc = nc
        self.q = {e: [] for e in ALLQ}
        self.cnt = {e: 0 for e in COMPUTE}
        self.known = {e: {} for e in ALLQ}
        self.semkeys = ["prog_" + e for e in COMPUTE]
        self.chans = []
        self.nins = 0
        self.nwait = 0
        self.dead = False

    def chan(self, name):
        c = Chan(self, name)
        self.chans.append(c)
        return c

    def op(self, eng, fn, reads=(), writes=(), chan=None, nowaw=False):
        if self.dead:
            return None
        need = {}

        def req(k, v):
            if need.get(k, 0) < v:
                need[k] = v

        me = None if chan is not None else eng
        for r in reads:
            if r.w is not None:
                req(r.w[0], r.w[1])
        for w in writes:
            if w.w is not None and (w.w[2] is None or w.w[2] != me) and not nowaw:
                req(w.w[0], w.w[1])
            for k, (v, e) in w.rs.items():
                if e is None or e != me:
                    req(k, v)
        kn = self.known[eng]
        for k, v in need.items():
            if kn.get(k, 0) < v:
                self.q[eng].append(("w", k, v))
                kn[k] = v
                self.nwait += 1
        if chan is not None:
            chan.cnt += 1
            ev = (chan.key, 16 * chan.cnt, None)
            self.q[eng].append(("i", fn, chan.key, 16))
        else:
            self.cnt[eng] += 1
            ev = ("prog_" + eng, self.cnt[eng], eng)
            self.q[eng].append(("i", fn, ev[0], 1))
        self.nins += 1
        for r in reads:
            old = r.rs.get(ev[0])
            if old is None or old[0] < ev[1]:
                r.rs[ev[0]] = (ev[1], ev[2])
        for w in writes:
            w.w = ev
            w.rs = {}
        return ev

    def wait(self, eng, key, val):
        if self.dead:
            return
        kn = self.known[eng]
        if val > 0 and kn.get(key, 0) < val:
            self.q[eng].append(("w", key, val))
            kn[key] = val
            self.nwait += 1

    def barrier(self, engs=COMPUTE, chans=()):
        for e in engs:
            for x in engs:
                if x != e and x in self.cnt:
                    self.wait(e, "prog_" + x, self.cnt[x])
            for c in chans:
                self.wait(e, c.key, 16 * c.cnt)

    def emit(self):
        nc = self.nc
        with ExitStack() as st:
            sems = {k: st.enter_context(nc.semaphore(k)) for k in self.semkeys}
            block = st.enter_context(nc.Block())

            def run(e, items):
                for it in items:
                    if it[0] == "w":
                        e.wait_ge(sems[it[1]], it[2])
                    else:
                        it[1](e).then_inc(sems[it[2]], it[3])

            @block.tensor
            def _(e):
                run(e, self.q["pe"])

            @block.scalar
            def _(e):
                run(e, self.q["act"])

            @block.vector
            def _(e):
                run(e, self.q["dve"])

            @block.gpsimd
            def _(e):
                run(e, self.q["pool"])

            @block.sync
            def _(e):
                run(e, self.q["sp"])


class Tl:
    __slots__ = ("t", "r", "name")

    def __init__(self, t, name):
        self.t = t
        self.name = name
        self.r = Reg(name)


def vec_cols(depth):
    spec = [("ln1_g", 8), ("ln2_g", 8), ("mu", 11), ("w0", 3), ("a0", 3), ("k_k", 3), ("k_a", 3), ("r_k", 3),
            ("lnx_w", 3), ("lnx_b", 3), ("v0", 3), ("q_gain", 1), ("k_gain", 1), ("pool_b", 2),
            ("pool_scale", 2), ("conv_w", 132), ("conv_b", 44)]
    off, o = {}, 0
    for n, c in spec:
        off[n] = o
        o += c
    return off, o


SM_W2A2, SM_G2, SM_V1, SM_V2, SM_PW, SM_N = 0, 384, 768, 864, 1248, 1504
CB_ID, CB_ONES, CB_BONES, CB_MASKL, CB_NTRI, CB_MASK4, CB_AMASK, CB_N = 0, 128, 256, 384, 512, 640, 1152, 3200


def pack_consts():
    c = np.zeros((128, CB_N), np.float32)
    i = np.arange(128)
    c[:, CB_ID:CB_ID + 128] = np.eye(128)
    c[:, CB_ONES:CB_ONES + 128] = 1.0
    c[:, CB_BONES:CB_BONES + 128] = (i[:, None] // 64 == i[None, :] // 64)
    c[:, CB_MASKL:CB_MASKL + 128] = (i[None, :] < i[:, None])
    c[:, CB_NTRI:CB_NTRI + 128] = -(i[:, None] >= i[None, :]).astype(np.float32)
    su = (i[:, None] < i[None, :]).astype(np.float32)
    iu = (i[:, None] <= i[None, :]).astype(np.float32)
    c[:, CB_MASK4:CB_MASK4 + 512] = np.concatenate([su, iu, su, iu], 1)
    q = np.arange(512)
    for d in range(4):
        c[:, CB_AMASK + 512 * d:CB_AMASK + 512 * (d + 1)] = (128 * d + i[:, None] < q[None, :])
    pc = np.ones((128, 2, 16), np.float32)
    wins = (2, 4, 8, 16)
    for ti in range(2):
        for e in range(2):
            w = wins[2 * ti + e]
            t = np.arange(16)
            pc[e * 64:(e + 1) * 64, ti, :] = (w / np.minimum(t + 1, w))[None, :]
    return c.astype(ml_dtypes.bfloat16), pc.reshape(128, 32)


def pack_params(inp, depth):
    off, W = vec_cols(depth)
    vec = np.zeros((128, depth * W), np.float32)
    sm = np.zeros((depth, 128, SM_N), np.float32)

    def put(l, name, v):
        v = np.asarray(v, np.float32).reshape(-1, 128).T
        vec[:, l * W + off[name]: l * W + off[name] + v.shape[1]] = v

    for l in range(depth):
        put(l, "ln1_g", inp["ln1_g"][l]); put(l, "ln2_g", inp["ln2_g"][l]); put(l, "mu", inp["mu_shift"][l])
        for n in ("w0", "a0", "k_k", "k_a", "lnx_w", "lnx_b"):
            put(l, n, inp[n][l])
        put(l, "r_k", inp["r_k"][l].reshape(-1))
        if l > 0:
            put(l, "v0", inp["v0"][l - 1])
        put(l, "q_gain", np.tile(inp["q_gain"][l], 2)); put(l, "k_gain", np.tile(inp["k_gain"][l], 2))
        put(l, "pool_b", inp["pool_b"][l].reshape(-1)); put(l, "pool_scale", inp["pool_scale"][l])
        put(l, "conv_w", inp["conv_w"][l].reshape(-1)); put(l, "conv_b", inp["conv_b"][l])
        sm[l, 0:64, SM_W2A2:SM_W2A2 + 384] = inp["w2"][l]
        sm[l, 64:128, SM_W2A2:SM_W2A2 + 384] = inp["a2"][l]
        sm[l, :, SM_G2:SM_G2 + 384] = inp["g2"][l]
        if l > 0:
            v1 = np.asarray(inp["v1"][l - 1]).reshape(3, 128, 32).transpose(1, 0, 2).reshape(128, 96)
            sm[l, :, SM_V1:SM_V1 + 96] = v1
            sm[l, 0:32, SM_V2:SM_V2 + 384] = inp["v2"][l - 1]
        for ti in range(2):
            for e in range(2):
                g = 2 * ti + e
                sm[l, e * 64:(e + 1) * 64, SM_PW + ti * 128 + e * 64: SM_PW + ti * 128 + (e + 1) * 64] = inp["pool_w"][l, g]
    return vec, sm


class _Stop(Exception):
    pass


def build(T, depth, stop=0):
    nc = bass.Bass("TRN2", target_bir_lowering=False)
    NT = T // TT
    voff, VW = vec_cols(depth)

    def din(name, shape, dt=F32):
        return nc.dram_tensor(name, shape, dt, kind="ExternalInput").ap()

    xT_d = din("xT", [D, T])
    w_in_d = din("w_in", [depth, D, NIN]); w_out_d = din("w_out", [depth, D, D])
    w_up_d = din("w_up", [depth, D, NUP]); w_down_d = din("w_down", [depth, DFF, D])
    vec_d = din("vec", [128, depth * VW]); sm_d = din("sm", [depth, 128, SM_N])
    cb_d = din("cb", [128, CB_N], BF16); pc_d = din("pc", [128, 32])
    yT_d = nc.dram_tensor("yT", [D, T], F32, kind="ExternalOutput").ap()
    if stop:
        dbgf = nc.dram_tensor("dbgf", [16, 128, 512], F32, kind="ExternalOutput").ap()
        dbgb = nc.dram_tensor("dbgb", [16, 128, 512], BF16, kind="ExternalOutput").ap()
    wbi = nc.dram_tensor("wbi", [depth, D, NIN], BF16, kind="Internal").ap()
    wbo = nc.dram_tensor("wbo", [depth, D, D], BF16, kind="Internal").ap()
    wbu = nc.dram_tensor("wbu", [depth, D, NUP], BF16, kind="Internal").ap()
    wbd = nc.dram_tensor("wbd", [depth, DFF, D], BF16, kind="Internal").ap()
    x1_d = nc.dram_tensor("x1s", [D, T], F32, kind="Internal").ap()
    vf_d = nc.dram_tensor("vfs", [384, T], F32, kind="Internal").ap()
    r_wb = Reg("wbscratch"); r_x1 = Reg("x1s"); r_vf = Reg("vfs")

    P = Prog(nc)
    rr = {"i": 0}

    def R(ts):
        return [t.r if isinstance(t, Tl) else t for t in ts]

    def mm(out, lhsT, rhs, start, stop, rd, wr):
        P.op("pe", lambda e: e.matmul(out, lhsT=lhsT, rhs=rhs, start=start, stop=stop), R(rd), R(wr))

    def tr(out, in_, ident, rd, wr):
        P.op("pe", lambda e: e.transpose(out, in_, ident), R(rd), R(wr))

    def act(out, in_, func, rd, wr, bias=None, scale=None):
        kw = {}
        if bias is not None:
            kw["bias"] = bias
        if scale is not None:
            kw["scale"] = scale
        P.op("act", lambda e: e.activation(out=out, in_=in_, func=func, **kw), R(rd), R(wr))

    def tt(eng, out, a, b, op, rd, wr):
        P.op(eng, lambda e: e.tensor_tensor(out=out, in0=a, in1=b, op=op), R(rd), R(wr))

    def stt(eng, out, in0, scalar, in1, op0, op1, rd, wr):
        P.op(eng, lambda e: e.scalar_tensor_tensor(out=out, in0=in0, scalar=scalar, in1=in1, op0=op0, op1=op1), R(rd), R(wr))

    def ts(eng, out, in0, s1, s2, op0, op1, rd, wr):
        if s2 is None:
            P.op(eng, lambda e: e.tensor_scalar(out=out, in0=in0, scalar1=s1, scalar2=None, op0=op0), R(rd), R(wr))
        else:
            P.op(eng, lambda e: e.tensor_scalar(out=out, in0=in0, scalar1=s1, scalar2=s2, op0=op0, op1=op1), R(rd), R(wr))

    def cp(eng, out, in_, rd, wr):
        if eng == "act":
            P.op("act", lambda e: e.activation(out=out, in_=in_, func=AF.Copy), R(rd), R(wr))
        else:
            P.op(eng, lambda e: e.tensor_copy(out=out, in_=in_), R(rd), R(wr))

    def mset(eng, out, val, wr):
        P.op(eng, lambda e: e.memset(out, val), [], R(wr))

    def dma(q, out, in_, chan, rd, wr, nowaw=False):
        return P.op(q, lambda e: e.dma_start(out=out, in_=in_), R(rd), R(wr), chan=chan, nowaw=nowaw)

    dch = {}

    def dump(tl, ap, idx, bf=False, w=512, np_=128):
        if not stop:
            return
        if "c" not in dch:
            dch["c"] = P.chan("dbg")
        dst = (dbgb if bf else dbgf)[idx, 0:np_, 0:w]
        dma("pool", dst, ap, dch["c"], [tl], [])

    def stage(k):
        if stop == k:
            P.dead = True

    def alt():
        rr["i"] += 1
        return "act" if rr["i"] % 2 else "dve"

    with ExitStack() as G:
        uid = {"i": 0}

        def sb(st, name, shape, dt=F32):
            uid["i"] += 1
            nm = f"s{uid['i']}_{name}"
            return Tl(st.enter_context(nc.sbuf_tensor(nm, shape, dt)), nm)

        banks = [Tl(G.enter_context(nc.psum_tensor(f"pb{i}", [128, 512], F32)), f"pb{i}") for i in range(6)]
        obank = Tl(G.enter_context(nc.psum_tensor("pob", [128, 512], F32)), "pob")
        tbank = Tl(G.enter_context(nc.psum_tensor("ptb", [128, 1024], BF16)), "ptb")
        bk = {"i": 0}

        def bank():
            bk["i"] += 1
            return banks[bk["i"] % 6]

        cb = sb(G, "cb", [128, CB_N], BF16)
        pc = sb(G, "pc", [128, 32])
        vec = sb(G, "vec", [128, depth * VW])
        smb = sb(G, "smb", [128, SM_N], BF16)
        qg8 = sb(G, "qg8", [128, 1])
        ones_f = sb(G, "ones_f", [128, 128])
        wsl = [sb(G, f"wsl{i}", [128, 4096], BF16) for i in range(3)]
        wch = [P.chan(f"w{i}") for i in range(3)]
        kc_ = [[sb(G, f"kc{p}_{t}", [128, 512], BF16) for t in range(NT)] for p in range(3)]
        vc_ = [sb(G, f"vc{m}", [128, 384], BF16) for m in range(NT * 4)]
        hT = [sb(G, f"hT{c}", [128, 512], BF16) for c in range(8)]
        mixT = [sb(G, f"mixT{c}", [128, 512], BF16) for c in range(8)]
        ST = sb(G, "ST", [128, 384])
        STb = [sb(G, f"STb{i}", [128, 384], BF16) for i in range(2)]
        rhalo = sb(G, "rhalo", [128, 11])
        uhalo = [sb(G, f"uhalo{i}", [128, 16]) for i in range(2)]
        chalo = sb(G, "chalo", [128, 44, 2])
        c_misc = P.chan("misc")
        c_x = P.chan("x"); c_o = P.chan("o"); c_vf = P.chan("vf"); c_vfl = [P.chan("vfl0"), P.chan("vfl1")]

        ident = cb.t[:, CB_ID:CB_ID + 128]
        ones_b = cb.t[:, CB_ONES:CB_ONES + 128]
        bones_b = cb.t[:, CB_BONES:CB_BONES + 128]
        maskL = cb.t[:, CB_MASKL:CB_MASKL + 128]
        ntri = cb.t[:, CB_NTRI:CB_NTRI + 128]
        mask4 = cb.t[:, CB_MASK4:CB_MASK4 + 512]

        def amask(d):
            return cb.t[:, CB_AMASK + 512 * d: CB_AMASK + 512 * (d + 1)]

        dma("sp", cb.t[:], cb_d, c_misc, [], [cb])
        dma("sp", pc.t[:], pc_d, c_misc, [], [pc])
        dma("sp", vec.t[:], vec_d, c_misc, [], [vec])
        mset("dve", ones_f.t[:], 1.0, [ones_f])

        def vcol(l, name, j=0):
            o = l * VW + voff[name] + j
            return vec.t[:, o:o + 1]

        with ExitStack() as S:
            stg = [sb(S, f"stg{i}", [128, 2048]) for i in range(4)]
            stb = [sb(S, f"stb{i}", [128, 2048], BF16) for i in range(4)]
            cl = [P.chan(f"pl{i}") for i in range(4)]
            cs = [P.chan(f"ps{i}") for i in range(4)]
            n = 0
            pend = []
            for l in range(depth):
                for (src, dst, K, N) in ((w_in_d, wbi, D, NIN), (w_out_d, wbo, D, D), (w_up_d, wbu, D, NUP), (w_down_d, wbd, DFF, D)):
                    for r0 in range(0, K, 128):
                        for c0 in range(0, N, 2048):
                            w = min(2048, N - c0)
                            i = n % 4
                            n += 1
                            dma("sp", stg[i].t[:, 0:w], src[l, r0:r0 + 128, c0:c0 + w], cl[i], [], [stg[i]])
                            cp(("act", "dve", "pool")[n % 3], stb[i].t[:, 0:w], stg[i].t[:, 0:w], [stg[i]], [stb[i]])
                            pend.append((dst[l, r0:r0 + 128, c0:c0 + w], i, w))
                            if len(pend) > 2:
                                d_, i_, w_ = pend.pop(0)
                                dma("sp", d_, stb[i_].t[:, 0:w_], cs[i_], [stb[i_]], [r_wb], nowaw=True)
            for (d_, i_, w_) in pend:
                dma("sp", d_, stb[i_].t[:, 0:w_], cs[i_], [stb[i_]], [r_wb], nowaw=True)
            P.barrier(ALLQ, chans=cl + cs + [c_misc])

        wn = {"i": 0}

        def load_w(W2d, nk, k0, ranges):
            i = wn["i"] % 3
            wn["i"] += 1
            s = wsl[i]
            wc = sum(n_ for _, n_ in ranges)
            view = s.t[:, 0:nk * wc].rearrange("p (k c) -> p k c", c=wc)
            src = W2d[k0 * 128:(k0 + nk) * 128, :].rearrange("(k p) c -> p k c", p=128)
            o = 0
            first = True
            for (c0, n_) in ranges:
                dma("sp", view[:, :, o:o + n_], src[:, :, c0:c0 + n_], wch[i], [r_wb], [s], nowaw=not first)
                first = False
                o += n_
            return s, view

        try:
            stage(1)
            for l in range(depth):
              x_src = xT_d if l == 0 else x1_d
              x_dst = yT_d if l == depth - 1 else x1_d
              x_src_r = [] if l == 0 else [r_x1]
              x_dst_r = [] if l == depth - 1 else [r_x1]
              with ExitStack() as S:
                  smf = sb(S, "smf", [128, SM_N])
                  dma("pool", smf.t[:], sm_d[l], c_misc, [], [smf])
                  cp("dve", smb.t[:], smf.t[:], [smf], [smb])
                  P.op("act", lambda e, l=l: e.mul(out=qg8.t[:], in_=vcol(l, "q_gain"), mul=0.125), R([vec]), R([qg8]))
                  mset("dve", ST.t[:], 0.0, [ST]); mset("dve", STb[0].t[:], 0.0, [STb[0]])
                  mset("dve", rhalo.t[:], 0.0, [rhalo]); mset("dve", chalo.t[:], 0.0, [chalo])
                  for i in range(2):
                      mset("dve", uhalo[i].t[:], 0.0, [uhalo[i]])
                  P.barrier(COMPUTE, chans=[c_misc])
              stcur = 0
              w2a2 = smb.t[:, SM_W2A2:SM_W2A2 + 384]; g2b = smb.t[:, SM_G2:SM_G2 + 384]
              Win, Wout, Wup, Wdn = wbi[l], wbo[l], wbu[l], wbd[l]

              for it in range(NT):
                  t0 = it * TT

                  def rmsnorm(S, xt, gname):
                      sq = [sb(S, f"sq{i}", [128, 512], BF16) for i in range(2)]
                      lnv = sb(S, "lnv", [128, 512]); rstd = sb(S, "rstd", [128, 512])
                      b = bank()
                      for c in range(8):
                          act(sq[c % 2].t[:], xt[c].t[:], AF.Square, [xt[c]], [sq[c % 2]])
                          mm(b.t[:, :], ones_b, sq[c % 2].t[:], c == 0, c == 7, [cb, sq[c % 2]], [b])
                      act(lnv.t[:], b.t[:, :], AF.Ln, [b], [lnv], bias=1e-6, scale=1.0 / D)
                      act(rstd.t[:], lnv.t[:], AF.Exp, [lnv], [rstd], scale=-0.5)
                      for c in range(8):
                          stt("dve", hT[c].t[:], xt[c].t[:], vcol(l, gname, c), rstd.t[:], ALU.mult, ALU.mult,
                              [xt[c], vec, rstd], [hT[c]])

                  def win_tiles(view, j0, cts):
                      out = []
                      for i, ct in enumerate(cts):
                          b = bank()
                          for kc in range(8):
                              mm(b.t[:, :], view[:, kc, (j0 + i) * 128:(j0 + i + 1) * 128], hT[kc].t[:], kc == 0, kc == 7, [view_s[0], hT[kc]], [b])
                          out.append(b)
                      return out

                  view_s = [None]

                  SA = ExitStack()
                  with ExitStack() as S:
                      S = SA
                      xt = [sb(S, f"xa{c}", [128, 512]) for c in range(8)]
                      for c in range(8):
                          ev_ = dma("pool", xt[c].t[:], x_src[c * 128:(c + 1) * 128, t0:t0 + TT], c_x, x_src_r, [xt[c]])
                          if c == 7 and ev_ is not None:
                              for c2 in range(8):
                                  xt[c2].r.w = ev_
                      rmsnorm(S, xt, "ln1_g")

                  dump(hT[0], hT[0].t[:], 0, bf=True); dump(hT[7], hT[7].t[:], 1, bf=True)
                  stage(2)
                  with ExitStack() as S:
                      S = SA
                      s_, view = load_w(Win, 8, 0, [(20 * 128, 256)])
                      view_s[0] = s_
                      bs = win_tiles(view, 0, [20, 21])
                      ur = [sb(S, f"ur{i}", [128, 528]) for i in range(2)]
                      sA = sb(S, "sA", [128, 528]); sB = sb(S, "sB", [128, 528])
                      pl = sb(S, "pl", [128, 512], BF16)
                      for i in range(2):
                          u = ur[i]
                          cp("act", u.t[:, 16:528], bs[i].t[:, :], [bs[i]], [u])
                          cp("dve", u.t[:, 0:16], uhalo[i].t[:], [uhalo[i]], [u])
                          cp("dve", uhalo[i].t[:], u.t[:, 512:528], [u], [uhalo[i]])
                          tt("dve", sA.t[:, 1:528], u.t[:, 1:528], u.t[:, 0:527], ALU.add, [u], [sA])
                          tt("dve", sB.t[:, 3:528], sA.t[:, 3:528], sA.t[:, 1:526], ALU.add, [sA], [sB])
                          if i == 0:
                              lo, hi, wl, wh = sA, sB, 2.0, 4.0
                          else:
                              tt("dve", sA.t[:, 7:528], sB.t[:, 7:528], sB.t[:, 3:524], ALU.add, [sB], [sA])
                              tt("dve", sB.t[:, 15:528], sA.t[:, 15:528], sA.t[:, 7:520], ALU.add, [sA], [sB])
                              lo, hi, wl, wh = sA, sB, 8.0, 16.0
                          if it == 0:
                              tt("dve", lo.t[0:64, 16:32], lo.t[0:64, 16:32], pc.t[0:64, i * 16:(i + 1) * 16], ALU.mult, [lo, pc], [lo])
                              tt("dve", hi.t[64:128, 16:32], hi.t[64:128, 16:32], pc.t[64:128, i * 16:(i + 1) * 16], ALU.mult, [hi, pc], [hi])
                          stt("dve", pl.t[0:64, :], lo.t[0:64, 16:528], 1.0 / wl, u.t[0:64, 16:528], ALU.mult, ALU.subtract, [lo, u], [pl])
                          stt("dve", pl.t[64:128, :], hi.t[64:128, 16:528], 1.0 / wh, u.t[64:128, 16:528], ALU.mult, ALU.subtract, [hi, u], [pl])
                          b = bank()
                          mm(b.t[:, :], smb.t[:, SM_PW + i * 128: SM_PW + (i + 1) * 128], pl.t[:], True, True, [smb, pl], [b])
                          ts("dve", mixT[6 + i].t[:], b.t[:, :], vcol(l, "pool_b", i), vcol(l, "pool_scale", i), ALU.add, ALU.mult,
                             [b, vec], [mixT[6 + i]])

                  dump(mixT[6], mixT[6].t[:], 2, bf=True); dump(mixT[7], mixT[7].t[:], 3, bf=True)
                  stage(3)
                  with ExitStack() as S:
                      S = SA
                      qT = [sb(S, f"qT{p}", [128, 512], BF16) for p in range(3)]
                      vT = [sb(S, f"vT{p}", [128, 512], BF16) for p in range(3)]
                      sqq = [sb(S, f"sqq{i}", [128, 512], BF16) for i in range(2)]
                      lnq2 = [sb(S, f"lnq{i}", [128, 512]) for i in range(2)]; rsq2 = [sb(S, f"rsq{i}", [128, 512]) for i in range(2)]
                      groups = [[(11 * 128, 512)], [(15 * 128, 512)], [(19 * 128, 128)]]
                      cts = [[11, 12, 13, 14], [15, 16, 17, 18], [19]]
                      nq = 0
                      for gi in range(3):
                          s_, view = load_w(Win, 8, 0, groups[gi])
                          view_s[0] = s_
                          bs = win_tiles(view, 0, cts[gi])
                          for b, ct in zip(bs, cts[gi]):
                              if ct < 17:
                                  isq = ct < 14
                                  p = ct - 11 if isq else ct - 14
                                  sq_ = sqq[nq % 2]; lnq = lnq2[nq % 2]; rsq = rsq2[nq % 2]; nq += 1
                                  act(sq_.t[:], b.t[:, :], AF.Square, [b], [sq_])
                                  b2 = bank()
                                  mm(b2.t[:, :], bones_b, sq_.t[:], True, True, [cb, sq_], [b2])
                                  act(lnq.t[:], b2.t[:, :], AF.Ln, [b2], [lnq], bias=1e-6, scale=1.0 / 64)
                                  act(rsq.t[:], lnq.t[:], AF.Exp, [lnq], [rsq], scale=-0.5)
                                  dst = qT[p] if isq else kc_[p][it]
                                  gcol = qg8.t[:, 0:1] if isq else vcol(l, "k_gain")
                                  stt("dve", dst.t[:], b.t[:, :], gcol, rsq.t[:], ALU.mult, ALU.mult, [b, rsq, qg8, vec], [dst])
                              else:
                                  cp("act", vT[ct - 17].t[:], b.t[:, :], [b], [vT[ct - 17]])
                      for s in range(4):
                          h = s % 2
                          for p in range(3):
                              tr(tbank.t[:, h * 512 + p * 128: h * 512 + (p + 1) * 128], vT[p].t[:, s * 128:(s + 1) * 128], ident, [vT[p], cb], [tbank])
                          cp(alt(), vc_[it * 4 + s].t[:], tbank.t[:, h * 512: h * 512 + 384], [tbank], [vc_[it * 4 + s]])
                      Eb = [sb(S, f"Eb{i}", [128, 512]) for i in range(4)]
                      SPb = [sb(S, f"SPb{i}", [128, 512], BF16) for i in range(4)]
                      Rbc2 = [sb(S, f"Rbc{i}", [128, 512]) for i in range(2)]
                      argb = [sb(S, f"argb{i}", [128, 512]) for i in range(4)]
                      Ab = [sb(S, f"Ab{i}", [128, 512], BF16) for i in range(4)]
                      last = 4 * it + 3
                      units = [(p, m, e_) for p in range(3) for m in range(last, -1, -1) for e_ in range(2)]
                      NU = len(units)
                      ust = [dict() for _ in range(NU)]

                      def c0of(m):
                          d = m - 4 * it
                          return 128 * d if d > 0 else 0

                      def s1(u):
                          p, m, e_ = units[u]; pb = 64 * e_; c0 = c0of(m)
                          kt = kc_[p][m // 4]
                          zb = bank(); ust[u]["zb"] = zb
                          mm(zb.t[:, c0:512], kt.t[pb:pb + 64, (m % 4) * 128:(m % 4 + 1) * 128], qT[p].t[pb:pb + 64, c0:512], True, True, [kt, qT[p]], [zb])

                      def s2(u):
                          p, m, e_ = units[u]; d = m - 4 * it; c0 = c0of(m)
                          zb = ust[u]["zb"]; E, SP = Eb[u % 4], SPb[u % 4]
                          act(E.t[:, c0:512], zb.t[:, c0:512], AF.Exp, [zb], [E])
                          act(SP.t[:, c0:512], E.t[:, c0:512], AF.Ln, [E], [SP], bias=1.0)
                          if d >= 0:
                              tt("pool", SP.t[:, c0:512], SP.t[:, c0:512], amask(d)[:, c0:512], ALU.mult, [SP, cb], [SP])

                      def s3(u):
                          p, m, e_ = units[u]; c0 = c0of(m)
                          zb = ust[u]["zb"]; SP = SPb[u % 4]
                          mm(zb.t[:, c0:512], ntri, SP.t[:, c0:512], False, True, [cb, SP], [zb])
                          if m > 0:
                              rb = bank(); ust[u]["rb"] = rb
                              mm(rb.t[:, c0:512], ones_b, SP.t[:, c0:512], True, True, [cb, SP], [rb])

                      def s4(u):
                          p, m, e_ = units[u]; c0 = c0of(m)
                          zb = ust[u]["zb"]; ar = argb[u % 4]; Rbc = Rbc2[e_]
                          if m != last:
                              tt("dve", ar.t[:, c0:512], zb.t[:, c0:512], Rbc.t[:, c0:512], ALU.subtract, [zb, Rbc], [ar])
                          if m > 0:
                              rb = ust[u]["rb"]
                              if m == last:
                                  if c0 > 0:
                                      mset("pool", Rbc.t[:, 0:c0], 0.0, [Rbc])
                                  cp("dve", Rbc.t[:, c0:512], rb.t[:, c0:512], [rb], [Rbc])
                              else:
                                  tt("dve", Rbc.t[:, c0:512], Rbc.t[:, c0:512], rb.t[:, c0:512], ALU.add, [Rbc, rb], [Rbc])

                      def s5(u):
                          p, m, e_ = units[u]; d = m - 4 * it; c0 = c0of(m)
                          zb = ust[u]["zb"]; ar = argb[u % 4]; A = Ab[u % 4]
                          if m == last:
                              act(A.t[:, c0:512], zb.t[:, c0:512], AF.Exp, [zb], [A])
                          else:
                              act(A.t[:, c0:512], ar.t[:, c0:512], AF.Exp, [ar], [A])
                          if d >= 0:
                              tt("pool", A.t[:, c0:512], A.t[:, c0:512], amask(d)[:, c0:512], ALU.mult, [A, cb], [A])

                      def s6(u):
                          p, m, e_ = units[u]; pb = 64 * e_; h = 2 * p + e_; d = m - 4 * it; c0 = c0of(m)
                          A = Ab[u % 4]
                          vv_ = vc_[m].t[:, h * 64:(h + 1) * 64]
                          mm(obank.t[pb:pb + 64, c0:512], vv_, A.t[:, c0:512], m == last, m == 0, [vc_[m], A], [obank])
                          if m == 0 and e_ == 1:
                              cp("act", mixT[3 + p].t[:], obank.t[:, :], [obank], [mixT[3 + p]])

                      for t in range(NU + 2):
                          if t < NU:
                              s1(t); s2(t)
                          if 0 <= t - 1 < NU:
                              s3(t - 1); s4(t - 1); s5(t - 1)
                          if 0 <= t - 2 < NU:
                              s6(t - 2)
                      P.barrier(COMPUTE, chans=[c_x])
                  SA.close()

                  dump(mixT[3], mixT[3].t[:], 4, bf=True); dump(kc_[0][it], kc_[0][it].t[:], 5, bf=True); dump(vc_[it * 4], vc_[it * 4].t[:], 6, bf=True, w=384); dump(mixT[5], mixT[5].t[:], 7, bf=True)
                  stage(4)
                  with ExitStack() as S:
                      lor = sb(S, "lor", [128, 512], BF16); sgx = sb(S, "sgx", [128, 512], BF16)
                      raw = [sb(S, f"raw{i}", [128, 513]) for i in range(2)]
                      dtmp = sb(S, "dtmp", [128, 512])
                      ART = [sb(S, f"ART{p}", [128, 4, 2, 128], BF16) for p in range(3)]
                      BT = [sb(S, f"BT{p}", [128, 512], BF16) for p in range(3)]
                      KT = [sb(S, f"KT{p}", [128, 512], BF16) for p in range(3)]
                      GC = [sb(S, f"GC{p}", [128, 4]) for p in range(3)]
                      gT = [sb(S, f"gT{p}", [128, 512], BF16) for p in range(3)]
                      bonT = [sb(S, f"bonT{p}", [128, 512], BF16) for p in range(3)]
                      TOK = [sb(S, f"TOK{c}", [128, 4, 384], BF16) for c in range(4)]
                      Lp = sb(S, "Lp", [128, 512]); a_t = sb(S, "a_t", [128, 512]); kkn = sb(S, "kkn", [128, 512])
                      bv = sb(S, "bv", [128, 512]); et = sb(S, "et", [128, 512]); t2 = sb(S, "t2", [128, 512])
                      BHT = sb(S, "BHT", [128, 512], BF16); KHT = sb(S, "KHT", [128, 512], BF16)
                      vb = [sb(S, f"vb{p}", [128, 512], BF16) for p in range(3)]
                      sqk = sb(S, "sqk", [128, 512], BF16); lob = sb(S, "lob", [32, 512], BF16)
                      cLC = sb(S, "cLC", [128, 4])

                      def evac_shift(b, ct, dst, dtmp=dtmp):
                          cp("act", dst.t[:, 1:513], b.t[:, :], [b], [dst])
                          cp("dve", dst.t[:, 0:1], rhalo.t[:, ct:ct + 1], [rhalo], [dst])
                          tt("dve", dtmp.t[:], dst.t[:, 0:512], dst.t[:, 1:513], ALU.subtract, [dst], [dtmp])
                          cp("dve", rhalo.t[:, ct:ct + 1], dst.t[:, 512:513], [dst], [rhalo])
                          stt("dve", dst.t[:, 1:513], dtmp.t[:], vcol(l, "mu", ct), dst.t[:, 1:513], ALU.mult, ALU.add, [dtmp, vec, dst], [dst])

                      s_, view = load_w(Win, 8, 0, [(9 * 128, 256)])
                      view_s[0] = s_
                      bs = win_tiles(view, 0, [9, 10])
                      evac_shift(bs[0], 9, raw[0])
                      act(lor.t[0:64, :], raw[0].t[0:64, 1:513], AF.Tanh, [raw[0]], [lor])
                      cp("dve", lor.t[64:128, :], raw[0].t[64:128, 1:513], [raw[0]], [lor])
                      evac_shift(bs[1], 10, raw[1])
                      act(sgx.t[:], raw[1].t[:, 1:513], AF.Sigmoid, [raw[1]], [sgx])
                      stage(41)
                      vr = [sb(S, f"vr{p}", [128, 513]) for p in range(3)]
                      s_, view = load_w(Win, 8, 0, [(6 * 128, 384)])
                      view_s[0] = s_
                      bs = win_tiles(view, 0, [6, 7, 8])
                      for p in range(3):
                          evac_shift(bs[p], 6 + p, vr[p])
                      if l > 0:
                          for p in range(3):
                              cp("pool", vb[p].t[:], vr[p].t[:, 1:513], [vr[p]], [vb[p]])
                          b = bank()
                          for pp in range(3):
                              mm(b.t[0:32, :], smb.t[:, SM_V1 + pp * 32: SM_V1 + (pp + 1) * 32], vb[pp].t[:], pp == 0, pp == 2, [smb, vb[pp]], [b])
                          cp("act", lob.t[:], b.t[0:32, :], [b], [lob])
                      rawS = [raw, [sb(S, f"rawb{i}", [128, 513]) for i in range(2)]]
                      dtmpS = [dtmp, sb(S, "dtmpb", [128, 512])]
                      t2S = [t2, sb(S, "t2b", [128, 512])]; etS = [et, sb(S, "etb", [128, 512])]
                      LpS = [Lp, sb(S, "Lpb", [128, 512])]; a_tS = [a_t, sb(S, "a_tb", [128, 512])]

                      def prep_early(p):
                          s2_ = p % 2
                          raw_, dtmp_, t2, et, Lp, a_t = rawS[s2_], dtmpS[s2_], t2S[s2_], etS[s2_], LpS[s2_], a_tS[s2_]
                          pc0 = p * 128
                          s_, view = load_w(Win, 8, 0, [(p * 128, 128), ((3 + p) * 128, 128)])
                          view_s[0] = s_
                          bs = win_tiles(view, 0, [p, 3 + p])
                          evac_shift(bs[0], p, raw_[0], dtmp_)
                          evac_shift(bs[1], 3 + p, raw_[1], dtmp_)
                          rw = [raw_[0], raw_[1], vr[p]]
                          r_, k_, v_ = rw[0], rw[1], rw[2]
                          rT, kTr, vTr = r_.t[:, 1:513], k_.t[:, 1:513], v_.t[:, 1:513]
                          if l == 0:
                              dma("pool", vf_d[pc0:pc0 + 128, t0:t0 + TT], vTr, c_vf, [v_], [r_vf], nowaw=True)
                          else:
                              b = bank()
                              mm(b.t[:, :], smb.t[0:32, SM_V2 + pc0: SM_V2 + pc0 + 128], lob.t[:], True, True, [smb, lob], [b])
                              act(et.t[:], b.t[:, :], AF.Sigmoid, [b], [et], bias=vcol(l, "v0", p))
                              dma("pool", t2.t[:], vf_d[pc0:pc0 + 128, t0:t0 + TT], c_vfl[s2_], [r_vf], [t2])
                              tt("dve", t2.t[:], t2.t[:], vTr, ALU.subtract, [t2, v_], [t2])
                              tt("dve", t2.t[:], t2.t[:], et.t[:], ALU.mult, [t2, et], [t2])
                              tt("dve", vTr, vTr, t2.t[:], ALU.add, [v_, t2], [v_])
                          cp("pool", vb[p].t[:], vTr, [v_], [vb[p]])
                          b = bank()
                          mm(b.t[:, :], w2a2[0:64, pc0:pc0 + 128], lor.t[0:64, :], True, True, [smb, lor], [b])
                          act(et.t[:], b.t[:, :], AF.Sigmoid, [b], [et], bias=vcol(l, "w0", p))
                          for c in range(4):
                              P.op("dve", (lambda o_, d0_, d1_: lambda e: e.tensor_tensor_scan(out=o_, data0=d0_, data1=d1_, initial=0.0, op0=ALU.mult, op1=ALU.add))(
                              Lp.t[:, c * 128:(c + 1) * 128], ones_f.t[:], et.t[:, c * 128:(c + 1) * 128]), R([ones_f, et]), R([Lp]))
                          b = bank()
                          mm(b.t[:, :], w2a2[64:128, pc0:pc0 + 128], lor.t[64:128, :], True, True, [smb, lor], [b])
                          act(a_t.t[:], b.t[:, :], AF.Sigmoid, [b], [a_t], bias=vcol(l, "a0", p))
                          b = bank()
                          mm(b.t[:, :], g2b[:, pc0:pc0 + 128], sgx.t[:], True, True, [smb, sgx], [b])
                          cp("act", gT[p].t[:], b.t[:, :], [b], [gT[p]])

                      def prep_late(p):
                          s2_ = p % 2
                          raw_, dtmp_, t2, et, Lp, a_t = rawS[s2_], dtmpS[s2_], t2S[s2_], etS[s2_], LpS[s2_], a_tS[s2_]
                          pc0 = p * 128
                          r_, k_, v_ = raw_[0], raw_[1], vr[p]
                          rT, kTr, vTr = r_.t[:, 1:513], k_.t[:, 1:513], v_.t[:, 1:513]
                          act(sqk.t[:], kTr, AF.Square, [k_, vec], [sqk], scale=vcol(l, "k_k", p))
                          b = bank()
                          mm(b.t[:, :], bones_b, sqk.t[:], True, True, [cb, sqk], [b])
                          act(t2.t[:], b.t[:, :], AF.Ln, [b], [t2], bias=1e-24)
                          act(t2.t[:], t2.t[:], AF.Exp, [t2], [t2], scale=-0.5)
                          stt("dve", kkn.t[:], kTr, vcol(l, "k_k", p), t2.t[:], ALU.mult, ALU.mult, [k_, vec, t2], [kkn])
                          tt("dve", bv.t[:], kkn.t[:], a_t.t[:], ALU.mult, [kkn, a_t], [bv])
                          ts("dve", t2.t[:], a_t.t[:], -1.0, vcol(l, "k_a", p), ALU.add, ALU.mult, [a_t, vec], [t2])
                          stt("dve", kTr, t2.t[:], 1.0, kTr, ALU.add, ALU.mult, [t2, k_], [k_])
                          stt("dve", sqk.t[:], rT, vcol(l, "r_k", p), kTr, ALU.mult, ALU.mult, [r_, k_, vec], [sqk])
                          b = bank()
                          mm(b.t[:, :], bones_b, sqk.t[:], True, True, [cb, sqk], [b])
                          tt("dve", bonT[p].t[:], b.t[:, :], vTr, ALU.mult, [b, v_], [bonT[p]])
                          A4 = ART[p].t[:, :, 0, :]; R4 = ART[p].t[:, :, 1, :]
                          v4 = lambda ap: ap.rearrange("p (c t) -> p c t", t=128)
                          tt("dve", t2.t[:], Lp.t[:], et.t[:], ALU.subtract, [Lp, et], [t2])
                          act(t2.t[:], t2.t[:], AF.Exp, [t2], [t2], scale=CDEC)
                          stt("dve", A4, v4(kkn.t[:]), -1.0, v4(t2.t[:]), ALU.mult, ALU.mult, [kkn, t2], [ART[p]])
                          act(t2.t[:], Lp.t[:], AF.Exp, [Lp], [t2], scale=CDEC)
                          tt("dve", R4, v4(rT), v4(t2.t[:]), ALU.mult, [r_, t2], [ART[p]])
                          for c in range(4):
                              cp("dve", GC[p].t[:, c:c + 1], t2.t[:, c * 128 + 127: c * 128 + 128], [t2], [GC[p]])
                          act(t2.t[:], Lp.t[:], AF.Exp, [Lp], [t2], scale=-CDEC)
                          tt("dve", BT[p].t[:], bv.t[:], t2.t[:], ALU.mult, [bv, t2], [BT[p]])
                          tt("dve", KT[p].t[:], kTr, t2.t[:], ALU.mult, [k_, t2], [KT[p]])
                          for c in range(4):
                              ts("dve", cLC.t[:, c:c + 1], Lp.t[:, c * 128 + 127: c * 128 + 128], CDEC, None, ALU.mult, None, [Lp], [cLC])
                          for c in range(4):
                              act(t2.t[:, c * 128:(c + 1) * 128], Lp.t[:, c * 128:(c + 1) * 128], AF.Exp, [Lp, cLC], [t2],
                                  scale=-CDEC, bias=cLC.t[:, c:c + 1])
                          tt("dve", BHT.t[:], bv.t[:], t2.t[:], ALU.mult, [bv, t2], [BHT])
                          tt("dve", KHT.t[:], kTr, t2.t[:], ALU.mult, [k_, t2], [KHT])
                          for c in range(4):
                              h = c % 2
                              srcs = [(ART[p], ART[p].t[:, c, 0, :]), (BHT, BHT.t[:, c * 128:(c + 1) * 128]),
                                      (KHT, KHT.t[:, c * 128:(c + 1) * 128]), (vb[p], vb[p].t[:, c * 128:(c + 1) * 128])]
                              for kd, (tl_, ap_) in enumerate(srcs):
                                  tr(tbank.t[:, h * 512 + kd * 128: h * 512 + (kd + 1) * 128], ap_, ident, [tl_, cb], [tbank])
                              cp(alt(), TOK[c].t[:, :, pc0:pc0 + 128],
                                 tbank.t[:, h * 512:(h + 1) * 512].rearrange("p (k c) -> p k c", c=128), [tbank], [TOK[c]])


                      prep_early(0)
                      for p in range(3):
                          if p + 1 < 3:
                              prep_early(p + 1)
                          prep_late(p)
                      stage(42)
                      SCb = [sb(S, f"SCb{h}", [128, 512], BF16) for h in range(6)]
                      XTb = [sb(S, f"XTb{h}", [128, 128], BF16) for h in range(6)]
                      PP = [[sb(S, f"PP{p}_{i}", [128, 512], BF16) for i in range(2)] for p in range(3)]
                      ACC = [[sb(S, f"ACC{p}_{i}", [128, 256], BF16) for i in range(2)] for p in range(3)]
                      T1b = sb(S, "T1b", [128, 384], BF16); UH = sb(S, "UH", [128, 384]); WhT = sb(S, "WhT", [128, 384], BF16)
                      Ub = sb(S, "Ub", [128, 384], BF16); ysq = sb(S, "ysq", [128, 384]); yn = sb(S, "yn", [128, 384], BF16)
                      st1 = sb(S, "st1", [128, 6]); st2 = sb(S, "st2", [128, 6]); st3 = sb(S, "st3", [128, 6])
                      YT = sb(S, "YT", [128, 3, 512], BF16)
                      for c in range(4):
                          tok = TOK[c]
                          csl = slice(c * 128, (c + 1) * 128)
                          for p in range(3):
                              for e_ in range(2):
                                  pb = 64 * e_; h = 2 * p + e_
                                  b = bank()
                                  ar2 = ART[p].t[pb:pb + 64, c, :, :].rearrange("p a t -> p (a t)")
                                  mm(b.t[:, 0:256], BT[p].t[pb:pb + 64, csl], ar2, True, True, [BT[p], ART[p]], [b])
                                  mm(b.t[:, 256:512], KT[p].t[pb:pb + 64, csl], ar2, True, True, [KT[p], ART[p]], [b])
                                  tt("dve", SCb[h].t[:], b.t[:, :], mask4, ALU.mult, [b, cb], [SCb[h]])
                                  b = bank()
                                  mm(b.t[:, 0:128], ART[p].t[pb:pb + 64, c, 0, :], BT[p].t[pb:pb + 64, csl], True, True, [BT[p], ART[p]], [b])
                                  tt("dve", XTb[h].t[:], b.t[:, 0:128], maskL, ALU.mult, [b, cb], [XTb[h]])
                          stage(43)
                          b = bank()
                          for h in range(6):
                              mm(b.t[:, h * 64:(h + 1) * 64], SCb[h].t[:, 256:384], tok.t[:, 3, h * 64:(h + 1) * 64], True, True, [SCb[h], tok], [b])
                          cp("act", T1b.t[:], b.t[:, 0:384], [b], [T1b])
                          cur = [0, 0, 0]
                          for p in range(3):
                              for e_ in range(2):
                                  h = 2 * p + e_
                                  tt("pool", ACC[p][0].t[:, e_ * 128:(e_ + 1) * 128], SCb[h].t[:, 0:128], ident, ALU.add, [SCb[h], cb], [ACC[p][0]])
                          for k in range(1, 7):
                              i0 = (k - 1) % 2; i1 = k % 2
                              bq = []
                              for p in range(3):
                                  b = bank(); bq.append(b)
                                  for e_ in range(2):
                                      h = 2 * p + e_
                                      if k == 1:
                                          Pm, PTm, rds = SCb[h].t[:, 0:128], XTb[h].t[:], [SCb[h], XTb[h]]
                                      else:
                                          Pm = PP[p][i0].t[:, e_ * 256: e_ * 256 + 128]; PTm = PP[p][i0].t[:, e_ * 256 + 128: e_ * 256 + 256]
                                          rds = [PP[p][i0]]
                                      if k < 6:
                                          mm(b.t[:, e_ * 256: e_ * 256 + 128], PTm, Pm, True, True, rds, [b])
                                      mm(b.t[:, e_ * 256 + 128: e_ * 256 + 256], Pm, PTm, True, True, rds, [b])
                              for p in range(3):
                                  b = bq[p]
                                  if k < 6:
                                      cp("act", PP[p][i1].t[:], b.t[:, :], [b], [PP[p][i1]])
                                  else:
                                      for e_ in range(2):
                                          cp("act", PP[p][i1].t[:, e_ * 256 + 128: e_ * 256 + 256], b.t[:, e_ * 256 + 128: e_ * 256 + 256], [b], [PP[p][i1]])
                              bq2 = []
                              for p in range(3):
                                  b2 = bank(); bq2.append(b2)
                                  for e_ in range(2):
                                      mm(b2.t[:, e_ * 128:(e_ + 1) * 128], PP[p][i1].t[:, e_ * 256 + 128: e_ * 256 + 256],
                                         ACC[p][i0].t[:, e_ * 128:(e_ + 1) * 128], True, True, [PP[p][i1], ACC[p][i0]], [b2])
                              for p in range(3):
                                  tt("dve", ACC[p][i1].t[:], ACC[p][i0].t[:], bq2[p].t[:, 0:256], ALU.add, [ACC[p][i0], bq2[p]], [ACC[p][i1]])
                          stage(44)
                          NTf = [ACC[p][0] for p in range(3)]
                          b = bank()
                          for h in range(6):
                              p, e_ = h // 2, h % 2
                              mm(b.t[64 * e_:64 * e_ + 64, p * 128:(p + 1) * 128], tok.t[:, 0, h * 64:(h + 1) * 64], NTf[p].t[:, e_ * 128:(e_ + 1) * 128],
                                 True, True, [tok, NTf[p]], [b])
                          cp("act", WhT.t[:], b.t[:, 0:384], [b], [WhT])
                          b = bank()
                          for h in range(6):
                              p, e_ = h // 2, h % 2
                              mm(b.t[:, h * 64:(h + 1) * 64], NTf[p].t[:, e_ * 128:(e_ + 1) * 128], T1b.t[:, h * 64:(h + 1) * 64], True, True, [NTf[p], T1b], [b])
                          cp("act", UH.t[:], b.t[:, 0:384], [b], [UH])
                          stage(45)
                          so = STb[stcur]; sn = STb[1 - stcur]
                          b = bank()
                          for p in range(3):
                              mm(b.t[:, p * 128:(p + 1) * 128], WhT.t[:, p * 128:(p + 1) * 128], so.t[:, p * 128:(p + 1) * 128], True, True, [WhT, so], [b])
                          tt("dve", Ub.t[:], b.t[:, 0:384], UH.t[:], ALU.add, [b, UH], [Ub])
                          stage(451)
                          yb = bank()
                          for h in range(6):
                              p, e_ = h // 2, h % 2; pb = 64 * e_
                              hs = slice(h * 64, (h + 1) * 64)
                              if e_ == 0:
                                  mm(yb.t[:, p * 128:(p + 1) * 128], ART[p].t[:, c, 1, :], so.t[:, p * 128:(p + 1) * 128], True, False, [ART[p], so], [yb])
                              mm(yb.t[:, hs], SCb[h].t[:, 128:256], Ub.t[:, hs], False, False, [SCb[h], Ub], [yb])
                              mm(yb.t[:, hs], SCb[h].t[:, 384:512], tok.t[:, 3, hs], False, True, [SCb[h], tok], [yb])
                          stage(452)
                          sbk = bank()
                          for h in range(6):
                              p, e_ = h // 2, h % 2; pb = 64 * e_
                              hs = slice(h * 64, (h + 1) * 64)
                              mm(sbk.t[pb:pb + 64, hs], tok.t[:, 1, hs], Ub.t[:, hs], True, False, [tok, Ub], [sbk])
                              mm(sbk.t[pb:pb + 64, hs], tok.t[:, 2, hs], tok.t[:, 3, hs], False, True, [tok], [sbk])
                          stage(453)
                          for h in range(6):
                              p, e_ = h // 2, h % 2; pb = 64 * e_
                              hs = slice(h * 64, (h + 1) * 64)
                              stt("dve", ST.t[pb:pb + 64, hs], ST.t[pb:pb + 64, hs], GC[p].t[pb:pb + 64, c:c + 1], sbk.t[pb:pb + 64, hs],
                                  ALU.mult, ALU.add, [ST, GC[p], sbk], [ST])
                          cp("act", sn.t[:], ST.t[:], [ST], [sn])
                          stcur = 1 - stcur
                          stage(46)
                          y3 = yb.t[:, 0:384].rearrange("p (h i) -> p h i", i=64)
                          P.op("dve", (lambda o_, i_: lambda e: e.tensor_reduce(out=o_, in_=i_, axis=AX.X, op=ALU.add))(st1.t[:], y3), R([yb]), R([st1]))
                          act(ysq.t[:], yb.t[:, 0:384], AF.Square, [yb], [ysq])
                          P.op("dve", (lambda o_, i_: lambda e: e.tensor_reduce(out=o_, in_=i_, axis=AX.X, op=ALU.add))(
                              st2.t[:], ysq.t[:].rearrange("p (h i) -> p h i", i=64)), R([ysq]), R([st2]))
                          ts("dve", st1.t[:], st1.t[:], 1.0 / 64, None, ALU.mult, None, [st1], [st1])
                          tt("dve", st3.t[:], st1.t[:], st1.t[:], ALU.mult, [st1], [st3])
                          stt("dve", st2.t[:], st2.t[:], 1.0 / 64, st3.t[:], ALU.mult, ALU.subtract, [st2, st3], [st2])
                          act(st2.t[:], st2.t[:], AF.Ln, [st2], [st2], bias=64e-5)
                          act(st2.t[:], st2.t[:], AF.Exp, [st2], [st2], scale=-0.5)
                          for h in range(6):
                              hs = slice(h * 64, (h + 1) * 64)
                              ts("dve", yn.t[:, hs], yb.t[:, hs], st1.t[:, h:h + 1], st2.t[:, h:h + 1], ALU.subtract, ALU.mult, [yb, st1, st2], [yn])
                          stage(47)
                          hh = c % 2
                          for p in range(3):
                              tr(tbank.t[:, hh * 512 + p * 128: hh * 512 + (p + 1) * 128], yn.t[:, p * 128:(p + 1) * 128], ident, [yn, cb], [tbank])
                          cp("act", YT.t[:, :, csl], tbank.t[:, hh * 512: hh * 512 + 384].rearrange("p (k c) -> p k c", c=128), [tbank], [YT])
                      for p in range(3):
                          act(t2.t[:], YT.t[:, p, :], AF.Identity, [YT, vec], [t2], scale=vcol(l, "lnx_w", p), bias=vcol(l, "lnx_b", p))
                          tt("dve", t2.t[:], t2.t[:], bonT[p].t[:], ALU.add, [t2, bonT[p]], [t2])
                          tt("dve", mixT[p].t[:], t2.t[:], gT[p].t[:], ALU.mult, [t2, gT[p]], [mixT[p]])
                      P.barrier(COMPUTE, chans=[c_vf] + c_vfl)

                  dump(mixT[0], mixT[0].t[:], 8, bf=True); dump(mixT[1], mixT[1].t[:], 9, bf=True); dump(mixT[2], mixT[2].t[:], 10, bf=True)
                  stage(5)
                  with ExitStack() as S:
                      xt = [sb(S, f"xa{c}", [128, 512]) for c in range(8)]
                      for c in range(8):
                          ev_ = dma("pool", xt[c].t[:], x_src[c * 128:(c + 1) * 128, t0:t0 + TT], c_x, x_src_r, [xt[c]])
                          if c == 7 and ev_ is not None:
                              for c2 in range(8):
                                  xt[c2].r.w = ev_
                      for g in range(2):
                          s_, view = load_w(Wout, 8, 0, [(g * 512, 512)])
                          for j in range(4):
                              dt_ = g * 4 + j
                              b = bank()
                              for kc in range(8):
                                  mm(b.t[:, :], view[:, kc, j * 128:(j + 1) * 128], mixT[kc].t[:], kc == 0, kc == 7, [s_, mixT[kc]], [b])
                              tt("dve", xt[dt_].t[:], xt[dt_].t[:], b.t[:, :], ALU.add, [xt[dt_], b], [xt[dt_]])
                      with ExitStack() as S2:
                          rmsnorm(S2, xt, "ln2_g")
                          P.barrier(COMPUTE)
                      graw = [sb(S, f"graw{i}", [128, 514]) for i in range(2)]
                      vraw = [sb(S, f"vraw{i}", [128, 514]) for i in range(2)]
                      cg = [sb(S, f"cg{i}", [128, 512]) for i in range(2)]
                      cv = [sb(S, f"cv{i}", [128, 512]) for i in range(2)]
                      gv = [sb(S, f"gv{i}", [128, 512], BF16) for i in range(11)]
                      nf = 0
                      for half in range(2):
                          for (g0, ng) in ((0, 4), (4, 4), (8, 3)):
                              ctg = half * 11 + g0
                              sg_, vg = load_w(Wup, 8, 0, [(ctg * 128, ng * 128)])
                              sv_, vv = load_w(Wup, 8, 0, [((22 + ctg) * 128, ng * 128)])
                              for j in range(ng):
                                  i2 = nf % 2; nf += 1
                                  ct = ctg + j
                                  res = []
                                  for (sl_, vw_, cti, rawt, cout) in ((sg_, vg, ct, graw[i2], cg[i2]), (sv_, vv, 22 + ct, vraw[i2], cv[i2])):
                                      b = bank()
                                      for kc in range(8):
                                          mm(b.t[:, :], vw_[:, kc, j * 128:(j + 1) * 128], hT[kc].t[:], kc == 0, kc == 7, [sl_, hT[kc]], [b])
                                      cp("act", rawt.t[:, 2:514], b.t[:, :], [b], [rawt])
                                      cp("pool", rawt.t[:, 0:2], chalo.t[:, cti, :], [chalo], [rawt])
                                      cp("pool", chalo.t[:, cti, :], rawt.t[:, 512:514], [rawt], [chalo])
                                      act(cout.t[:], b.t[:, :], AF.Identity, [b, vec], [cout], scale=vcol(l, "conv_w", 2 * 44 + cti), bias=vcol(l, "conv_b", cti))
                                      stt("dve", cout.t[:], rawt.t[:, 1:513], vcol(l, "conv_w", 44 + cti), cout.t[:], ALU.mult, ALU.add, [rawt, vec, cout], [cout])
                                      stt("dve", cout.t[:], rawt.t[:, 0:512], vcol(l, "conv_w", cti), cout.t[:], ALU.mult, ALU.add, [rawt, vec, cout], [cout])
                                  act(cg[i2].t[:], cg[i2].t[:], AF.Silu, [cg[i2]], [cg[i2]])
                                  tt("pool", gv[g0 + j].t[:], cg[i2].t[:], cv[i2].t[:], ALU.mult, [cg[i2], cv[i2]], [gv[g0 + j]])
                          for dq in range(4):
                              s_, view = load_w(Wdn, 11, half * 11, [(dq * 256, 256)])
                              for j in range(2):
                                  dt_ = dq * 2 + j
                                  b = bank()
                                  for kc in range(11):
                                      mm(b.t[:, :], view[:, kc, j * 128:(j + 1) * 128], gv[kc].t[:], kc == 0, kc == 10, [s_, gv[kc]], [b])
                                  tt("dve", xt[dt_].t[:], xt[dt_].t[:], b.t[:, :], ALU.add, [xt[dt_], b], [xt[dt_]])
                      for c in range(8):
                          dma("pool", x_dst[c * 128:(c + 1) * 128, t0:t0 + TT], xt[c].t[:], c_o, [xt[c]], x_dst_r, nowaw=True)
                      P.barrier(COMPUTE, chans=[c_o, c_x])

        except _Stop:
            pass
        P.dead = False
        P.barrier(ALLQ, chans=P.chans)
        P.emit()
    return nc, P


_CACHE = {}


def run(inputs, T, depth, n_cores, stop=0):
    vec, sm = pack_params(inputs, depth)
    cbv, pcv = pack_consts()
    key = (T, depth, stop)
    if key not in _CACHE:
        _CACHE[key] = build(T, depth, stop)[0]
    nc = _CACHE[key]
    x = np.asarray(inputs["x"], np.float32)
    shared = {
        "w_in": np.ascontiguousarray(np.asarray(inputs["w_in"], np.float32)[:depth]),
        "w_out": np.ascontiguousarray(np.asarray(inputs["w_out"], np.float32)[:depth]),
        "w_up": np.ascontiguousarray(np.asarray(inputs["w_up"], np.float32)[:depth]),
        "w_down": np.ascontiguousarray(np.asarray(inputs["w_down"], np.float32)[:depth]),
        "vec": vec, "sm": sm, "cb": cbv, "pc": pcv,
    }
    in_maps = []
    for b in range(n_cores):
        m = dict(shared)
        m["xT"] = np.ascontiguousarray(x[b, :T].T)
        in_maps.append(m)
    res = run_bass_kernel_spmd(nc, in_maps, core_ids=list(range(n_cores)))
    if stop:
        return res.results[0]
    out = np.stack([np.ascontiguousarray(res.results[b]["yT"].T) for b in range(n_cores)], 0)
    return out.astype(np.float32)


def kernel(**inputs):
    inputs = {k: np.asarray(v) for k, v in inputs.items()}
    x = inputs["x"]
    return run(inputs, x.shape[1], 2, x.shape[0])
```

```python
from contextlib import ExitStack
import numpy as np
import ml_dtypes
import concourse.bass as bass
import concourse.mybir as mybir
from concourse.bass_utils import run_bass_kernel_spmd

F32 = mybir.dt.float32
BF16 = mybir.dt.bfloat16
AF = mybir.ActivationFunctionType
ALU = mybir.AluOpType
AX = mybir.AxisListType

D = 1024
NIN = 2816
DFF = 2816
NUP = 5632
TT = 512
CDEC = -float(np.exp(-0.5))

COMPUTE = ("pe", "act", "dve", "pool")
ALLQ = ("pe", "act", "dve", "pool", "sp")


class Reg:
    __slots__ = ("name", "w", "rs")

    def __init__(self, name):
        self.name = name
        self.w = None
        self.rs = {}


class Chan:
    def __init__(self, prog, name):
        self.key = "dma_" + name
        prog.semkeys.append(self.key)
        self.cnt = 0


class Prog:
    def __init__(self, nc):
        self.nc = nc
        self.q = {e: [] for e in ALLQ}
        self.cnt = {e: 0 for e in COMPUTE}
        self.known = {e: {} for e in ALLQ}
        self.semkeys = ["prog_" + e for e in COMPUTE]
        self.chans = []
        self.nins = 0
        self.nwait = 0
        self.dead = False

    def chan(self, name):
        c = Chan(self, name)
        self.chans.append(c)
        return c

    def op(self, eng, fn, reads=(), writes=(), chan=None, nowaw=False):
        if self.dead:
            return None
        need = {}

        def req(k, v):
            if need.get(k, 0) < v:
                need[k] = v

        me = None if chan is not None else eng
        for r in reads:
            if r.w is not None:
                req(r.w[0], r.w[1])
        for w in writes:
            if w.w is not None and (w.w[2] is None or w.w[2] != me) and not nowaw:
                req(w.w[0], w.w[1])
            for k, (v, e) in w.rs.items():
                if e is None or e != me:
                    req(k, v)
        kn = self.known[eng]
        for k, v in need.items():
            if kn.get(k, 0) < v:
                self.q[eng].append(("w", k, v))
                kn[k] = v
                self.nwait += 1
        if chan is not None:
            chan.cnt += 1
            ev = (chan.key, 16 * chan.cnt, None)
            self.q[eng].append(("i", fn, chan.key, 16))
        else:
            self.cnt[eng] += 1
            ev = ("prog_" + eng, self.cnt[eng], eng)
            self.q[eng].append(("i", fn, ev[0], 1))
        self.nins += 1
        for r in reads:
            old = r.rs.get(ev[0])
            if old is None or old[0] < ev[1]:
                r.rs[ev[0]] = (ev[1], ev[2])
        for w in writes:
            w.w = ev
            w.rs = {}
        return ev

    def wait(self, eng, key, val):
        if self.dead:
            return
        kn = self.known[eng]
        if val > 0 and kn.get(key, 0) < val:
            self.q[eng].append(("w", key, val))
            kn[key] = val
            self.nwait += 1

    def barrier(self, engs=COMPUTE, chans=()):
        for e in engs:
            for x in engs:
                if x != e and x in self.cnt:
                    self.wait(e, "prog_" + x, self.cnt[x])
            for c in chans:
                self.wait(e, c.key, 16 * c.cnt)

    def emit(self):
        nc = self.nc
        with ExitStack() as st:
            sems = {k: st.enter_context(nc.semaphore(k)) for k in self.semkeys}
            block = st.enter_context(nc.Block())

            def run(e, items):
                for it in items:
                    if it[0] == "w":
                        e.wait_ge(sems[it[1]], it[2])
                    else:
                        it[1](e).then_inc(sems[it[2]], it[3])

            @block.tensor
            def _(e):
                run(e, self.q["pe"])

            @block.scalar
            def _(e):
                run(e, self.q["act"])

            @block.vector
            def _(e):
                run(e, self.q["dve"])

            @block.gpsimd
            def _(e):
                run(e, self.q["pool"])

            @block.sync
            def _(e):
                run(e, self.q["sp"])


class Tl:
    __slots__ = ("t", "r", "name")

    def __init__(self, t, name):
        self.t = t
        self.name = name
        self.r = Reg(name)


def vec_cols(depth):
    spec = [("ln1_g", 8), ("ln2_g", 8), ("mu", 11), ("w0", 3), ("a0", 3), ("k_k", 3), ("k_a", 3), ("r_k", 3),
            ("lnx_w", 3), ("lnx_b", 3), ("v0", 3), ("q_gain", 1), ("k_gain", 1), ("pool_b", 2),
            ("pool_scale", 2), ("conv_w", 132), ("conv_b", 44)]
    off, o = {}, 0
    for n, c in spec:
        off[n] = o
        o += c
    return off, o


SM_W2A2, SM_G2, SM_V1, SM_V2, SM_PW, SM_N = 0, 384, 768, 864, 1248, 1504
CB_ID, CB_ONES, CB_BONES, CB_MASKL, CB_NTRI, CB_MASK4, CB_AMASK, CB_N = 0, 128, 256, 384, 512, 640, 1152, 3200


def pack_consts():
    c = np.zeros((128, CB_N), np.float32)
    i = np.arange(128)
    c[:, CB_ID:CB_ID + 128] = np.eye(128)
    c[:, CB_ONES:CB_ONES + 128] = 1.0
    c[:, CB_BONES:CB_BONES + 128] = (i[:, None] // 64 == i[None, :] // 64)
    c[:, CB_MASKL:CB_MASKL + 128] = (i[None, :] < i[:, None])
    c[:, CB_NTRI:CB_NTRI + 128] = -(i[:, None] >= i[None, :]).astype(np.float32)
    su = (i[:, None] < i[None, :]).astype(np.float32)
    iu = (i[:, None] <= i[None, :]).astype(np.float32)
    c[:, CB_MASK4:CB_MASK4 + 512] = np.concatenate([su, iu, su, iu], 1)
    q = np.arange(512)
    for d in range(4):
        c[:, CB_AMASK + 512 * d:CB_AMASK + 512 * (d + 1)] = (128 * d + i[:, None] < q[None, :])
    pc = np.ones((128, 2, 16), np.float32)
    wins = (2, 4, 8, 16)
    for ti in range(2):
        for e in range(2):
            w = wins[2 * ti + e]
            t = np.arange(16)
            pc[e * 64:(e + 1) * 64, ti, :] = (w / np.minimum(t + 1, w))[None, :]
    return c.astype(ml_dtypes.bfloat16), pc.reshape(128, 32)


def pack_params(inp, depth):
    off, W = vec_cols(depth)
    vec = np.zeros((128, depth * W), np.float32)
    sm = np.zeros((depth, 128, SM_N), np.float32)

    def put(l, name, v):
        v = np.asarray(v, np.float32).reshape(-1, 128).T
        vec[:, l * W + off[name]: l * W + off[name] + v.shape[1]] = v

    for l in range(depth):
        put(l, "ln1_g", inp["ln1_g"][l]); put(l, "ln2_g", inp["ln2_g"][l]); put(l, "mu", inp["mu_shift"][l])
        for n in ("w0", "a0", "k_k", "k_a", "lnx_w", "lnx_b"):
            put(l, n, inp[n][l])
        put(l, "r_k", inp["r_k"][l].reshape(-1))
        if l > 0:
            put(l, "v0", inp["v0"][l - 1])
        put(l, "q_gain", np.tile(inp["q_gain"][l], 2)); put(l, "k_gain", np.tile(inp["k_gain"][l], 2))
        put(l, "pool_b", inp["pool_b"][l].reshape(-1)); put(l, "pool_scale", inp["pool_scale"][l])
        put(l, "conv_w", inp["conv_w"][l].reshape(-1)); put(l, "conv_b", inp["conv_b"][l])
        sm[l, 0:64, SM_W2A2:SM_W2A2 + 384] = inp["w2"][l]
        sm[l, 64:128, SM_W2A2:SM_W2A2 + 384] = inp["a2"][l]
        sm[l, :, SM_G2:SM_G2 + 384] = inp["g2"][l]
        if l > 0:
            v1 = np.asarray(inp["v1"][l - 1]).reshape(3, 128, 32).transpose(1, 0, 2).reshape(128, 96)
            sm[l, :, SM_V1:SM_V1 + 96] = v1
            sm[l, 0:32, SM_V2:SM_V2 + 384] = inp["v2"][l - 1]
        for ti in range(2):
            for e in range(2):
                g = 2 * ti + e
                sm[l, e * 64:(e + 1) * 64, SM_PW + ti * 128 + e * 64: SM_PW + ti * 128 + (e + 1) * 64] = inp["pool_w"][l, g]
    return vec, sm


class _Stop(Exception):
    pass


def build(T, depth, stop=0):
    nc = bass.Bass("TRN2", target_bir_lowering=False)
    NT = T // TT
    voff, VW = vec_cols(depth)

    def din(name, shape, dt=F32):
        return nc.dram_tensor(name, shape, dt, kind="ExternalInput").ap()

    xT_d = din("xT", [D, T])
    w_in_d = din("w_in", [depth, D, NIN]); w_out_d = din("w_out", [depth, D, D])
    w_up_d = din("w_up", [depth, D, NUP]); w_down_d = din("w_down", [depth, DFF, D])
    vec_d = din("vec", [128, depth * VW]); sm_d = din("sm", [depth, 128, SM_N])
    cb_d = din("cb", [128, CB_N], BF16); pc_d = din("pc", [128, 32])
    yT_d = nc.dram_tensor("yT", [D, T], F32, kind="ExternalOutput").ap()
    if stop:
        dbgf = nc.dram_tensor("dbgf", [16, 128, 512], F32, kind="ExternalOutput").ap()
        dbgb = nc.dram_tensor("dbgb", [16, 128, 512], BF16, kind="ExternalOutput").ap()
    wbi = nc.dram_tensor("wbi", [depth, D, NIN], BF16, kind="Internal").ap()
    wbo = nc.dram_tensor("wbo", [depth, D, D], BF16, kind="Internal").ap()
    wbu = nc.dram_tensor("wbu", [depth, D, NUP], BF16, kind="Internal").ap()
    wbd = nc.dram_tensor("wbd", [depth, DFF, D], BF16, kind="Internal").ap()
    x1_d = nc.dram_tensor("x1s", [D, T], F32, kind="Internal").ap()
    vf_d = nc.dram_tensor("vfs", [384, T], F32, kind="Internal").ap()
    r_wb = Reg("wbscratch"); r_x1 = Reg("x1s"); r_vf = Reg("vfs")

    P = Prog(nc)
    rr = {"i": 0}

    def R(ts):
        return [t.r if isinstance(t, Tl) else t for t in ts]

    def mm(out, lhsT, rhs, start, stop, rd, wr):
        P.op("pe", lambda e: e.matmul(out, lhsT=lhsT, rhs=rhs, start=start, stop=stop), R(rd), R(wr))

    def tr(out, in_, ident, rd, wr):
        P.op("pe", lambda e: e.transpose(out, in_, ident), R(rd), R(wr))

    def act(out, in_, func, rd, wr, bias=None, scale=None):
        kw = {}
        if bias is not None:
            kw["bias"] = bias
        if scale is not None:
            kw["scale"] = scale
        P.op("act", lambda e: e.activation(out=out, in_=in_, func=func, **kw), R(rd), R(wr))

    def tt(eng, out, a, b, op, rd, wr):
        P.op(eng, lambda e: e.tensor_tensor(out=out, in0=a, in1=b, op=op), R(rd), R(wr))

    def stt(eng, out, in0, scalar, in1, op0, op1, rd, wr):
        P.op(eng, lambda e: e.scalar_tensor_tensor(out=out, in0=in0, scalar=scalar, in1=in1, op0=op0, op1=op1), R(rd), R(wr))

    def ts(eng, out, in0, s1, s2, op0, op1, rd, wr):
        if s2 is None:
            P.op(eng, lambda e: e.tensor_scalar(out=out, in0=in0, scalar1=s1, scalar2=None, op0=op0), R(rd), R(wr))
        else:
            P.op(eng, lambda e: e.tensor_scalar(out=out, in0=in0, scalar1=s1, scalar2=s2, op0=op0, op1=op1), R(rd), R(wr))

    def cp(eng, out, in_, rd, wr):
        if eng == "act":
            P.op("act", lambda e: e.activation(out=out, in_=in_, func=AF.Copy), R(rd), R(wr))
        else:
            P.op(eng, lambda e: e.tensor_copy(out=out, in_=in_), R(rd), R(wr))

    def mset(eng, out, val, wr):
        P.op(eng, lambda e: e.memset(out, val), [], R(wr))

    def dma(q, out, in_, chan, rd, wr, nowaw=False):
        return P.op(q, lambda e: e.dma_start(out=out, in_=in_), R(rd), R(wr), chan=chan, nowaw=nowaw)

    dch = {}

    def dump(tl, ap, idx, bf=False, w=512, np_=128):
        if not stop:
            return
        if "c" not in dch:
            dch["c"] = P.chan("dbg")
        dst = (dbgb if bf else dbgf)[idx, 0:np_, 0:w]
        dma("pool", dst, ap, dch["c"], [tl], [])

    def stage(k):
        if stop == k:
            P.dead = True

    def alt():
        rr["i"] += 1
        return "act" if rr["i"] % 2 else "dve"

    with ExitStack() as G:
        uid = {"i": 0}

        def sb(st, name, shape, dt=F32):
            uid["i"] += 1
            nm = f"s{uid['i']}_{name}"
            return Tl(st.enter_context(nc.sbuf_tensor(nm, shape, dt)), nm)

        banks = [Tl(G.enter_context(nc.psum_tensor(f"pb{i}", [128, 512], F32)), f"pb{i}") for i in range(6)]
        obank = Tl(G.enter_context(nc.psum_tensor("pob", [128, 512], F32)), "pob")
        tbank = Tl(G.enter_context(nc.psum_tensor("ptb", [128, 1024], BF16)), "ptb")
        bk = {"i": 0}

        def bank():
            bk["i"] += 1
            return banks[bk["i"] % 6]

        cb = sb(G, "cb", [128, CB_N], BF16)
        pc = sb(G, "pc", [128, 32])
        vec = sb(G, "vec", [128, depth * VW])
        smb = sb(G, "smb", [128, SM_N], BF16)
        qg8 = sb(G, "qg8", [128, 1])
        ones_f = sb(G, "ones_f", [128, 128])
        wsl = [sb(G, f"wsl{i}", [128, 4096], BF16) for i in range(3)]
        wch = [P.chan(f"w{i}") for i in range(3)]
        kc_ = [[sb(G, f"kc{p}_{t}", [128, 512], BF16) for t in range(NT)] for p in range(3)]
        vc_ = [sb(G, f"vc{m}", [128, 384], BF16) for m in range(NT * 4)]
        hT = [sb(G, f"hT{c}", [128, 512], BF16) for c in range(8)]
        mixT = [sb(G, f"mixT{c}", [128, 512], BF16) for c in range(8)]
        ST = sb(G, "ST", [128, 384])
        STb = [sb(G, f"STb{i}", [128, 384], BF16) for i in range(2)]
        rhalo = sb(G, "rhalo", [128, 11])
        uhalo = [sb(G, f"uhalo{i}", [128, 16]) for i in range(2)]
        chalo = sb(G, "chalo", [128, 44, 2])
        c_misc = P.chan("misc")
        c_x = P.chan("x"); c_o = P.chan("o"); c_vf = P.chan("vf"); c_vfl = [P.chan("vfl0"), P.chan("vfl1")]

        ident = cb.t[:, CB_ID:CB_ID + 128]
        ones_b = cb.t[:, CB_ONES:CB_ONES + 128]
        bones_b = cb.t[:, CB_BONES:CB_BONES + 128]
        maskL = cb.t[:, CB_MASKL:CB_MASKL + 128]
        ntri = cb.t[:, CB_NTRI:CB_NTRI + 128]
        mask4 = cb.t[:, CB_MASK4:CB_MASK4 + 512]

        def amask(d):
            return cb.t[:, CB_AMASK + 512 * d: CB_AMASK + 512 * (d + 1)]

        dma("sp", cb.t[:], cb_d, c_misc, [], [cb])
        dma("sp", pc.t[:], pc_d, c_misc, [], [pc])
        dma("sp", vec.t[:], vec_d, c_misc, [], [vec])
        mset("dve", ones_f.t[:], 1.0, [ones_f])

        def vcol(l, name, j=0):
            o = l * VW + voff[name] + j
            return vec.t[:, o:o + 1]

        with ExitStack() as S:
            stg = [sb(S, f"stg{i}", [128, 2048]) for i in range(4)]
            stb = [sb(S, f"stb{i}", [128, 2048], BF16) for i in range(4)]
            cl = [P.chan(f"pl{i}") for i in range(4)]
            cs = [P.chan(f"ps{i}") for i in range(4)]
            n = 0
            pend = []
            for l in range(depth):
                for (src, dst, K, N) in ((w_in_d, wbi, D, NIN), (w_out_d, wbo, D, D), (w_up_d, wbu, D, NUP), (w_down_d, wbd, DFF, D)):
                    for r0 in range(0, K, 128):
                        for c0 in range(0, N, 2048):
                            w = min(2048, N - c0)
                            i = n % 4
                            n += 1
                            dma("sp", stg[i].t[:, 0:w], src[l, r0:r0 + 128, c0:c0 + w], cl[i], [], [stg[i]])
                            cp(("act", "dve", "pool")[n % 3], stb[i].t[:, 0:w], stg[i].t[:, 0:w], [stg[i]], [stb[i]])
                            pend.append((dst[l, r0:r0 + 128, c0:c0 + w], i, w))
                            if len(pend) > 2:
                                d_, i_, w_ = pend.pop(0)
                                dma("sp", d_, stb[i_].t[:, 0:w_], cs[i_], [stb[i_]], [r_wb], nowaw=True)
            for (d_, i_, w_) in pend:
                dma("sp", d_, stb[i_].t[:, 0:w_], cs[i_], [stb[i_]], [r_wb], nowaw=True)
            P.barrier(ALLQ, chans=cl + cs + [c_misc])

        wn = {"i": 0}

        def load_w(W2d, nk, k0, ranges):
            i = wn["i"] % 3
            wn["i"] += 1
            s = wsl[i]
            wc = sum(n_ for _, n_ in ranges)
            view = s.t[:, 0:nk * wc].rearrange("p (k c) -> p k c", c=wc)
            src = W2d[k0 * 128:(k0 + nk) * 128, :].rearrange("(k p) c -> p k c", p=128)
            o = 0
            first = True
            for (c0, n_) in ranges:
                dma("sp", view[:, :, o:o + n_], src[:, :, c0:c0 + n_], wch[i], [r_wb], [s], nowaw=not first)
                first = False
                o += n_
            return s, view

        try:
            stage(1)
            for l in range(depth):
              x_src = xT_d if l == 0 else x1_d
              x_dst = yT_d if l == depth - 1 else x1_d
              x_src_r = [] if l == 0 else [r_x1]
              x_dst_r = [] if l == depth - 1 else [r_x1]
              with ExitStack() as S:
                  smf = sb(S, "smf", [128, SM_N])
                  dma("pool", smf.t[:], sm_d[l], c_misc, [], [smf])
                  cp("dve", smb.t[:], smf.t[:], [smf], [smb])
                  P.op("act", lambda e, l=l: e.mul(out=qg8.t[:], in_=vcol(l, "q_gain"), mul=0.125), R([vec]), R([qg8]))
                  mset("dve", ST.t[:], 0.0, [ST]); mset("dve", STb[0].t[:], 0.0, [STb[0]])
                  mset("dve", rhalo.t[:], 0.0, [rhalo]); mset("dve", chalo.t[:], 0.0, [chalo])
                  for i in range(2):
                      mset("dve", uhalo[i].t[:], 0.0, [uhalo[i]])
                  P.barrier(COMPUTE, chans=[c_misc])
              stcur = 0
              w2a2 = smb.t[:, SM_W2A2:SM_W2A2 + 384]; g2b = smb.t[:, SM_G2:SM_G2 + 384]
              Win, Wout, Wup, Wdn = wbi[l], wbo[l], wbu[l], wbd[l]

              for it in range(NT):
                  t0 = it * TT

                  def rmsnorm(S, xt, gname):
                      sq = [sb(S, f"sq{i}", [128, 512], BF16) for i in range(2)]
                      lnv = sb(S, "lnv", [128, 512]); rstd = sb(S, "rstd", [128, 512])
                      b = bank()
                      for c in range(8):
                          act(sq[c % 2].t[:], xt[c].t[:], AF.Square, [xt[c]], [sq[c % 2]])
                          mm(b.t[:, :], ones_b, sq[c % 2].t[:], c == 0, c == 7, [cb, sq[c % 2]], [b])
                      act(lnv.t[:], b.t[:, :], AF.Ln, [b], [lnv], bias=1e-6, scale=1.0 / D)
                      act(rstd.t[:], lnv.t[:], AF.Exp, [lnv], [rstd], scale=-0.5)
                      for c in range(8):
                          stt("dve", hT[c].t[:], xt[c].t[:], vcol(l, gname, c), rstd.t[:], ALU.mult, ALU.mult,
                              [xt[c], vec, rstd], [hT[c]])

                  def win_tiles(view, j0, cts):
                      out = []
                      for i, ct in enumerate(cts):
                          b = bank()
                          for kc in range(8):
                              mm(b.t[:, :], view[:, kc, (j0 + i) * 128:(j0 + i + 1) * 128], hT[kc].t[:], kc == 0, kc == 7, [view_s[0], hT[kc]], [b])
                          out.append(b)
                      return out

                  view_s = [None]

                  SA = ExitStack()
                  with ExitStack() as S:
                      S = SA
                      xt = [sb(S, f"xa{c}", [128, 512]) for c in range(8)]
                      for c in range(8):
                          ev_ = dma("pool", xt[c].t[:], x_src[c * 128:(c + 1) * 128, t0:t0 + TT], c_x, x_src_r, [xt[c]])
                          if c == 7 and ev_ is not None:
                              for c2 in range(8):
                                  xt[c2].r.w = ev_
                      rmsnorm(S, xt, "ln1_g")

                  dump(hT[0], hT[0].t[:], 0, bf=True); dump(hT[7], hT[7].t[:], 1, bf=True)
                  stage(2)
                  with ExitStack() as S:
                      S = SA
                      s_, view = load_w(Win, 8, 0, [(20 * 128, 256)])
                      view_s[0] = s_
                      bs = win_tiles(view, 0, [20, 21])
                      ur = [sb(S, f"ur{i}", [128, 528]) for i in range(2)]
                      sA = sb(S, "sA", [128, 528]); sB = sb(S, "sB", [128, 528])
                      pl = sb(S, "pl", [128, 512], BF16)
                      for i in range(2):
                          u = ur[i]
                          cp("act", u.t[:, 16:528], bs[i].t[:, :], [bs[i]], [u])
                          cp("dve", u.t[:, 0:16], uhalo[i].t[:], [uhalo[i]], [u])
                          cp("dve", uhalo[i].t[:], u.t[:, 512:528], [u], [uhalo[i]])
                          tt("dve", sA.t[:, 1:528], u.t[:, 1:528], u.t[:, 0:527], ALU.add, [u], [sA])
                          tt("dve", sB.t[:, 3:528], sA.t[:, 3:528], sA.t[:, 1:526], ALU.add, [sA], [sB])
                          if i == 0:
                              lo, hi, wl, wh = sA, sB, 2.0, 4.0
                          else:
                              tt("dve", sA.t[:, 7:528], sB.t[:, 7:528], sB.t[:, 3:524], ALU.add, [sB], [sA])
                              tt("dve", sB.t[:, 15:528], sA.t[:, 15:528], sA.t[:, 7:520], ALU.add, [sA], [sB])
                              lo, hi, wl, wh = sA, sB, 8.0, 16.0
                          if it == 0:
                              tt("dve", lo.t[0:64, 16:32], lo.t[0:64, 16:32], pc.t[0:64, i * 16:(i + 1) * 16], ALU.mult, [lo, pc], [lo])
                              tt("dve", hi.t[64:128, 16:32], hi.t[64:128, 16:32], pc.t[64:128, i * 16:(i + 1) * 16], ALU.mult, [hi, pc], [hi])
                          stt("dve", pl.t[0:64, :], lo.t[0:64, 16:528], 1.0 / wl, u.t[0:64, 16:528], ALU.mult, ALU.subtract, [lo, u], [pl])
                          stt("dve", pl.t[64:128, :], hi.t[64:128, 16:528], 1.0 / wh, u.t[64:128, 16:528], ALU.mult, ALU.subtract, [hi, u], [pl])
                          b = bank()
                          mm(b.t[:, :], smb.t[:, SM_PW + i * 128: SM_PW + (i + 1) * 128], pl.t[:], True, True, [smb, pl], [b])
                          ts("dve", mixT[6 + i].t[:], b.t[:, :], vcol(l, "pool_b", i), vcol(l, "pool_scale", i), ALU.add, ALU.mult,
                             [b, vec], [mixT[6 + i]])

                  dump(mixT[6], mixT[6].t[:], 2, bf=True); dump(mixT[7], mixT[7].t[:], 3, bf=True)
                  stage(3)
                  with ExitStack() as S:
                      S = SA
                      qT = [sb(S, f"qT{p}", [128, 512], BF16) for p in range(3)]
                      vT = [sb(S, f"vT{p}", [128, 512], BF16) for p in range(3)]
                      sqq = [sb(S, f"sqq{i}", [128, 512], BF16) for i in range(2)]
                      lnq2 = [sb(S, f"lnq{i}", [128, 512]) for i in range(2)]; rsq2 = [sb(S, f"rsq{i}", [128, 512]) for i in range(2)]
                      groups = [[(11 * 128, 512)], [(15 * 128, 512)], [(19 * 128, 128)]]
                      cts = [[11, 12, 13, 14], [15, 16, 17, 18], [19]]
                      nq = 0
                      for gi in range(3):
                          s_, view = load_w(Win, 8, 0, groups[gi])
                          view_s[0] = s_
                          bs = win_tiles(view, 0, cts[gi])
                          for b, ct in zip(bs, cts[gi]):
                              if ct < 17:
                                  isq = ct < 14
                                  p = ct - 11 if isq else ct - 14
                                  sq_ = sqq[nq % 2]; lnq = lnq2[nq % 2]; rsq = rsq2[nq % 2]; nq += 1
                                  act(sq_.t[:], b.t[:, :], AF.Square, [b], [sq_])
                                  b2 = bank()
                                  mm(b2.t[:, :], bones_b, sq_.t[:], True, True, [cb, sq_], [b2])
                                  act(lnq.t[:], b2.t[:, :], AF.Ln, [b2], [lnq], bias=1e-6, scale=1.0 / 64)
                                  act(rsq.t[:], lnq.t[:], AF.Exp, [lnq], [rsq], scale=-0.5)
                                  dst = qT[p] if isq else kc_[p][it]
                                  gcol = qg8.t[:, 0:1] if isq else vcol(l, "k_gain")
                                  stt("dve", dst.t[:], b.t[:, :], gcol, rsq.t[:], ALU.mult, ALU.mult, [b, rsq, qg8, vec], [dst])
                              else:
                                  cp("act", vT[ct - 17].t[:], b.t[:, :], [b], [vT[ct - 17]])
                      for s in range(4):
                          h = s % 2
                          for p in range(3):
                              tr(tbank.t[:, h * 512 + p * 128: h * 512 + (p + 1) * 128], vT[p].t[:, s * 128:(s + 1) * 128], ident, [vT[p], cb], [tbank])
                          cp(alt(), vc_[it * 4 + s].t[:], tbank.t[:, h * 512: h * 512 + 384], [tbank], [vc_[it * 4 + s]])
                      Eb = [sb(S, f"Eb{i}", [128, 512]) for i in range(4)]
                      SPb = [sb(S, f"SPb{i}", [128, 512], BF16) for i in range(4)]
                      Rbc2 = [sb(S, f"Rbc{i}", [128, 512]) for i in range(2)]
                      argb = [sb(S, f"argb{i}", [128, 512]) for i in range(4)]
                      Ab = [sb(S, f"Ab{i}", [128, 512], BF16) for i in range(4)]
                      last = 4 * it + 3
                      units = [(p, m, e_) for p in range(3) for m in range(last, -1, -1) for e_ in range(2)]
                      NU = len(units)
                      ust = [dict() for _ in range(NU)]

                      def c0of(m):
                          d = m - 4 * it
                          return 128 * d if d > 0 else 0

                      def s1(u):
                          p, m, e_ = units[u]; pb = 64 * e_; c0 = c0of(m)
                          kt = kc_[p][m // 4]
                          zb = bank(); ust[u]["zb"] = zb
                          mm(zb.t[:, c0:512], kt.t[pb:pb + 64, (m % 4) * 128:(m % 4 + 1) * 128], qT[p].t[pb:pb + 64, c0:512], True, True, [kt, qT[p]], [zb])

                      def s2(u):
                          p, m, e_ = units[u]; d = m - 4 * it; c0 = c0of(m)
                          zb = ust[u]["zb"]; E, SP = Eb[u % 4], SPb[u % 4]
                          act(E.t[:, c0:512], zb.t[:, c0:512], AF.Exp, [zb], [E])
                          act(SP.t[:, c0:512], E.t[:, c0:512], AF.Ln, [E], [SP], bias=1.0)
                          if d >= 0:
                              tt("pool", SP.t[:, c0:512], SP.t[:, c0:512], amask(d)[:, c0:512], ALU.mult, [SP, cb], [SP])

                      def s3(u):
                          p, m, e_ = units[u]; c0 = c0of(m)
                          zb = ust[u]["zb"]; SP = SPb[u % 4]
                          mm(zb.t[:, c0:512], ntri, SP.t[:, c0:512], False, True, [cb, SP], [zb])
                          if m > 0:
                              rb = bank(); ust[u]["rb"] = rb
                              mm(rb.t[:, c0:512], ones_b, SP.t[:, c0:512], True, True, [cb, SP], [rb])

                      def s4(u):
                          p, m, e_ = units[u]; c0 = c0of(m)
                          zb = ust[u]["zb"]; ar = argb[u % 4]; Rbc = Rbc2[e_]
                          if m != last:
                              tt("dve", ar.t[:, c0:512], zb.t[:, c0:512], Rbc.t[:, c0:512], ALU.subtract, [zb, Rbc], [ar])
                          if m > 0:
                              rb = ust[u]["rb"]
                              if m == last:
                                  if c0 > 0:
                                      mset("pool", Rbc.t[:, 0:c0], 0.0, [Rbc])
                                  cp("dve", Rbc.t[:, c0:512], rb.t[:, c0:512], [rb], [Rbc])
                              else:
                                  tt("dve", Rbc.t[:, c0:512], Rbc.t[:, c0:512], rb.t[:, c0:512], ALU.add, [Rbc, rb], [Rbc])

                      def s5(u):
                          p, m, e_ = units[u]; d = m - 4 * it; c0 = c0of(m)
                          zb = ust[u]["zb"]; ar = argb[u % 4]; A = Ab[u % 4]
                          if m == last:
                              act(A.t[:, c0:512], zb.t[:, c0:512], AF.Exp, [zb], [A])
                          else:
                              act(A.t[:, c0:512], ar.t[:, c0:512], AF.Exp, [ar], [A])
                          if d >= 0:
                              tt("pool", A.t[:, c0:512], A.t[:, c0:512], amask(d)[:, c0:512], ALU.mult, [A, cb], [A])

                      def s6(u):
                          p, m, e_ = units[u]; pb = 64 * e_; h = 2 * p + e_; d = m - 4 * it; c0 = c0of(m)
                          A = Ab[u % 4]
                          vv_ = vc_[m].t[:, h * 64:(h + 1) * 64]
                          mm(obank.t[pb:pb + 64, c0:512], vv_, A.t[:, c0:512], m == last, m == 0, [vc_[m], A], [obank])
                          if m == 0 and e_ == 1:
                              cp("act", mixT[3 + p].t[:], obank.t[:, :], [obank], [mixT[3 + p]])

                      for t in range(NU + 2):
                          if t < NU:
                              s1(t); s2(t)
                          if 0 <= t - 1 < NU:
                              s3(t - 1); s4(t - 1); s5(t - 1)
                          if 0 <= t - 2 < NU:
                              s6(t - 2)
                      P.barrier(COMPUTE, chans=[c_x])
                  SA.close()

                  dump(mixT[3], mixT[3].t[:], 4, bf=True); dump(kc_[0][it], kc_[0][it].t[:], 5, bf=True); dump(vc_[it * 4], vc_[it * 4].t[:], 6, bf=True, w=384); dump(mixT[5], mixT[5].t[:], 7, bf=True)
                  stage(4)
                  with ExitStack() as S:
                      lor = sb(S, "lor", [128, 512], BF16); sgx = sb(S, "sgx", [128, 512], BF16)
                      raw = [sb(S, f"raw{i}", [128, 513]) for i in range(2)]
                      dtmp = sb(S, "dtmp", [128, 512])
                      ART = [sb(S, f"ART{p}", [128, 4, 2, 128], BF16) for p in range(3)]
                      BT = [sb(S, f"BT{p}", [128, 512], BF16) for p in range(3)]
                      KT = [sb(S, f"KT{p}", [128, 512], BF16) for p in range(3)]
                      GC = [sb(S, f"GC{p}", [128, 4]) for p in range(3)]
                      gT = [sb(S, f"gT{p}", [128, 512], BF16) for p in range(3)]
                      bonT = [sb(S, f"bonT{p}", [128, 512], BF16) for p in range(3)]
                      TOK = [sb(S, f"TOK{c}", [128, 4, 384], BF16) for c in range(4)]
                      Lp = sb(S, "Lp", [128, 512]); a_t = sb(S, "a_t", [128, 512]); kkn = sb(S, "kkn", [128, 512])
                      bv = sb(S, "bv", [128, 512]); et = sb(S, "et", [128, 512]); t2 = sb(S, "t2", [128, 512])
                      BHT = sb(S, "BHT", [128, 512], BF16); KHT = sb(S, "KHT", [128, 512], BF16)
                      vb = [sb(S, f"vb{p}", [128, 512], BF16) for p in range(3)]
                      sqk = sb(S, "sqk", [128, 512], BF16); lob = sb(S, "lob", [32, 512], BF16)
                      cLC = sb(S, "cLC", [128, 4])

                      def evac_shift(b, ct, dst, dtmp=dtmp):
                          cp("act", dst.t[:, 1:513], b.t[:, :], [b], [dst])
                          cp("dve", dst.t[:, 0:1], rhalo.t[:, ct:ct + 1], [rhalo], [dst])
                          tt("dve", dtmp.t[:], dst.t[:, 0:512], dst.t[:, 1:513], ALU.subtract, [dst], [dtmp])
                          cp("dve", rhalo.t[:, ct:ct + 1], dst.t[:, 512:513], [dst], [rhalo])
                          stt("dve", dst.t[:, 1:513], dtmp.t[:], vcol(l, "mu", ct), dst.t[:, 1:513], ALU.mult, ALU.add, [dtmp, vec, dst], [dst])

                      s_, view = load_w(Win, 8, 0, [(9 * 128, 256)])
                      view_s[0] = s_
                      bs = win_tiles(view, 0, [9, 10])
                      evac_shift(bs[0], 9, raw[0])
                      act(lor.t[0:64, :], raw[0].t[0:64, 1:513], AF.Tanh, [raw[0]], [lor])
                      cp("dve", lor.t[64:128, :], raw[0].t[64:128, 1:513], [raw[0]], [lor])
                      evac_shift(bs[1], 10, raw[1])
                      act(sgx.t[:], raw[1].t[:, 1:513], AF.Sigmoid, [raw[1]], [sgx])
                      stage(41)
                      vr = [sb(S, f"vr{p}", [128, 513]) for p in range(3)]
                      s_, view = load_w(Win, 8, 0, [(6 * 128, 384)])
                      view_s[0] = s_
                      bs = win_tiles(view, 0, [6, 7, 8])
                      for p in range(3):
                          evac_shift(bs[p], 6 + p, vr[p])
                      if l > 0:
                          for p in range(3):
                              cp("pool", vb[p].t[:], vr[p].t[:, 1:513], [vr[p]], [vb[p]])
                          b = bank()
                          for pp in range(3):
                              mm(b.t[0:32, :], smb.t[:, SM_V1 + pp * 32: SM_V1 + (pp + 1) * 32], vb[pp].t[:], pp == 0, pp == 2, [smb, vb[pp]], [b])
                          cp("act", lob.t[:], b.t[0:32, :], [b], [lob])
                      rawS = [raw, [sb(S, f"rawb{i}", [128, 513]) for i in range(2)]]
                      dtmpS = [dtmp, sb(S, "dtmpb", [128, 512])]
                      t2S = [t2, sb(S, "t2b", [128, 512])]; etS = [et, sb(S, "etb", [128, 512])]
                      LpS = [Lp, sb(S, "Lpb", [128, 512])]; a_tS = [a_t, sb(S, "a_tb", [128, 512])]

                      def prep_early(p):
                          s2_ = p % 2
                          raw_, dtmp_, t2, et, Lp, a_t = rawS[s2_], dtmpS[s2_], t2S[s2_], etS[s2_], LpS[s2_], a_tS[s2_]
                          pc0 = p * 128
                          s_, view = load_w(Win, 8, 0, [(p * 128, 128), ((3 + p) * 128, 128)])
                          view_s[0] = s_
                          bs = win_tiles(view, 0, [p, 3 + p])
                          evac_shift(bs[0], p, raw_[0], dtmp_)
                          evac_shift(bs[1], 3 + p, raw_[1], dtmp_)
                          rw = [raw_[0], raw_[1], vr[p]]
                          r_, k_, v_ = rw[0], rw[1], rw[2]
                          rT, kTr, vTr = r_.t[:, 1:513], k_.t[:, 1:513], v_.t[:, 1:513]
                          if l == 0:
                              dma("pool", vf_d[pc0:pc0 + 128, t0:t0 + TT], vTr, c_vf, [v_], [r_vf], nowaw=True)
                          else:
                              b = bank()
                              mm(b.t[:, :], smb.t[0:32, SM_V2 + pc0: SM_V2 + pc0 + 128], lob.t[:], True, True, [smb, lob], [b])
                              act(et.t[:], b.t[:, :], AF.Sigmoid, [b], [et], bias=vcol(l, "v0", p))
                              dma("pool", t2.t[:], vf_d[pc0:pc0 + 128, t0:t0 + TT], c_vfl[s2_], [r_vf], [t2])
                              tt("dve", t2.t[:], t2.t[:], vTr, ALU.subtract, [t2, v_], [t2])
                              tt("dve", t2.t[:], t2.t[:], et.t[:], ALU.mult, [t2, et], [t2])
                              tt("dve", vTr, vTr, t2.t[:], ALU.add, [v_, t2], [v_])
                          cp("pool", vb[p].t[:], vTr, [v_], [vb[p]])
                          b = bank()
                          mm(b.t[:, :], w2a2[0:64, pc0:pc0 + 128], lor.t[0:64, :], True, True, [smb, lor], [b])
                          act(et.t[:], b.t[:, :], AF.Sigmoid, [b], [et], bias=vcol(l, "w0", p))
                          for c in range(4):
                              P.op("dve", (lambda o_, d0_, d1_: lambda e: e.tensor_tensor_scan(out=o_, data0=d0_, data1=d1_, initial=0.0, op0=ALU.mult, op1=ALU.add))(
                              Lp.t[:, c * 128:(c + 1) * 128], ones_f.t[:], et.t[:, c * 128:(c + 1) * 128]), R([ones_f, et]), R([Lp]))
                          b = bank()
                          mm(b.t[:, :], w2a2[64:128, pc0:pc0 + 128], lor.t[64:128, :], True, True, [smb, lor], [b])
                          act(a_t.t[:], b.t[:, :], AF.Sigmoid, [b], [a_t], bias=vcol(l, "a0", p))
                          b = bank()
                          mm(b.t[:, :], g2b[:, pc0:pc0 + 128], sgx.t[:], True, True, [smb, sgx], [b])
                          cp("act", gT[p].t[:], b.t[:, :], [b], [gT[p]])

                      def prep_late(p):
                          s2_ = p % 2
                          raw_, dtmp_, t2, et, Lp, a_t = rawS[s2_], dtmpS[s2_], t2S[s2_], etS[s2_], LpS[s2_], a_tS[s2_]
                          pc0 = p * 128
                          r_, k_, v_ = raw_[0], raw_[1], vr[p]
                          rT, kTr, vTr = r_.t[:, 1:513], k_.t[:, 1:513], v_.t[:, 1:513]
                          act(sqk.t[:], kTr, AF.Square, [k_, vec], [sqk], scale=vcol(l, "k_k", p))
                          b = bank()
                          mm(b.t[:, :], bones_b, sqk.t[:], True, True, [cb, sqk], [b])
                          act(t2.t[:], b.t[:, :], AF.Ln, [b], [t2], bias=1e-24)
                          act(t2.t[:], t2.t[:], AF.Exp, [t2], [t2], scale=-0.5)
                          stt("dve", kkn.t[:], kTr, vcol(l, "k_k", p), t2.t[:], ALU.mult, ALU.mult, [k_, vec, t2], [kkn])
                          tt("dve", bv.t[:], kkn.t[:], a_t.t[:], ALU.mult, [kkn, a_t], [bv])
                          ts("dve", t2.t[:], a_t.t[:], -1.0, vcol(l, "k_a", p), ALU.add, ALU.mult, [a_t, vec], [t2])
                          stt("dve", kTr, t2.t[:], 1.0, kTr, ALU.add, ALU.mult, [t2, k_], [k_])
                          stt("dve", sqk.t[:], rT, vcol(l, "r_k", p), kTr, ALU.mult, ALU.mult, [r_, k_, vec], [sqk])
                          b = bank()
                          mm(b.t[:, :], bones_b, sqk.t[:], True, True, [cb, sqk], [b])
                          tt("dve", bonT[p].t[:], b.t[:, :], vTr, ALU.mult, [b, v_], [bonT[p]])
                          A4 = ART[p].t[:, :, 0, :]; R4 = ART[p].t[:, :, 1, :]
                          v4 = lambda ap: ap.rearrange("p (c t) -> p c t", t=128)
                          tt("dve", t2.t[:], Lp.t[:], et.t[:], ALU.subtract, [Lp, et], [t2])
                          act(t2.t[:], t2.t[:], AF.Exp, [t2], [t2], scale=CDEC)
                          stt("dve", A4, v4(kkn.t[:]), -1.0, v4(t2.t[:]), ALU.mult, ALU.mult, [kkn, t2], [ART[p]])
                          act(t2.t[:], Lp.t[:], AF.Exp, [Lp], [t2], scale=CDEC)
                          tt("dve", R4, v4(rT), v4(t2.t[:]), ALU.mult, [r_, t2], [ART[p]])
                          for c in range(4):
                              cp("dve", GC[p].t[:, c:c + 1], t2.t[:, c * 128 + 127: c * 128 + 128], [t2], [GC[p]])
                          act(t2.t[:], Lp.t[:], AF.Exp, [Lp], [t2], scale=-CDEC)
                          tt("dve", BT[p].t[:], bv.t[:], t2.t[:], ALU.mult, [bv, t2], [BT[p]])
                          tt("dve", KT[p].t[:], kTr, t2.t[:], ALU.mult, [k_, t2], [KT[p]])
                          for c in range(4):
                              ts("dve", cLC.t[:, c:c + 1], Lp.t[:, c * 128 + 127: c * 128 + 128], CDEC, None, ALU.mult, None, [Lp], [cLC])
                          for c in range(4):
                              act(t2.t[:, c * 128:(c + 1) * 128], Lp.t[:, c * 128:(c + 1) * 128], AF.Exp, [Lp, cLC], [t2],
                                  scale=-CDEC, bias=cLC.t[:, c:c + 1])
                          tt("dve", BHT.t[:], bv.t[:], t2.t[:], ALU.mult, [bv, t2], [BHT])
                          tt("dve", KHT.t[:], kTr, t2.t[:], ALU.mult, [k_, t2], [KHT])
                          for c in range(4):
                              h = c % 2
                              srcs = [(ART[p], ART[p].t[:, c, 0, :]), (BHT, BHT.t[:, c * 128:(c + 1) * 128]),
                                      (KHT, KHT.t[:, c * 128:(c + 1) * 128]), (vb[p], vb[p].t[:, c * 128:(c + 1) * 128])]
                              for kd, (tl_, ap_) in enumerate(srcs):
                                  tr(tbank.t[:, h * 512 + kd * 128: h * 512 + (kd + 1) * 128], ap_, ident, [tl_, cb], [tbank])
                              cp(alt(), TOK[c].t[:, :, pc0:pc0 + 128],
                                 tbank.t[:, h * 512:(h + 1) * 512].rearrange("p (k c) -> p k c", c=128), [tbank], [TOK[c]])


                      prep_early(0)
                      for p in range(3):
                          if p + 1 < 3:
                              prep_early(p + 1)
                          prep_late(p)
                      stage(42)
                      SCb = [sb(S, f"SCb{h}", [128, 512], BF16) for h in range(6)]
                      XTb = [sb(S, f"XTb{h}", [128, 128], BF16) for h in range(6)]
                      PP = [[sb(S, f"PP{p}_{i}", [128, 512], BF16) for i in range(2)] for p in range(3)]
                      ACC = [[sb(S, f"ACC{p}_{i}", [128, 256], BF16) for i in range(2)] for p in range(3)]
                      T1b = sb(S, "T1b", [128, 384], BF16); UH = sb(S, "UH", [128, 384]); WhT = sb(S, "WhT", [128, 384], BF16)
                      Ub = sb(S, "Ub", [128, 384], BF16); ysq = sb(S, "ysq", [128, 384]); yn = sb(S, "yn", [128, 384], BF16)
                      st1 = sb(S, "st1", [128, 6]); st2 = sb(S, "st2", [128, 6]); st3 = sb(S, "st3", [128, 6])
                      YT = sb(S, "YT", [128, 3, 512], BF16)
                      for c in range(4):
                          tok = TOK[c]
                          csl = slice(c * 128, (c + 1) * 128)
                          for p in range(3):
                              for e_ in range(2):
                                  pb = 64 * e_; h = 2 * p + e_
                                  b = bank()
                                  ar2 = ART[p].t[pb:pb + 64, c, :, :].rearrange("p a t -> p (a t)")
                                  mm(b.t[:, 0:256], BT[p].t[pb:pb + 64, csl], ar2, True, True, [BT[p], ART[p]], [b])
                                  mm(b.t[:, 256:512], KT[p].t[pb:pb + 64, csl], ar2, True, True, [KT[p], ART[p]], [b])
                                  tt("dve", SCb[h].t[:], b.t[:, :], mask4, ALU.mult, [b, cb], [SCb[h]])
                                  b = bank()
                                  mm(b.t[:, 0:128], ART[p].t[pb:pb + 64, c, 0, :], BT[p].t[pb:pb + 64, csl], True, True, [BT[p], ART[p]], [b])
                                  tt("dve", XTb[h].t[:], b.t[:, 0:128], maskL, ALU.mult, [b, cb], [XTb[h]])
                          stage(43)
                          b = bank()
                          for h in range(6):
                              mm(b.t[:, h * 64:(h + 1) * 64], SCb[h].t[:, 256:384], tok.t[:, 3, h * 64:(h + 1) * 64], True, True, [SCb[h], tok], [b])
                          cp("act", T1b.t[:], b.t[:, 0:384], [b], [T1b])
                          cur = [0, 0, 0]
                          for p in range(3):
                              for e_ in range(2):
                                  h = 2 * p + e_
                                  tt("pool", ACC[p][0].t[:, e_ * 128:(e_ + 1) * 128], SCb[h].t[:, 0:128], ident, ALU.add, [SCb[h], cb], [ACC[p][0]])
                          for k in range(1, 7):
                              i0 = (k - 1) % 2; i1 = k % 2
                              bq = []
                              for p in range(3):
                                  b = bank(); bq.append(b)
                                  for e_ in range(2):
                                      h = 2 * p + e_
                                      if k == 1:
                                          Pm, PTm, rds = SCb[h].t[:, 0:128], XTb[h].t[:], [SCb[h], XTb[h]]
                                      else:
                                          Pm = PP[p][i0].t[:, e_ * 256: e_ * 256 + 128]; PTm = PP[p][i0].t[:, e_ * 256 + 128: e_ * 256 + 256]
                                          rds = [PP[p][i0]]
                                      if k < 6:
                                          mm(b.t[:, e_ * 256: e_ * 256 + 128], PTm, Pm, True, True, rds, [b])
                                      mm(b.t[:, e_ * 256 + 128: e_ * 256 + 256], Pm, PTm, True, True, rds, [b])
                              for p in range(3):
                                  b = bq[p]
                                  if k < 6:
                                      cp(alt(), PP[p][i1].t[:], b.t[:, :], [b], [PP[p][i1]])
                                  else:
                                      for e_ in range(2):
                                          cp(alt(), PP[p][i1].t[:, e_ * 256 + 128: e_ * 256 + 256], b.t[:, e_ * 256 + 128: e_ * 256 + 256], [b], [PP[p][i1]])
                              bq2 = []
                              for p in range(3):
                                  b2 = bank(); bq2.append(b2)
                                  for e_ in range(2):
                                      mm(b2.t[:, e_ * 128:(e_ + 1) * 128], PP[p][i1].t[:, e_ * 256 + 128: e_ * 256 + 256],
                                         ACC[p][i0].t[:, e_ * 128:(e_ + 1) * 128], True, True, [PP[p][i1], ACC[p][i0]], [b2])
                              for p in range(3):
                                  tt("dve", ACC[p][i1].t[:], ACC[p][i0].t[:], bq2[p].t[:, 0:256], ALU.add, [ACC[p][i0], bq2[p]], [ACC[p][i1]])
                          stage(44)
                          NTf = [ACC[p][0] for p in range(3)]
                          b = bank()
                          for h in range(6):
                              p, e_ = h // 2, h % 2
                              mm(b.t[64 * e_:64 * e_ + 64, p * 128:(p + 1) * 128], tok.t[:, 0, h * 64:(h + 1) * 64], NTf[p].t[:, e_ * 128:(e_ + 1) * 128],
                                 True, True, [tok, NTf[p]], [b])
                          cp("act", WhT.t[:], b.t[:, 0:384], [b], [WhT])
                          b = bank()
                          for h in range(6):
                              p, e_ = h // 2, h % 2
                              mm(b.t[:, h * 64:(h + 1) * 64], NTf[p].t[:, e_ * 128:(e_ + 1) * 128], T1b.t[:, h * 64:(h + 1) * 64], True, True, [NTf[p], T1b], [b])
                          cp("act", UH.t[:], b.t[:, 0:384], [b], [UH])
                          stage(45)
                          so = STb[stcur]; sn = STb[1 - stcur]
                          b = bank()
                          for p in range(3):
                              mm(b.t[:, p * 128:(p + 1) * 128], WhT.t[:, p * 128:(p + 1) * 128], so.t[:, p * 128:(p + 1) * 128], True, True, [WhT, so], [b])
                          tt("dve", Ub.t[:], b.t[:, 0:384], UH.t[:], ALU.add, [b, UH], [Ub])
                          stage(451)
                          yb = bank()
                          for h in range(6):
                              p, e_ = h // 2, h % 2; pb = 64 * e_
                              hs = slice(h * 64, (h + 1) * 64)
                              if e_ == 0:
                                  mm(yb.t[:, p * 128:(p + 1) * 128], ART[p].t[:, c, 1, :], so.t[:, p * 128:(p + 1) * 128], True, False, [ART[p], so], [yb])
                              mm(yb.t[:, hs], SCb[h].t[:, 128:256], Ub.t[:, hs], False, False, [SCb[h], Ub], [yb])
                              mm(yb.t[:, hs], SCb[h].t[:, 384:512], tok.t[:, 3, hs], False, True, [SCb[h], tok], [yb])
                          stage(452)
                          sbk = bank()
                          for h in range(6):
                              p, e_ = h // 2, h % 2; pb = 64 * e_
                              hs = slice(h * 64, (h + 1) * 64)
                              mm(sbk.t[pb:pb + 64, hs], tok.t[:, 1, hs], Ub.t[:, hs], True, False, [tok, Ub], [sbk])
                              mm(sbk.t[pb:pb + 64, hs], tok.t[:, 2, hs], tok.t[:, 3, hs], False, True, [tok], [sbk])
                          stage(453)
                          for h in range(6):
                              p, e_ = h // 2, h % 2; pb = 64 * e_
                              hs = slice(h * 64, (h + 1) * 64)
                              stt("dve", ST.t[pb:pb + 64, hs], ST.t[pb:pb + 64, hs], GC[p].t[pb:pb + 64, c:c + 1], sbk.t[pb:pb + 64, hs],
                                  ALU.mult, ALU.add, [ST, GC[p], sbk], [ST])
                          cp("act", sn.t[:], ST.t[:], [ST], [sn])
                          stcur = 1 - stcur
                          stage(46)
                          y3 = yb.t[:, 0:384].rearrange("p (h i) -> p h i", i=64)
                          P.op("dve", (lambda o_, i_: lambda e: e.tensor_reduce(out=o_, in_=i_, axis=AX.X, op=ALU.add))(st1.t[:], y3), R([yb]), R([st1]))
                          act(ysq.t[:], yb.t[:, 0:384], AF.Square, [yb], [ysq])
                          P.op("dve", (lambda o_, i_: lambda e: e.tensor_reduce(out=o_, in_=i_, axis=AX.X, op=ALU.add))(
                              st2.t[:], ysq.t[:].rearrange("p (h i) -> p h i", i=64)), R([ysq]), R([st2]))
                          ts("dve", st1.t[:], st1.t[:], 1.0 / 64, None, ALU.mult, None, [st1], [st1])
                          tt("dve", st3.t[:], st1.t[:], st1.t[:], ALU.mult, [st1], [st3])
                          stt("dve", st2.t[:], st2.t[:], 1.0 / 64, st3.t[:], ALU.mult, ALU.subtract, [st2, st3], [st2])
                          act(st2.t[:], st2.t[:], AF.Ln, [st2], [st2], bias=64e-5)
                          act(st2.t[:], st2.t[:], AF.Exp, [st2], [st2], scale=-0.5)
                          for h in range(6):
                              hs = slice(h * 64, (h + 1) * 64)
                              ts("dve", yn.t[:, hs], yb.t[:, hs], st1.t[:, h:h + 1], st2.t[:, h:h + 1], ALU.subtract, ALU.mult, [yb, st1, st2], [yn])
                          stage(47)
                          hh = c % 2
                          for p in range(3):
                              tr(tbank.t[:, hh * 512 + p * 128: hh * 512 + (p + 1) * 128], yn.t[:, p * 128:(p + 1) * 128], ident, [yn, cb], [tbank])
                          cp("act", YT.t[:, :, csl], tbank.t[:, hh * 512: hh * 512 + 384].rearrange("p (k c) -> p k c", c=128), [tbank], [YT])
                      for p in range(3):
                          act(t2.t[:], YT.t[:, p, :], AF.Identity, [YT, vec], [t2], scale=vcol(l, "lnx_w", p), bias=vcol(l, "lnx_b", p))
                          tt("dve", t2.t[:], t2.t[:], bonT[p].t[:], ALU.add, [t2, bonT[p]], [t2])
                          tt("dve", mixT[p].t[:], t2.t[:], gT[p].t[:], ALU.mult, [t2, gT[p]], [mixT[p]])
                      P.barrier(COMPUTE, chans=[c_vf] + c_vfl)

                  dump(mixT[0], mixT[0].t[:], 8, bf=True); dump(mixT[1], mixT[1].t[:], 9, bf=True); dump(mixT[2], mixT[2].t[:], 10, bf=True)
                  stage(5)
                  with ExitStack() as S:
                      xt = [sb(S, f"xa{c}", [128, 512]) for c in range(8)]
                      for c in range(8):
                          ev_ = dma("pool", xt[c].t[:], x_src[c * 128:(c + 1) * 128, t0:t0 + TT], c_x, x_src_r, [xt[c]])
                          if c == 7 and ev_ is not None:
                              for c2 in range(8):
                                  xt[c2].r.w = ev_
                      for g in range(2):
                          s_, view = load_w(Wout, 8, 0, [(g * 512, 512)])
                          for j in range(4):
                              dt_ = g * 4 + j
                              b = bank()
                              for kc in range(8):
                                  mm(b.t[:, :], view[:, kc, j * 128:(j + 1) * 128], mixT[kc].t[:], kc == 0, kc == 7, [s_, mixT[kc]], [b])
                              tt("dve", xt[dt_].t[:], xt[dt_].t[:], b.t[:, :], ALU.add, [xt[dt_], b], [xt[dt_]])
                      rmsnorm(S, xt, "ln2_g")
                      graw = [sb(S, f"graw{i}", [128, 514]) for i in range(2)]
                      vraw = [sb(S, f"vraw{i}", [128, 514]) for i in range(2)]
                      cg = [sb(S, f"cg{i}", [128, 512]) for i in range(2)]
                      cv = [sb(S, f"cv{i}", [128, 512]) for i in range(2)]
                      gv = [sb(S, f"gv{i}", [128, 512], BF16) for i in range(11)]
                      nf = 0
                      for half in range(2):
                          for (g0, ng) in ((0, 4), (4, 4), (8, 3)):
                              ctg = half * 11 + g0
                              sg_, vg = load_w(Wup, 8, 0, [(ctg * 128, ng * 128)])
                              sv_, vv = load_w(Wup, 8, 0, [((22 + ctg) * 128, ng * 128)])
                              for j in range(ng):
                                  i2 = nf % 2; nf += 1
                                  ct = ctg + j
                                  res = []
                                  for (sl_, vw_, cti, rawt, cout) in ((sg_, vg, ct, graw[i2], cg[i2]), (sv_, vv, 22 + ct, vraw[i2], cv[i2])):
                                      b = bank()
                                      for kc in range(8):
                                          mm(b.t[:, :], vw_[:, kc, j * 128:(j + 1) * 128], hT[kc].t[:], kc == 0, kc == 7, [sl_, hT[kc]], [b])
                                      cp("act", rawt.t[:, 2:514], b.t[:, :], [b], [rawt])
                                      cp("pool", rawt.t[:, 0:2], chalo.t[:, cti, :], [chalo], [rawt])
                                      cp("pool", chalo.t[:, cti, :], rawt.t[:, 512:514], [rawt], [chalo])
                                      act(cout.t[:], b.t[:, :], AF.Identity, [b, vec], [cout], scale=vcol(l, "conv_w", 2 * 44 + cti), bias=vcol(l, "conv_b", cti))
                                      stt("dve", cout.t[:], rawt.t[:, 1:513], vcol(l, "conv_w", 44 + cti), cout.t[:], ALU.mult, ALU.add, [rawt, vec, cout], [cout])
                                      stt("dve", cout.t[:], rawt.t[:, 0:512], vcol(l, "conv_w", cti), cout.t[:], ALU.mult, ALU.add, [rawt, vec, cout], [cout])
                                  act(cg[i2].t[:], cg[i2].t[:], AF.Silu, [cg[i2]], [cg[i2]])
                                  tt("pool", gv[g0 + j].t[:], cg[i2].t[:], cv[i2].t[:], ALU.mult, [cg[i2], cv[i2]], [gv[g0 + j]])
                          for dq in range(4):
                              s_, view = load_w(Wdn, 11, half * 11, [(dq * 256, 256)])
                              for j in range(2):
                                  dt_ = dq * 2 + j
                                  b = bank()
                                  for kc in range(11):
                                      mm(b.t[:, :], view[:, kc, j * 128:(j + 1) * 128], gv[kc].t[:], kc == 0, kc == 10, [s_, gv[kc]], [b])
                                  tt("dve", xt[dt_].t[:], xt[dt_].t[:], b.t[:, :], ALU.add, [xt[dt_], b], [xt[dt_]])
                      for c in range(8):
                          dma("pool", x_dst[c * 128:(c + 1) * 128, t0:t0 + TT], xt[c].t[:], c_o, [xt[c]], x_dst_r, nowaw=True)
                      P.barrier(COMPUTE, chans=[c_o, c_x])

        except _Stop:
            pass
        P.dead = False
        P.barrier(ALLQ, chans=P.chans)
        P.emit()
    return nc, P


_CACHE = {}


def run(inputs, T, depth, n_cores, stop=0):
    vec, sm = pack_params(inputs, depth)
    cbv, pcv = pack_consts()
    key = (T, depth, stop)
    if key not in _CACHE:
        _CACHE[key] = build(T, depth, stop)[0]
    nc = _CACHE[key]
    x = np.asarray(inputs["x"], np.float32)
    shared = {
        "w_in": np.ascontiguousarray(np.asarray(inputs["w_in"], np.float32)[:depth]),
        "w_out": np.ascontiguousarray(np.asarray(inputs["w_out"], np.float32)[:depth]),
        "w_up": np.ascontiguousarray(np.asarray(inputs["w_up"], np.float32)[:depth]),
        "w_down": np.ascontiguousarray(np.asarray(inputs["w_down"], np.float32)[:depth]),
        "vec": vec, "sm": sm, "cb": cbv, "pc": pcv,
    }
    in_maps = []
    for b in range(n_cores):
        m = dict(shared)
        m["xT"] = np.ascontiguousarray(x[b, :T].T)
        in_maps.append(m)
    res = run_bass_kernel_spmd(nc, in_maps, core_ids=list(range(n_cores)))
    if stop:
        return res.results[0]
    out = np.stack([np.ascontiguousarray(res.results[b]["yT"].T) for b in range(n_cores)], 0)
    return out.astype(np.float32)


def kernel(**inputs):
    inputs = {k: np.asarray(v) for k, v in inputs.items()}
    x = inputs["x"]
    return run(inputs, x.shape[1], 2, x.shape[0])
```

```python
from contextlib import ExitStack
import numpy as np
import ml_dtypes
import concourse.bass as bass
import concourse.mybir as mybir
from concourse.bass_utils import run_bass_kernel_spmd

F32 = mybir.dt.float32
BF16 = mybir.dt.bfloat16
AF = mybir.ActivationFunctionType
ALU = mybir.AluOpType
AX = mybir.AxisListType

D = 1024
NIN = 2816
DFF = 2816
NUP = 5632
TT = 512
CDEC = -float(np.exp(-0.5))

COMPUTE = ("pe", "act", "dve", "pool")
ALLQ = ("pe", "act", "dve", "pool", "sp")


class Reg:
    __slots__ = ("name", "w", "rs")

    def __init__(self, name):
        self.name = name
        self.w = None
        self.rs = {}


class Chan:
    def __init__(self, prog, name):
        self.key = "dma_" + name
        prog.semkeys.append(self.key)
        self.cnt = 0


class Prog:
    def __init__(self, nc):
        self.nc = nc
        self.q = {e: [] for e in ALLQ}
        self.cnt = {e: 0 for e in COMPUTE}
        self.known = {e: {} for e in ALLQ}
        self.semkeys = ["prog_" + e for e in COMPUTE]
        self.chans = []
        self.nins = 0
        self.nwait = 0
        self.dead = False

    def chan(self, name):
        c = Chan(self, name)
        self.chans.append(c)
        return c

    def op(self, eng, fn, reads=(), writes=(), chan=None, nowaw=False):
        if self.dead:
            return None
        need = {}

        def req(k, v):
            if need.get(k, 0) < v:
                need[k] = v

        me = None if chan is not None else eng
        for r in reads:
            if r.w is not None:
                req(r.w[0], r.w[1])
        for w in writes:
            if w.w is not None and (w.w[2] is None or w.w[2] != me) and not nowaw:
                req(w.w[0], w.w[1])
            for k, (v, e) in w.rs.items():
                if e is None or e != me:
                    req(k, v)
        kn = self.known[eng]
        for k, v in need.items():
            if kn.get(k, 0) < v:
                self.q[eng].append(("w", k, v))
                kn[k] = v
                self.nwait += 1
        if chan is not None:
            chan.cnt += 1
            ev = (chan.key, 16 * chan.cnt, None)
            self.q[eng].append(("i", fn, chan.key, 16))
        else:
            self.cnt[eng] += 1
            ev = ("prog_" + eng, self.cnt[eng], eng)
            self.q[eng].append(("i", fn, ev[0], 1))
        self.nins += 1
        for r in reads:
            old = r.rs.get(ev[0])
            if old is None or old[0] < ev[1]:
                r.rs[ev[0]] = (ev[1], ev[2])
        for w in writes:
            w.w = ev
            w.rs = {}
        return ev

    def wait(self, eng, key, val):
        if self.dead:
            return
        kn = self.known[eng]
        if val > 0 and kn.get(key, 0) < val:
            self.q[eng].append(("w", key, val))
            kn[key] = val
            self.nwait += 1

    def barrier(self, engs=COMPUTE, chans=()):
        for e in engs:
            for x in engs:
                if x != e and x in self.cnt:
                    self.wait(e, "prog_" + x, self.cnt[x])
            for c in chans:
                self.wait(e, c.key, 16 * c.cnt)

    def emit(self):
        nc = self.nc
        with ExitStack() as st:
            sems = {k: st.enter_context(nc.semaphore(k)) for k in self.semkeys}
            block = st.enter_context(nc.Block())

            def run(e, items):
                for it in items:
                    if it[0] == "w":
                        e.wait_ge(sems[it[1]], it[2])
                    else:
                        it[1](e).then_inc(sems[it[2]], it[3])

            @block.tensor
            def _(e):
                run(e, self.q["pe"])

            @block.scalar
            def _(e):
                run(e, self.q["act"])

            @block.vector
            def _(e):
                run(e, self.q["dve"])

            @block.gpsimd
            def _(e):
                run(e, self.q["pool"])

            @block.sync
            def _(e):
                run(e, self.q["sp"])


class Tl:
    __slots__ = ("t", "r", "name")

    def __init__(self, t, name):
        self.t = t
        self.name = name
        self.r = Reg(name)


def vec_cols(depth):
    spec = [("ln1_g", 8), ("ln2_g", 8), ("mu", 11), ("w0", 3), ("a0", 3), ("k_k", 3), ("k_a", 3), ("r_k", 3),
            ("lnx_w", 3), ("lnx_b", 3), ("v0", 3), ("q_gain", 1), ("k_gain", 1), ("pool_b", 2),
            ("pool_scale", 2), ("conv_w", 132), ("conv_b", 44)]
    off, o = {}, 0
    for n, c in spec:
        off[n] = o
        o += c
    return off, o


SM_W2A2, SM_G2, SM_V1, SM_V2, SM_PW, SM_N = 0, 384, 768, 864, 1248, 1504
CB_ID, CB_ONES, CB_BONES, CB_MASKL, CB_NTRI, CB_MASK4, CB_AMASK, CB_N = 0, 128, 256, 384, 512, 640, 1152, 3200


def pack_consts():
    c = np.zeros((128, CB_N), np.float32)
    i = np.arange(128)
    c[:, CB_ID:CB_ID + 128] = np.eye(128)
    c[:, CB_ONES:CB_ONES + 128] = 1.0
    c[:, CB_BONES:CB_BONES + 128] = (i[:, None] // 64 == i[None, :] // 64)
    c[:, CB_MASKL:CB_MASKL + 128] = (i[None, :] < i[:, None])
    c[:, CB_NTRI:CB_NTRI + 128] = -(i[:, None] >= i[None, :]).astype(np.float32)
    su = (i[:, None] < i[None, :]).astype(np.float32)
    iu = (i[:, None] <= i[None, :]).astype(np.float32)
    c[:, CB_MASK4:CB_MASK4 + 512] = np.concatenate([su, iu, su, iu], 1)
    q = np.arange(512)
    for d in range(4):
        c[:, CB_AMASK + 512 * d:CB_AMASK + 512 * (d + 1)] = (128 * d + i[:, None] < q[None, :])
    pc = np.ones((128, 2, 16), np.float32)
    wins = (2, 4, 8, 16)
    for ti in range(2):
        for e in range(2):
            w = wins[2 * ti + e]
            t = np.arange(16)
            pc[e * 64:(e + 1) * 64, ti, :] = (w / np.minimum(t + 1, w))[None, :]
    return c.astype(ml_dtypes.bfloat16), pc.reshape(128, 32)


def pack_params(inp, depth):
    off, W = vec_cols(depth)
    vec = np.zeros((128, depth * W), np.float32)
    sm = np.zeros((depth, 128, SM_N), np.float32)

    def put(l, name, v):
        v = np.asarray(v, np.float32).reshape(-1, 128).T
        vec[:, l * W + off[name]: l * W + off[name] + v.shape[1]] = v

    for l in range(depth):
        put(l, "ln1_g", inp["ln1_g"][l]); put(l, "ln2_g", inp["ln2_g"][l]); put(l, "mu", inp["mu_shift"][l])
        for n in ("w0", "a0", "k_k", "k_a", "lnx_w", "lnx_b"):
            put(l, n, inp[n][l])
        put(l, "r_k", inp["r_k"][l].reshape(-1))
        if l > 0:
            put(l, "v0", inp["v0"][l - 1])
        put(l, "q_gain", np.tile(inp["q_gain"][l], 2)); put(l, "k_gain", np.tile(inp["k_gain"][l], 2))
        put(l, "pool_b", inp["pool_b"][l].reshape(-1)); put(l, "pool_scale", inp["pool_scale"][l])
        put(l, "conv_w", inp["conv_w"][l].reshape(-1)); put(l, "conv_b", inp["conv_b"][l])
        sm[l, 0:64, SM_W2A2:SM_W2A2 + 384] = inp["w2"][l]
        sm[l, 64:128, SM_W2A2:SM_W2A2 + 384] = inp["a2"][l]
        sm[l, :, SM_G2:SM_G2 + 384] = inp["g2"][l]
        if l > 0:
            v1 = np.asarray(inp["v1"][l - 1]).reshape(3, 128, 32).transpose(1, 0, 2).reshape(128, 96)
            sm[l, :, SM_V1:SM_V1 + 96] = v1
            sm[l, 0:32, SM_V2:SM_V2 + 384] = inp["v2"][l - 1]
        for ti in range(2):
            for e in range(2):
                g = 2 * ti + e
                sm[l, e * 64:(e + 1) * 64, SM_PW + ti * 128 + e * 64: SM_PW + ti * 128 + (e + 1) * 64] = inp["pool_w"][l, g]
    return vec, sm


class _Stop(Exception):
    pass


def build(T, depth, stop=0):
    nc = bass.Bass("TRN2", target_bir_lowering=False)
    NT = T // TT
    voff, VW = vec_cols(depth)

    def din(name, shape, dt=F32):
        return nc.dram_tensor(name, shape, dt, kind="ExternalInput").ap()

    xT_d = din("xT", [D, T])
    w_in_d = din("w_in", [depth, D, NIN]); w_out_d = din("w_out", [depth, D, D])
    w_up_d = din("w_up", [depth, D, NUP]); w_down_d = din("w_down", [depth, DFF, D])
    vec_d = din("vec", [128, depth * VW]); sm_d = din("sm", [depth, 128, SM_N])
    cb_d = din("cb", [128, CB_N], BF16); pc_d = din("pc", [128, 32])
    yT_d = nc.dram_tensor("yT", [D, T], F32, kind="ExternalOutput").ap()
    if stop:
        dbgf = nc.dram_tensor("dbgf", [16, 128, 512], F32, kind="ExternalOutput").ap()
        dbgb = nc.dram_tensor("dbgb", [16, 128, 512], BF16, kind="ExternalOutput").ap()
    wbi = nc.dram_tensor("wbi", [depth, D, NIN], BF16, kind="Internal").ap()
    wbo = nc.dram_tensor("wbo", [depth, D, D], BF16, kind="Internal").ap()
    wbu = nc.dram_tensor("wbu", [depth, D, NUP], BF16, kind="Internal").ap()
    wbd = nc.dram_tensor("wbd", [depth, DFF, D], BF16, kind="Internal").ap()
    x1_d = nc.dram_tensor("x1s", [D, T], F32, kind="Internal").ap()
    vf_d = nc.dram_tensor("vfs", [384, T], F32, kind="Internal").ap()
    r_wb = Reg("wbscratch"); r_x1 = Reg("x1s"); r_vf = Reg("vfs")

    P = Prog(nc)
    rr = {"i": 0}

    def R(ts):
        return [t.r if isinstance(t, Tl) else t for t in ts]

    def mm(out, lhsT, rhs, start, stop, rd, wr):
        P.op("pe", lambda e: e.matmul(out, lhsT=lhsT, rhs=rhs, start=start, stop=stop), R(rd), R(wr))

    def tr(out, in_, ident, rd, wr):
        P.op("pe", lambda e: e.transpose(out, in_, ident), R(rd), R(wr))

    def act(out, in_, func, rd, wr, bias=None, scale=None):
        kw = {}
        if bias is not None:
            kw["bias"] = bias
        if scale is not None:
            kw["scale"] = scale
        P.op("act", lambda e: e.activation(out=out, in_=in_, func=func, **kw), R(rd), R(wr))

    def tt(eng, out, a, b, op, rd, wr):
        P.op(eng, lambda e: e.tensor_tensor(out=out, in0=a, in1=b, op=op), R(rd), R(wr))

    def stt(eng, out, in0, scalar, in1, op0, op1, rd, wr):
        P.op(eng, lambda e: e.scalar_tensor_tensor(out=out, in0=in0, scalar=scalar, in1=in1, op0=op0, op1=op1), R(rd), R(wr))

    def ts(eng, out, in0, s1, s2, op0, op1, rd, wr):
        if s2 is None:
            P.op(eng, lambda e: e.tensor_scalar(out=out, in0=in0, scalar1=s1, scalar2=None, op0=op0), R(rd), R(wr))
        else:
            P.op(eng, lambda e: e.tensor_scalar(out=out, in0=in0, scalar1=s1, scalar2=s2, op0=op0, op1=op1), R(rd), R(wr))

    def cp(eng, out, in_, rd, wr):
        if eng == "act":
            P.op("act", lambda e: e.activation(out=out, in_=in_, func=AF.Copy), R(rd), R(wr))
        else:
            P.op(eng, lambda e: e.tensor_copy(out=out, in_=in_), R(rd), R(wr))

    def mset(eng, out, val, wr):
        P.op(eng, lambda e: e.memset(out, val), [], R(wr))

    def dma(q, out, in_, chan, rd, wr, nowaw=False):
        return P.op(q, lambda e: e.dma_start(out=out, in_=in_), R(rd), R(wr), chan=chan, nowaw=nowaw)

    dch = {}

    def dump(tl, ap, idx, bf=False, w=512, np_=128):
        if not stop:
            return
        if "c" not in dch:
            dch["c"] = P.chan("dbg")
        dst = (dbgb if bf else dbgf)[idx, 0:np_, 0:w]
        dma("pool", dst, ap, dch["c"], [tl], [])

    def stage(k):
        if stop == k:
            P.dead = True

    def alt():
        rr["i"] += 1
        return "act" if rr["i"] % 2 else "dve"

    with ExitStack() as G:
        uid = {"i": 0}

        def sb(st, name, shape, dt=F32):
            uid["i"] += 1
            nm = f"s{uid['i']}_{name}"
            return Tl(st.enter_context(nc.sbuf_tensor(nm, shape, dt)), nm)

        banks = [Tl(G.enter_context(nc.psum_tensor(f"pb{i}", [128, 512], F32)), f"pb{i}") for i in range(6)]
        obank = Tl(G.enter_context(nc.psum_tensor("pob", [128, 512], F32)), "pob")
        tbank = Tl(G.enter_context(nc.psum_tensor("ptb", [128, 1024], BF16)), "ptb")
        bk = {"i": 0}

        def bank():
            bk["i"] += 1
            return banks[bk["i"] % 6]

        cb = sb(G, "cb", [128, CB_N], BF16)
        pc = sb(G, "pc", [128, 32])
        vec = sb(G, "vec", [128, depth * VW])
        smb = sb(G, "smb", [128, SM_N], BF16)
        qg8 = sb(G, "qg8", [128, 1])
        ones_f = sb(G, "ones_f", [128, 128])
        wsl = [sb(G, f"wsl{i}", [128, 4096], BF16) for i in range(3)]
        wch = [P.chan(f"w{i}") for i in range(3)]
        kc_ = [[sb(G, f"kc{p}_{t}", [128, 512], BF16) for t in range(NT)] for p in range(3)]
        vc_ = [sb(G, f"vc{m}", [128, 384], BF16) for m in range(NT * 4)]
        hT = [sb(G, f"hT{c}", [128, 512], BF16) for c in range(8)]
        mixT = [sb(G, f"mixT{c}", [128, 512], BF16) for c in range(8)]
        ST = sb(G, "ST", [128, 384])
        STb = [sb(G, f"STb{i}", [128, 384], BF16) for i in range(2)]
        rhalo = sb(G, "rhalo", [128, 11])
        uhalo = [sb(G, f"uhalo{i}", [128, 16]) for i in range(2)]
        chalo = sb(G, "chalo", [128, 44, 2])
        c_misc = P.chan("misc")
        c_x = P.chan("x"); c_o = P.chan("o"); c_vf = P.chan("vf"); c_vfl = [P.chan("vfl0"), P.chan("vfl1")]

        ident = cb.t[:, CB_ID:CB_ID + 128]
        ones_b = cb.t[:, CB_ONES:CB_ONES + 128]
        bones_b = cb.t[:, CB_BONES:CB_BONES + 128]
        maskL = cb.t[:, CB_MASKL:CB_MASKL + 128]
        ntri = cb.t[:, CB_NTRI:CB_NTRI + 128]
        mask4 = cb.t[:, CB_MASK4:CB_MASK4 + 512]

        def amask(d):
            return cb.t[:, CB_AMASK + 512 * d: CB_AMASK + 512 * (d + 1)]

        dma("sp", cb.t[:], cb_d, c_misc, [], [cb])
        dma("sp", pc.t[:], pc_d, c_misc, [], [pc])
        dma("sp", vec.t[:], vec_d, c_misc, [], [vec])
        mset("dve", ones_f.t[:], 1.0, [ones_f])

        def vcol(l, name, j=0):
            o = l * VW + voff[name] + j
            return vec.t[:, o:o + 1]

        with ExitStack() as S:
            stg = [sb(S, f"stg{i}", [128, 2048]) for i in range(4)]
            stb = [sb(S, f"stb{i}", [128, 2048], BF16) for i in range(4)]
            cl = [P.chan(f"pl{i}") for i in range(4)]
            cs = [P.chan(f"ps{i}") for i in range(4)]
            n = 0
            pend = []
            for l in range(depth):
                for (src, dst, K, N) in ((w_in_d, wbi, D, NIN), (w_out_d, wbo, D, D), (w_up_d, wbu, D, NUP), (w_down_d, wbd, DFF, D)):
                    for r0 in range(0, K, 128):
                        for c0 in range(0, N, 2048):
                            w = min(2048, N - c0)
                            i = n % 4
                            n += 1
                            dma("sp", stg[i].t[:, 0:w], src[l, r0:r0 + 128, c0:c0 + w], cl[i], [], [stg[i]])
                            cp(("act", "dve", "pool")[n % 3], stb[i].t[:, 0:w], stg[i].t[:, 0:w], [stg[i]], [stb[i]])
                            pend.append((dst[l, r0:r0 + 128, c0:c0 + w], i, w))
                            if len(pend) > 2:
                                d_, i_, w_ = pend.pop(0)
                                dma("sp", d_, stb[i_].t[:, 0:w_], cs[i_], [stb[i_]], [r_wb], nowaw=True)
            for (d_, i_, w_) in pend:
                dma("sp", d_, stb[i_].t[:, 0:w_], cs[i_], [stb[i_]], [r_wb], nowaw=True)
            P.barrier(ALLQ, chans=cl + cs + [c_misc])

        wn = {"i": 0}

        def load_w(W2d, nk, k0, ranges):
            i = wn["i"] % 3
            wn["i"] += 1
            s = wsl[i]
            wc = sum(n_ for _, n_ in ranges)
            view = s.t[:, 0:nk * wc].rearrange("p (k c) -> p k c", c=wc)
            src = W2d[k0 * 128:(k0 + nk) * 128, :].rearrange("(k p) c -> p k c", p=128)
            o = 0
            first = True
            for (c0, n_) in ranges:
                dma("sp", view[:, :, o:o + n_], src[:, :, c0:c0 + n_], wch[i], [r_wb], [s], nowaw=not first)
                first = False
                o += n_
            return s, view

        try:
            stage(1)
            for l in range(depth):
              x_src = xT_d if l == 0 else x1_d
              x_dst = yT_d if l == depth - 1 else x1_d
              x_src_r = [] if l == 0 else [r_x1]
              x_dst_r = [] if l == depth - 1 else [r_x1]
              with ExitStack() as S:
                  smf = sb(S, "smf", [128, SM_N])
                  dma("pool", smf.t[:], sm_d[l], c_misc, [], [smf])
                  cp("dve", smb.t[:], smf.t[:], [smf], [smb])
                  P.op("act", lambda e, l=l: e.mul(out=qg8.t[:], in_=vcol(l, "q_gain"), mul=0.125), R([vec]), R([qg8]))
                  mset("dve", ST.t[:], 0.0, [ST]); mset("dve", STb[0].t[:], 0.0, [STb[0]])
                  mset("dve", rhalo.t[:], 0.0, [rhalo]); mset("dve", chalo.t[:], 0.0, [chalo])
                  for i in range(2):
                      mset("dve", uhalo[i].t[:], 0.0, [uhalo[i]])
                  P.barrier(COMPUTE, chans=[c_misc])
              stcur = 0
              w2a2 = smb.t[:, SM_W2A2:SM_W2A2 + 384]; g2b = smb.t[:, SM_G2:SM_G2 + 384]
              Win, Wout, Wup, Wdn = wbi[l], wbo[l], wbu[l], wbd[l]

              for it in range(NT):
                  t0 = it * TT

                  def rmsnorm(S, xt, gname):
                      sq = [sb(S, f"sq{i}", [128, 512], BF16) for i in range(2)]
                      lnv = sb(S, "lnv", [128, 512]); rstd = sb(S, "rstd", [128, 512])
                      b = bank()
                      for c in range(8):
                          act(sq[c % 2].t[:], xt[c].t[:], AF.Square, [xt[c]], [sq[c % 2]])
                          mm(b.t[:, :], ones_b, sq[c % 2].t[:], c == 0, c == 7, [cb, sq[c % 2]], [b])
                      act(lnv.t[:], b.t[:, :], AF.Ln, [b], [lnv], bias=1e-6, scale=1.0 / D)
                      act(rstd.t[:], lnv.t[:], AF.Exp, [lnv], [rstd], scale=-0.5)
                      for c in range(8):
                          stt("dve", hT[c].t[:], xt[c].t[:], vcol(l, gname, c), rstd.t[:], ALU.mult, ALU.mult,
                              [xt[c], vec, rstd], [hT[c]])

                  def win_tiles(view, j0, cts):
                      out = []
                      for i, ct in enumerate(cts):
                          b = bank()
                          for kc in range(8):
                              mm(b.t[:, :], view[:, kc, (j0 + i) * 128:(j0 + i + 1) * 128], hT[kc].t[:], kc == 0, kc == 7, [view_s[0], hT[kc]], [b])
                          out.append(b)
                      return out

                  view_s = [None]

                  SA = ExitStack()
                  with ExitStack() as S:
                      S = SA
                      xt = [sb(S, f"xa{c}", [128, 512]) for c in range(8)]
                      for c in range(8):
                          ev_ = dma("act", xt[c].t[:], x_src[c * 128:(c + 1) * 128, t0:t0 + TT], c_x, x_src_r, [xt[c]])
                          if c == 7 and ev_ is not None:
                              for c2 in range(8):
                                  xt[c2].r.w = ev_
                      rmsnorm(S, xt, "ln1_g")

                  dump(hT[0], hT[0].t[:], 0, bf=True); dump(hT[7], hT[7].t[:], 1, bf=True)
                  stage(2)
                  with ExitStack() as S:
                      S = SA
                      s_, view = load_w(Win, 8, 0, [(20 * 128, 256)])
                      view_s[0] = s_
                      bs = win_tiles(view, 0, [20, 21])
                      ur = [sb(S, f"ur{i}", [128, 528]) for i in range(2)]
                      sA = sb(S, "sA", [128, 528]); sB = sb(S, "sB", [128, 528])
                      pl = sb(S, "pl", [128, 512], BF16)
                      for i in range(2):
                          u = ur[i]
                          cp("act", u.t[:, 16:528], bs[i].t[:, :], [bs[i]], [u])
                          cp("dve", u.t[:, 0:16], uhalo[i].t[:], [uhalo[i]], [u])
                          cp("dve", uhalo[i].t[:], u.t[:, 512:528], [u], [uhalo[i]])
                          tt("dve", sA.t[:, 1:528], u.t[:, 1:528], u.t[:, 0:527], ALU.add, [u], [sA])
                          tt("dve", sB.t[:, 3:528], sA.t[:, 3:528], sA.t[:, 1:526], ALU.add, [sA], [sB])
                          if i == 0:
                              lo, hi, wl, wh = sA, sB, 2.0, 4.0
                          else:
                              tt("dve", sA.t[:, 7:528], sB.t[:, 7:528], sB.t[:, 3:524], ALU.add, [sB], [sA])
                              tt("dve", sB.t[:, 15:528], sA.t[:, 15:528], sA.t[:, 7:520], ALU.add, [sA], [sB])
                              lo, hi, wl, wh = sA, sB, 8.0, 16.0
                          if it == 0:
                              tt("dve", lo.t[0:64, 16:32], lo.t[0:64, 16:32], pc.t[0:64, i * 16:(i + 1) * 16], ALU.mult, [lo, pc], [lo])
                              tt("dve", hi.t[64:128, 16:32], hi.t[64:128, 16:32], pc.t[64:128, i * 16:(i + 1) * 16], ALU.mult, [hi, pc], [hi])
                          stt("dve", pl.t[0:64, :], lo.t[0:64, 16:528], 1.0 / wl, u.t[0:64, 16:528], ALU.mult, ALU.subtract, [lo, u], [pl])
                          stt("dve", pl.t[64:128, :], hi.t[64:128, 16:528], 1.0 / wh, u.t[64:128, 16:528], ALU.mult, ALU.subtract, [hi, u], [pl])
                          b = bank()
                          mm(b.t[:, :], smb.t[:, SM_PW + i * 128: SM_PW + (i + 1) * 128], pl.t[:], True, True, [smb, pl], [b])
                          ts("dve", mixT[6 + i].t[:], b.t[:, :], vcol(l, "pool_b", i), vcol(l, "pool_scale", i), ALU.add, ALU.mult,
                             [b, vec], [mixT[6 + i]])

                  dump(mixT[6], mixT[6].t[:], 2, bf=True); dump(mixT[7], mixT[7].t[:], 3, bf=True)
                  stage(3)
                  with ExitStack() as S:
                      S = SA
                      qT = [sb(S, f"qT{p}", [128, 512], BF16) for p in range(3)]
                      vT = [sb(S, f"vT{p}", [128, 512], BF16) for p in range(3)]
                      sqq = [sb(S, f"sqq{i}", [128, 512], BF16) for i in range(2)]
                      lnq2 = [sb(S, f"lnq{i}", [128, 512]) for i in range(2)]; rsq2 = [sb(S, f"rsq{i}", [128, 512]) for i in range(2)]
                      groups = [[(11 * 128, 512)], [(15 * 128, 512)], [(19 * 128, 128)]]
                      cts = [[11, 12, 13, 14], [15, 16, 17, 18], [19]]
                      nq = 0
                      for gi in range(3):
                          s_, view = load_w(Win, 8, 0, groups[gi])
                          view_s[0] = s_
                          bs = win_tiles(view, 0, cts[gi])
                          for b, ct in zip(bs, cts[gi]):
                              if ct < 17:
                                  isq = ct < 14
                                  p = ct - 11 if isq else ct - 14
                                  sq_ = sqq[nq % 2]; lnq = lnq2[nq % 2]; rsq = rsq2[nq % 2]; nq += 1
                                  act(sq_.t[:], b.t[:, :], AF.Square, [b], [sq_])
                                  b2 = bank()
                                  mm(b2.t[:, :], bones_b, sq_.t[:], True, True, [cb, sq_], [b2])
                                  act(lnq.t[:], b2.t[:, :], AF.Ln, [b2], [lnq], bias=1e-6, scale=1.0 / 64)
                                  act(rsq.t[:], lnq.t[:], AF.Exp, [lnq], [rsq], scale=-0.5)
                                  dst = qT[p] if isq else kc_[p][it]
                                  gcol = qg8.t[:, 0:1] if isq else vcol(l, "k_gain")
                                  stt("dve", dst.t[:], b.t[:, :], gcol, rsq.t[:], ALU.mult, ALU.mult, [b, rsq, qg8, vec], [dst])
                              else:
                                  cp("act", vT[ct - 17].t[:], b.t[:, :], [b], [vT[ct - 17]])
                      for s in range(4):
                          h = s % 2
                          for p in range(3):
                              tr(tbank.t[:, h * 512 + p * 128: h * 512 + (p + 1) * 128], vT[p].t[:, s * 128:(s + 1) * 128], ident, [vT[p], cb], [tbank])
                          cp(alt(), vc_[it * 4 + s].t[:], tbank.t[:, h * 512: h * 512 + 384], [tbank], [vc_[it * 4 + s]])
                      Eb = [sb(S, f"Eb{i}", [128, 512]) for i in range(4)]
                      SPb = [sb(S, f"SPb{i}", [128, 512], BF16) for i in range(4)]
                      Rbc2 = [sb(S, f"Rbc{i}", [128, 512]) for i in range(2)]
                      argb = [sb(S, f"argb{i}", [128, 512]) for i in range(4)]
                      Ab = [sb(S, f"Ab{i}", [128, 512], BF16) for i in range(4)]
                      last = 4 * it + 3
                      units = [(p, m, e_) for p in range(3) for m in range(last, -1, -1) for e_ in range(2)]
                      NU = len(units)
                      ust = [dict() for _ in range(NU)]

                      def c0of(m):
                          d = m - 4 * it
                          return 128 * d if d > 0 else 0

                      def s1(u):
                          p, m, e_ = units[u]; pb = 64 * e_; c0 = c0of(m)
                          kt = kc_[p][m // 4]
                          zb = bank(); ust[u]["zb"] = zb
                          mm(zb.t[:, c0:512], kt.t[pb:pb + 64, (m % 4) * 128:(m % 4 + 1) * 128], qT[p].t[pb:pb + 64, c0:512], True, True, [kt, qT[p]], [zb])

                      def s2(u):
                          p, m, e_ = units[u]; d = m - 4 * it; c0 = c0of(m)
                          zb = ust[u]["zb"]; E, SP = Eb[u % 4], SPb[u % 4]
                          act(E.t[:, c0:512], zb.t[:, c0:512], AF.Exp, [zb], [E])
                          act(SP.t[:, c0:512], E.t[:, c0:512], AF.Ln, [E], [SP], bias=1.0)
                          if d >= 0:
                              tt("pool", SP.t[:, c0:512], SP.t[:, c0:512], amask(d)[:, c0:512], ALU.mult, [SP, cb], [SP])

                      def s3(u):
                          p, m, e_ = units[u]; c0 = c0of(m)
                          zb = ust[u]["zb"]; SP = SPb[u % 4]
                          mm(zb.t[:, c0:512], ntri, SP.t[:, c0:512], False, True, [cb, SP], [zb])
                          if m > 0:
                              rb = bank(); ust[u]["rb"] = rb
                              mm(rb.t[:, c0:512], ones_b, SP.t[:, c0:512], True, True, [cb, SP], [rb])

                      def s4(u):
                          p, m, e_ = units[u]; c0 = c0of(m)
                          zb = ust[u]["zb"]; ar = argb[u % 4]; Rbc = Rbc2[e_]
                          if m != last:
                              tt("dve", ar.t[:, c0:512], zb.t[:, c0:512], Rbc.t[:, c0:512], ALU.subtract, [zb, Rbc], [ar])
                          if m > 0:
                              rb = ust[u]["rb"]
                              if m == last:
                                  if c0 > 0:
                                      mset("pool", Rbc.t[:, 0:c0], 0.0, [Rbc])
                                  cp("dve", Rbc.t[:, c0:512], rb.t[:, c0:512], [rb], [Rbc])
                              else:
                                  tt("dve", Rbc.t[:, c0:512], Rbc.t[:, c0:512], rb.t[:, c0:512], ALU.add, [Rbc, rb], [Rbc])

                      def s5(u):
                          p, m, e_ = units[u]; d = m - 4 * it; c0 = c0of(m)
                          zb = ust[u]["zb"]; ar = argb[u % 4]; A = Ab[u % 4]
                          if m == last:
                              act(A.t[:, c0:512], zb.t[:, c0:512], AF.Exp, [zb], [A])
                          else:
                              act(A.t[:, c0:512], ar.t[:, c0:512], AF.Exp, [ar], [A])
                          if d >= 0:
                              tt("pool", A.t[:, c0:512], A.t[:, c0:512], amask(d)[:, c0:512], ALU.mult, [A, cb], [A])

                      def s6(u):
                          p, m, e_ = units[u]; pb = 64 * e_; h = 2 * p + e_; d = m - 4 * it; c0 = c0of(m)
                          A = Ab[u % 4]
                          vv_ = vc_[m].t[:, h * 64:(h + 1) * 64]
                          mm(obank.t[pb:pb + 64, c0:512], vv_, A.t[:, c0:512], m == last, m == 0, [vc_[m], A], [obank])
                          if m == 0 and e_ == 1:
                              cp("act", mixT[3 + p].t[:], obank.t[:, :], [obank], [mixT[3 + p]])

                      for t in range(NU + 2):
                          if t < NU:
                              s1(t); s2(t)
                          if 0 <= t - 1 < NU:
                              s3(t - 1); s4(t - 1); s5(t - 1)
                          if 0 <= t - 2 < NU:
                              s6(t - 2)
                      P.barrier(COMPUTE, chans=[c_x])
                  SA.close()

                  dump(mixT[3], mixT[3].t[:], 4, bf=True); dump(kc_[0][it], kc_[0][it].t[:], 5, bf=True); dump(vc_[it * 4], vc_[it * 4].t[:], 6, bf=True, w=384); dump(mixT[5], mixT[5].t[:], 7, bf=True)
                  stage(4)
                  with ExitStack() as S:
                      lor = sb(S, "lor", [128, 512], BF16); sgx = sb(S, "sgx", [128, 512], BF16)
                      raw = [sb(S, f"raw{i}", [128, 513]) for i in range(2)]
                      dtmp = sb(S, "dtmp", [128, 512])
                      ART = [sb(S, f"ART{p}", [128, 4, 2, 128], BF16) for p in range(3)]
                      BT = [sb(S, f"BT{p}", [128, 512], BF16) for p in range(3)]
                      KT = [sb(S, f"KT{p}", [128, 512], BF16) for p in range(3)]
                      GC = [sb(S, f"GC{p}", [128, 4]) for p in range(3)]
                      gT = [sb(S, f"gT{p}", [128, 512], BF16) for p in range(3)]
                      bonT = [sb(S, f"bonT{p}", [128, 512], BF16) for p in range(3)]
                      TOK = [sb(S, f"TOK{c}", [128, 4, 384], BF16) for c in range(4)]
                      Lp = sb(S, "Lp", [128, 512]); a_t = sb(S, "a_t", [128, 512]); kkn = sb(S, "kkn", [128, 512])
                      bv = sb(S, "bv", [128, 512]); et = sb(S, "et", [128, 512]); t2 = sb(S, "t2", [128, 512])
                      BHT = sb(S, "BHT", [128, 512], BF16); KHT = sb(S, "KHT", [128, 512], BF16)
                      vb = [sb(S, f"vb{p}", [128, 512], BF16) for p in range(3)]
                      sqk = sb(S, "sqk", [128, 512], BF16); lob = sb(S, "lob", [32, 512], BF16)
                      cLC = sb(S, "cLC", [128, 4])

                      def evac_shift(b, ct, dst, dtmp=dtmp):
                          cp("act", dst.t[:, 1:513], b.t[:, :], [b], [dst])
                          cp("dve", dst.t[:, 0:1], rhalo.t[:, ct:ct + 1], [rhalo], [dst])
                          tt("dve", dtmp.t[:], dst.t[:, 0:512], dst.t[:, 1:513], ALU.subtract, [dst], [dtmp])
                          cp("dve", rhalo.t[:, ct:ct + 1], dst.t[:, 512:513], [dst], [rhalo])
                          stt("dve", dst.t[:, 1:513], dtmp.t[:], vcol(l, "mu", ct), dst.t[:, 1:513], ALU.mult, ALU.add, [dtmp, vec, dst], [dst])

                      s_, view = load_w(Win, 8, 0, [(9 * 128, 256)])
                      view_s[0] = s_
                      bs = win_tiles(view, 0, [9, 10])
                      evac_shift(bs[0], 9, raw[0])
                      act(lor.t[0:64, :], raw[0].t[0:64, 1:513], AF.Tanh, [raw[0]], [lor])
                      cp("dve", lor.t[64:128, :], raw[0].t[64:128, 1:513], [raw[0]], [lor])
                      evac_shift(bs[1], 10, raw[1])
                      act(sgx.t[:], raw[1].t[:, 1:513], AF.Sigmoid, [raw[1]], [sgx])
                      stage(41)
                      vr = [sb(S, f"vr{p}", [128, 513]) for p in range(3)]
                      s_, view = load_w(Win, 8, 0, [(6 * 128, 384)])
                      view_s[0] = s_
                      bs = win_tiles(view, 0, [6, 7, 8])
                      for p in range(3):
                          evac_shift(bs[p], 6 + p, vr[p])
                      if l > 0:
                          for p in range(3):
                              cp("pool", vb[p].t[:], vr[p].t[:, 1:513], [vr[p]], [vb[p]])
                          b = bank()
                          for pp in range(3):
                              mm(b.t[0:32, :], smb.t[:, SM_V1 + pp * 32: SM_V1 + (pp + 1) * 32], vb[pp].t[:], pp == 0, pp == 2, [smb, vb[pp]], [b])
                          cp("act", lob.t[:], b.t[0:32, :], [b], [lob])
                      rawS = [raw, [sb(S, f"rawb{i}", [128, 513]) for i in range(2)]]
                      dtmpS = [dtmp, sb(S, "dtmpb", [128, 512])]
                      t2S = [t2, sb(S, "t2b", [128, 512])]; etS = [et, sb(S, "etb", [128, 512])]
                      LpS = [Lp, sb(S, "Lpb", [128, 512])]; a_tS = [a_t, sb(S, "a_tb", [128, 512])]

                      def prep_early(p):
                          s2_ = p % 2
                          raw_, dtmp_, t2, et, Lp, a_t = rawS[s2_], dtmpS[s2_], t2S[s2_], etS[s2_], LpS[s2_], a_tS[s2_]
                          pc0 = p * 128
                          s_, view = load_w(Win, 8, 0, [(p * 128, 128), ((3 + p) * 128, 128)])
                          view_s[0] = s_
                          bs = win_tiles(view, 0, [p, 3 + p])
                          evac_shift(bs[0], p, raw_[0], dtmp_)
                          evac_shift(bs[1], 3 + p, raw_[1], dtmp_)
                          rw = [raw_[0], raw_[1], vr[p]]
                          r_, k_, v_ = rw[0], rw[1], rw[2]
                          rT, kTr, vTr = r_.t[:, 1:513], k_.t[:, 1:513], v_.t[:, 1:513]
                          if l == 0:
                              dma("pool", vf_d[pc0:pc0 + 128, t0:t0 + TT], vTr, c_vf, [v_], [r_vf], nowaw=True)
                          else:
                              b = bank()
                              mm(b.t[:, :], smb.t[0:32, SM_V2 + pc0: SM_V2 + pc0 + 128], lob.t[:], True, True, [smb, lob], [b])
                              act(et.t[:], b.t[:, :], AF.Sigmoid, [b], [et], bias=vcol(l, "v0", p))
                              dma("pool", t2.t[:], vf_d[pc0:pc0 + 128, t0:t0 + TT], c_vfl[s2_], [r_vf], [t2])
                              tt("dve", t2.t[:], t2.t[:], vTr, ALU.subtract, [t2, v_], [t2])
                              tt("dve", t2.t[:], t2.t[:], et.t[:], ALU.mult, [t2, et], [t2])
                              tt("dve", vTr, vTr, t2.t[:], ALU.add, [v_, t2], [v_])
                          cp("pool", vb[p].t[:], vTr, [v_], [vb[p]])
                          b = bank()
                          mm(b.t[:, :], w2a2[0:64, pc0:pc0 + 128], lor.t[0:64, :], True, True, [smb, lor], [b])
                          act(et.t[:], b.t[:, :], AF.Sigmoid, [b], [et], bias=vcol(l, "w0", p))
                          for c in range(4):
                              P.op("dve", (lambda o_, d0_, d1_: lambda e: e.tensor_tensor_scan(out=o_, data0=d0_, data1=d1_, initial=0.0, op0=ALU.mult, op1=ALU.add))(
                              Lp.t[:, c * 128:(c + 1) * 128], ones_f.t[:], et.t[:, c * 128:(c + 1) * 128]), R([ones_f, et]), R([Lp]))
                          b = bank()
                          mm(b.t[:, :], w2a2[64:128, pc0:pc0 + 128], lor.t[64:128, :], True, True, [smb, lor], [b])
                          act(a_t.t[:], b.t[:, :], AF.Sigmoid, [b], [a_t], bias=vcol(l, "a0", p))
                          b = bank()
                          mm(b.t[:, :], g2b[:, pc0:pc0 + 128], sgx.t[:], True, True, [smb, sgx], [b])
                          cp("act", gT[p].t[:], b.t[:, :], [b], [gT[p]])

                      def prep_late(p):
                          s2_ = p % 2
                          raw_, dtmp_, t2, et, Lp, a_t = rawS[s2_], dtmpS[s2_], t2S[s2_], etS[s2_], LpS[s2_], a_tS[s2_]
                          pc0 = p * 128
                          r_, k_, v_ = raw_[0], raw_[1], vr[p]
                          rT, kTr, vTr = r_.t[:, 1:513], k_.t[:, 1:513], v_.t[:, 1:513]
                          act(sqk.t[:], kTr, AF.Square, [k_, vec], [sqk], scale=vcol(l, "k_k", p))
                          b = bank()
                          mm(b.t[:, :], bones_b, sqk.t[:], True, True, [cb, sqk], [b])
                          act(t2.t[:], b.t[:, :], AF.Ln, [b], [t2], bias=1e-24)
                          act(t2.t[:], t2.t[:], AF.Exp, [t2], [t2], scale=-0.5)
                          stt("dve", kkn.t[:], kTr, vcol(l, "k_k", p), t2.t[:], ALU.mult, ALU.mult, [k_, vec, t2], [kkn])
                          tt("dve", bv.t[:], kkn.t[:], a_t.t[:], ALU.mult, [kkn, a_t], [bv])
                          ts("dve", t2.t[:], a_t.t[:], -1.0, vcol(l, "k_a", p), ALU.add, ALU.mult, [a_t, vec], [t2])
                          stt("dve", kTr, t2.t[:], 1.0, kTr, ALU.add, ALU.mult, [t2, k_], [k_])
                          stt("dve", sqk.t[:], rT, vcol(l, "r_k", p), kTr, ALU.mult, ALU.mult, [r_, k_, vec], [sqk])
                          b = bank()
                          mm(b.t[:, :], bones_b, sqk.t[:], True, True, [cb, sqk], [b])
                          tt("dve", bonT[p].t[:], b.t[:, :], vTr, ALU.mult, [b, v_], [bonT[p]])
                          A4 = ART[p].t[:, :, 0, :]; R4 = ART[p].t[:, :, 1, :]
                          v4 = lambda ap: ap.rearrange("p (c t) -> p c t", t=128)
                          tt("dve", t2.t[:], Lp.t[:], et.t[:], ALU.subtract, [Lp, et], [t2])
                          act(t2.t[:], t2.t[:], AF.Exp, [t2], [t2], scale=CDEC)
                          stt("dve", A4, v4(kkn.t[:]), -1.0, v4(t2.t[:]), ALU.mult, ALU.mult, [kkn, t2], [ART[p]])
                          act(t2.t[:], Lp.t[:], AF.Exp, [Lp], [t2], scale=CDEC)
                          tt("dve", R4, v4(rT), v4(t2.t[:]), ALU.mult, [r_, t2], [ART[p]])
                          for c in range(4):
                              cp("dve", GC[p].t[:, c:c + 1], t2.t[:, c * 128 + 127: c * 128 + 128], [t2], [GC[p]])
                          act(t2.t[:], Lp.t[:], AF.Exp, [Lp], [t2], scale=-CDEC)
                          tt("dve", BT[p].t[:], bv.t[:], t2.t[:], ALU.mult, [bv, t2], [BT[p]])
                          tt("dve", KT[p].t[:], kTr, t2.t[:], ALU.mult, [k_, t2], [KT[p]])
                          for c in range(4):
                              ts("dve", cLC.t[:, c:c + 1], Lp.t[:, c * 128 + 127: c * 128 + 128], CDEC, None, ALU.mult, None, [Lp], [cLC])
                          for c in range(4):
                              act(t2.t[:, c * 128:(c + 1) * 128], Lp.t[:, c * 128:(c + 1) * 128], AF.Exp, [Lp, cLC], [t2],
                                  scale=-CDEC, bias=cLC.t[:, c:c + 1])
                          tt("dve", BHT.t[:], bv.t[:], t2.t[:], ALU.mult, [bv, t2], [BHT])
                          tt("dve", KHT.t[:], kTr, t2.t[:], ALU.mult, [k_, t2], [KHT])
                          for c in range(4):
                              h = c % 2
                              srcs = [(ART[p], ART[p].t[:, c, 0, :]), (BHT, BHT.t[:, c * 128:(c + 1) * 128]),
                                      (KHT, KHT.t[:, c * 128:(c + 1) * 128]), (vb[p], vb[p].t[:, c * 128:(c + 1) * 128])]
                              for kd, (tl_, ap_) in enumerate(srcs):
                                  tr(tbank.t[:, h * 512 + kd * 128: h * 512 + (kd + 1) * 128], ap_, ident, [tl_, cb], [tbank])
                              cp(alt(), TOK[c].t[:, :, pc0:pc0 + 128],
                                 tbank.t[:, h * 512:(h + 1) * 512].rearrange("p (k c) -> p k c", c=128), [tbank], [TOK[c]])


                      prep_early(0)
                      for p in range(3):
                          if p + 1 < 3:
                              prep_early(p + 1)
                          prep_late(p)
                      stage(42)
                      SCb = [sb(S, f"SCb{h}", [128, 512], BF16) for h in range(6)]
                      XTb = [sb(S, f"XTb{h}", [128, 128], BF16) for h in range(6)]
                      PP = [[sb(S, f"PP{p}_{i}", [128, 512], BF16) for i in range(2)] for p in range(3)]
                      ACC = [[sb(S, f"ACC{p}_{i}", [128, 256], BF16) for i in range(2)] for p in range(3)]
                      T1b = sb(S, "T1b", [128, 384], BF16); UH = sb(S, "UH", [128, 384]); WhT = sb(S, "WhT", [128, 384], BF16)
                      Ub = sb(S, "Ub", [128, 384], BF16); ysq = sb(S, "ysq", [128, 384]); yn = sb(S, "yn", [128, 384], BF16)
                      st1 = sb(S, "st1", [128, 6]); st2 = sb(S, "st2", [128, 6]); st3 = sb(S, "st3", [128, 6])
                      YT = sb(S, "YT", [128, 3, 512], BF16)
                      for c in range(4):
                          tok = TOK[c]
                          csl = slice(c * 128, (c + 1) * 128)
                          for p in range(3):
                              for e_ in range(2):
                                  pb = 64 * e_; h = 2 * p + e_
                                  b = bank()
                                  ar2 = ART[p].t[pb:pb + 64, c, :, :].rearrange("p a t -> p (a t)")
                                  mm(b.t[:, 0:256], BT[p].t[pb:pb + 64, csl], ar2, True, True, [BT[p], ART[p]], [b])
                                  mm(b.t[:, 256:512], KT[p].t[pb:pb + 64, csl], ar2, True, True, [KT[p], ART[p]], [b])
                                  tt("dve", SCb[h].t[:], b.t[:, :], mask4, ALU.mult, [b, cb], [SCb[h]])
                                  b = bank()
                                  mm(b.t[:, 0:128], ART[p].t[pb:pb + 64, c, 0, :], BT[p].t[pb:pb + 64, csl], True, True, [BT[p], ART[p]], [b])
                                  tt("dve", XTb[h].t[:], b.t[:, 0:128], maskL, ALU.mult, [b, cb], [XTb[h]])
                          stage(43)
                          b = bank()
                          for h in range(6):
                              mm(b.t[:, h * 64:(h + 1) * 64], SCb[h].t[:, 256:384], tok.t[:, 3, h * 64:(h + 1) * 64], True, True, [SCb[h], tok], [b])
                          cp("act", T1b.t[:], b.t[:, 0:384], [b], [T1b])
                          cur = [0, 0, 0]
                          for p in range(3):
                              for e_ in range(2):
                                  h = 2 * p + e_
                                  tt("pool", ACC[p][0].t[:, e_ * 128:(e_ + 1) * 128], SCb[h].t[:, 0:128], ident, ALU.add, [SCb[h], cb], [ACC[p][0]])
                          for k in range(1, 7):
                              i0 = (k - 1) % 2; i1 = k % 2
                              bq = []
                              for p in range(3):
                                  b = bank(); bq.append(b)
                                  for e_ in range(2):
                                      h = 2 * p + e_
                                      if k == 1:
                                          Pm, PTm, rds = SCb[h].t[:, 0:128], XTb[h].t[:], [SCb[h], XTb[h]]
                                      else:
                                          Pm = PP[p][i0].t[:, e_ * 256: e_ * 256 + 128]; PTm = PP[p][i0].t[:, e_ * 256 + 128: e_ * 256 + 256]
                                          rds = [PP[p][i0]]
                                      if k < 6:
                                          mm(b.t[:, e_ * 256: e_ * 256 + 128], PTm, Pm, True, True, rds, [b])
                                      mm(b.t[:, e_ * 256 + 128: e_ * 256 + 256], Pm, PTm, True, True, rds, [b])
                              for p in range(3):
                                  b = bq[p]
                                  if k < 6:
                                      cp(alt(), PP[p][i1].t[:], b.t[:, :], [b], [PP[p][i1]])
                                  else:
                                      for e_ in range(2):
                                          cp(alt(), PP[p][i1].t[:, e_ * 256 + 128: e_ * 256 + 256], b.t[:, e_ * 256 + 128: e_ * 256 + 256], [b], [PP[p][i1]])
                              bq2 = []
                              for p in range(3):
                                  b2 = bank(); bq2.append(b2)
                                  for e_ in range(2):
                                      mm(b2.t[:, e_ * 128:(e_ + 1) * 128], PP[p][i1].t[:, e_ * 256 + 128: e_ * 256 + 256],
                                         ACC[p][i0].t[:, e_ * 128:(e_ + 1) * 128], True, True, [PP[p][i1], ACC[p][i0]], [b2])
                              for p in range(3):
                                  tt("dve", ACC[p][i1].t[:], ACC[p][i0].t[:], bq2[p].t[:, 0:256], ALU.add, [ACC[p][i0], bq2[p]], [ACC[p][i1]])
                          stage(44)
                          NTf = [ACC[p][0] for p in range(3)]
                          b = bank()
                          for h in range(6):
                              p, e_ = h // 2, h % 2
                              mm(b.t[64 * e_:64 * e_ + 64, p * 128:(p + 1) * 128], tok.t[:, 0, h * 64:(h + 1) * 64], NTf[p].t[:, e_ * 128:(e_ + 1) * 128],
                                 True, True, [tok, NTf[p]], [b])
                          cp("act", WhT.t[:], b.t[:, 0:384], [b], [WhT])
                          b = bank()
                          for h in range(6):
                              p, e_ = h // 2, h % 2
                              mm(b.t[:, h * 64:(h + 1) * 64], NTf[p].t[:, e_ * 128:(e_ + 1) * 128], T1b.t[:, h * 64:(h + 1) * 64], True, True, [NTf[p], T1b], [b])
                          cp("act", UH.t[:], b.t[:, 0:384], [b], [UH])
                          stage(45)
                          so = STb[stcur]; sn = STb[1 - stcur]
                          b = bank()
                          for p in range(3):
                              mm(b.t[:, p * 128:(p + 1) * 128], WhT.t[:, p * 128:(p + 1) * 128], so.t[:, p * 128:(p + 1) * 128], True, True, [WhT, so], [b])
                          tt("dve", Ub.t[:], b.t[:, 0:384], UH.t[:], ALU.add, [b, UH], [Ub])
                          stage(451)
                          yb = bank()
                          for h in range(6):
                              p, e_ = h // 2, h % 2; pb = 64 * e_
                              hs = slice(h * 64, (h + 1) * 64)
                              if e_ == 0:
                                  mm(yb.t[:, p * 128:(p + 1) * 128], ART[p].t[:, c, 1, :], so.t[:, p * 128:(p + 1) * 128], True, False, [ART[p], so], [yb])
                              mm(yb.t[:, hs], SCb[h].t[:, 128:256], Ub.t[:, hs], False, False, [SCb[h], Ub], [yb])
                              mm(yb.t[:, hs], SCb[h].t[:, 384:512], tok.t[:, 3, hs], False, True, [SCb[h], tok], [yb])
                          stage(452)
                          sbk = bank()
                          for h in range(6):
                              p, e_ = h // 2, h % 2; pb = 64 * e_
                              hs = slice(h * 64, (h + 1) * 64)
                              mm(sbk.t[pb:pb + 64, hs], tok.t[:, 1, hs], Ub.t[:, hs], True, False, [tok, Ub], [sbk])
                              mm(sbk.t[pb:pb + 64, hs], tok.t[:, 2, hs], tok.t[:, 3, hs], False, True, [tok], [sbk])
                          stage(453)
                          for h in range(6):
                              p, e_ = h // 2, h % 2; pb = 64 * e_
                              hs = slice(h * 64, (h + 1) * 64)
                              stt("dve", ST.t[pb:pb + 64, hs], ST.t[pb:pb + 64, hs], GC[p].t[pb:pb + 64, c:c + 1], sbk.t[pb:pb + 64, hs],
                                  ALU.mult, ALU.add, [ST, GC[p], sbk], [ST])
                          cp("act", sn.t[:], ST.t[:], [ST], [sn])
                          stcur = 1 - stcur
                          stage(46)
                          y3 = yb.t[:, 0:384].rearrange("p (h i) -> p h i", i=64)
                          P.op("dve", (lambda o_, i_: lambda e: e.tensor_reduce(out=o_, in_=i_, axis=AX.X, op=ALU.add))(st1.t[:], y3), R([yb]), R([st1]))
                          act(ysq.t[:], yb.t[:, 0:384], AF.Square, [yb], [ysq])
                          P.op("dve", (lambda o_, i_: lambda e: e.tensor_reduce(out=o_, in_=i_, axis=AX.X, op=ALU.add))(
                              st2.t[:], ysq.t[:].rearrange("p (h i) -> p h i", i=64)), R([ysq]), R([st2]))
                          ts("dve", st1.t[:], st1.t[:], 1.0 / 64, None, ALU.mult, None, [st1], [st1])
                          tt("dve", st3.t[:], st1.t[:], st1.t[:], ALU.mult, [st1], [st3])
                          stt("dve", st2.t[:], st2.t[:], 1.0 / 64, st3.t[:], ALU.mult, ALU.subtract, [st2, st3], [st2])
                          act(st2.t[:], st2.t[:], AF.Ln, [st2], [st2], bias=64e-5)
                          act(st2.t[:], st2.t[:], AF.Exp, [st2], [st2], scale=-0.5)
                          for h in range(6):
                              hs = slice(h * 64, (h + 1) * 64)
                              ts("dve", yn.t[:, hs], yb.t[:, hs], st1.t[:, h:h + 1], st2.t[:, h:h + 1], ALU.subtract, ALU.mult, [yb, st1, st2], [yn])
                          stage(47)
                          hh = c % 2
                          for p in range(3):
                              tr(tbank.t[:, hh * 512 + p * 128: hh * 512 + (p + 1) * 128], yn.t[:, p * 128:(p + 1) * 128], ident, [yn, cb], [tbank])
                          cp("act", YT.t[:, :, csl], tbank.t[:, hh * 512: hh * 512 + 384].rearrange("p (k c) -> p k c", c=128), [tbank], [YT])
                      for p in range(3):
                          act(t2.t[:], YT.t[:, p, :], AF.Identity, [YT, vec], [t2], scale=vcol(l, "lnx_w", p), bias=vcol(l, "lnx_b", p))
                          tt("dve", t2.t[:], t2.t[:], bonT[p].t[:], ALU.add, [t2, bonT[p]], [t2])
                          tt("dve", mixT[p].t[:], t2.t[:], gT[p].t[:], ALU.mult, [t2, gT[p]], [mixT[p]])
                      P.barrier(COMPUTE, chans=[c_vf] + c_vfl)

                  dump(mixT[0], mixT[0].t[:], 8, bf=True); dump(mixT[1], mixT[1].t[:], 9, bf=True); dump(mixT[2], mixT[2].t[:], 10, bf=True)
                  stage(5)
                  with ExitStack() as S:
                      xt = [sb(S, f"xa{c}", [128, 512]) for c in range(8)]
                      for c in range(8):
                          ev_ = dma("act", xt[c].t[:], x_src[c * 128:(c + 1) * 128, t0:t0 + TT], c_x, x_src_r, [xt[c]])
                          if c == 7 and ev_ is not None:
                              for c2 in range(8):
                                  xt[c2].r.w = ev_
                      for g in range(2):
                          s_, view = load_w(Wout, 8, 0, [(g * 512, 512)])
                          for j in range(4):
                              dt_ = g * 4 + j
                              b = bank()
                              for kc in range(8):
                                  mm(b.t[:, :], view[:, kc, j * 128:(j + 1) * 128], mixT[kc].t[:], kc == 0, kc == 7, [s_, mixT[kc]], [b])
                              tt("dve", xt[dt_].t[:], xt[dt_].t[:], b.t[:, :], ALU.add, [xt[dt_], b], [xt[dt_]])
                      rmsnorm(S, xt, "ln2_g")
                      graw = [sb(S, f"graw{i}", [128, 514]) for i in range(2)]
                      vraw = [sb(S, f"vraw{i}", [128, 514]) for i in range(2)]
                      cg = [sb(S, f"cg{i}", [128, 512]) for i in range(2)]
                      cv = [sb(S, f"cv{i}", [128, 512]) for i in range(2)]
                      gv = [sb(S, f"gv{i}", [128, 512], BF16) for i in range(11)]
                      nf = 0
                      for half in range(2):
                          for (g0, ng) in ((0, 4), (4, 4), (8, 3)):
                              ctg = half * 11 + g0
                              sg_, vg = load_w(Wup, 8, 0, [(ctg * 128, ng * 128)])
                              sv_, vv = load_w(Wup, 8, 0, [((22 + ctg) * 128, ng * 128)])
                              for j in range(ng):
                                  i2 = nf % 2; nf += 1
                                  ct = ctg + j
                                  res = []
                                  for (sl_, vw_, cti, rawt, cout) in ((sg_, vg, ct, graw[i2], cg[i2]), (sv_, vv, 22 + ct, vraw[i2], cv[i2])):
                                      b = bank()
                                      for kc in range(8):
                                          mm(b.t[:, :], vw_[:, kc, j * 128:(j + 1) * 128], hT[kc].t[:], kc == 0, kc == 7, [sl_, hT[kc]], [b])
                                      cp("act", rawt.t[:, 2:514], b.t[:, :], [b], [rawt])
                                      cp("pool", rawt.t[:, 0:2], chalo.t[:, cti, :], [chalo], [rawt])
                                      cp("pool", chalo.t[:, cti, :], rawt.t[:, 512:514], [rawt], [chalo])
                                      act(cout.t[:], b.t[:, :], AF.Identity, [b, vec], [cout], scale=vcol(l, "conv_w", 2 * 44 + cti), bias=vcol(l, "conv_b", cti))
                                      stt("dve", cout.t[:], rawt.t[:, 1:513], vcol(l, "conv_w", 44 + cti), cout.t[:], ALU.mult, ALU.add, [rawt, vec, cout], [cout])
                                      stt("dve", cout.t[:], rawt.t[:, 0:512], vcol(l, "conv_w", cti), cout.t[:], ALU.mult, ALU.add, [rawt, vec, cout], [cout])
                                  act(cg[i2].t[:], cg[i2].t[:], AF.Silu, [cg[i2]], [cg[i2]])
                                  tt("pool", gv[g0 + j].t[:], cg[i2].t[:], cv[i2].t[:], ALU.mult, [cg[i2], cv[i2]], [gv[g0 + j]])
                          for dq in range(4):
                              s_, view = load_w(Wdn, 11, half * 11, [(dq * 256, 256)])
                              for j in range(2):
                                  dt_ = dq * 2 + j
                                  b = bank()
                                  for kc in range(11):
                                      mm(b.t[:, :], view[:, kc, j * 128:(j + 1) * 128], gv[kc].t[:], kc == 0, kc == 10, [s_, gv[kc]], [b])
                                  tt("dve", xt[dt_].t[:], xt[dt_].t[:], b.t[:, :], ALU.add, [xt[dt_], b], [xt[dt_]])
                      for c in range(8):
                          dma("pool", x_dst[c * 128:(c + 1) * 128, t0:t0 + TT], xt[c].t[:], c_o, [xt[c]], x_dst_r, nowaw=True)
                      P.barrier(COMPUTE, chans=[c_o, c_x])

        except _Stop:
            pass
        P.dead = False
        P.barrier(ALLQ, chans=P.chans)
        P.emit()
    return nc, P


_CACHE = {}


def run(inputs, T, depth, n_cores, stop=0):
    vec, sm = pack_params(inputs, depth)
    cbv, pcv = pack_consts()
    key = (T, depth, stop)
    if key not in _CACHE:
        _CACHE[key] = build(T, depth, stop)[0]
    nc = _CACHE[key]
    x = np.asarray(inputs["x"], np.float32)
    shared = {
        "w_in": np.ascontiguousarray(np.asarray(inputs["w_in"], np.float32)[:depth]),
        "w_out": np.ascontiguousarray(np.asarray(inputs["w_out"], np.float32)[:depth]),
        "w_up": np.ascontiguousarray(np.asarray(inputs["w_up"], np.float32)[:depth]),
        "w_down": np.ascontiguousarray(np.asarray(inputs["w_down"], np.float32)[:depth]),
        "vec": vec, "sm": sm, "cb": cbv, "pc": pcv,
    }
    in_maps = []
    for b in range(n_cores):
        m = dict(shared)
        m["xT"] = np.ascontiguousarray(x[b, :T].T)
        in_maps.append(m)
    res = run_bass_kernel_spmd(nc, in_maps, core_ids=list(range(n_cores)))
    if stop:
        return res.results[0]
    out = np.stack([np.ascontiguousarray(res.results[b]["yT"].T) for b in range(n_cores)], 0)
    return out.astype(np.float32)


def kernel(**inputs):
    inputs = {k: np.asarray(v) for k, v in inputs.items()}
    x = inputs["x"]
    return run(inputs, x.shape[1], 2, x.shape[0])
```

```python
from contextlib import ExitStack
import numpy as np
import ml_dtypes
import concourse.bass as bass
import concourse.mybir as mybir
from concourse.bass_utils import run_bass_kernel_spmd

F32 = mybir.dt.float32
BF16 = mybir.dt.bfloat16
AF = mybir.ActivationFunctionType
ALU = mybir.AluOpType
AX = mybir.AxisListType

D = 1024
NIN = 2816
DFF = 2816
NUP = 5632
TT = 512
CDEC = -float(np.exp(-0.5))

COMPUTE = ("pe", "act", "dve", "pool")
ALLQ = ("pe", "act", "dve", "pool", "sp")


class Reg:
    __slots__ = ("name", "w", "rs")

    def __init__(self, name):
        self.name = name
        self.w = None
        self.rs = {}


class Chan:
    def __init__(self, prog, name):
        self.key = "dma_" + name
        prog.semkeys.append(self.key)
        self.cnt = 0


class Prog:
    def __init__(self, nc):
        self.nc = nc
        self.q = {e: [] for e in ALLQ}
        self.cnt = {e: 0 for e in COMPUTE}
        self.known = {e: {} for e in ALLQ}
        self.semkeys = ["prog_" + e for e in COMPUTE]
        self.chans = []
        self.nins = 0
        self.nwait = 0
        self.dead = False

    def chan(self, name):
        c = Chan(self, name)
        self.chans.append(c)
        return c

    def op(self, eng, fn, reads=(), writes=(), chan=None, nowaw=False):
        if self.dead:
            return None
        need = {}

        def req(k, v):
            if need.get(k, 0) < v:
                need[k] = v

        me = None if chan is not None else eng
        for r in reads:
            if r.w is not None:
                req(r.w[0], r.w[1])
        for w in writes:
            if w.w is not None and (w.w[2] is None or w.w[2] != me) and not nowaw:
                req(w.w[0], w.w[1])
            for k, (v, e) in w.rs.items():
                if e is None or e != me:
                    req(k, v)
        kn = self.known[eng]
        for k, v in need.items():
            if kn.get(k, 0) < v:
                self.q[eng].append(("w", k, v))
                kn[k] = v
                self.nwait += 1
        if chan is not None:
            chan.cnt += 1
            ev = (chan.key, 16 * chan.cnt, None)
            self.q[eng].append(("i", fn, chan.key, 16))
        else:
            self.cnt[eng] += 1
            ev = ("prog_" + eng, self.cnt[eng], eng)
            self.q[eng].append(("i", fn, ev[0], 1))
        self.nins += 1
        for r in reads:
            old = r.rs.get(ev[0])
            if old is None or old[0] < ev[1]:
                r.rs[ev[0]] = (ev[1], ev[2])
        for w in writes:
            w.w = ev
            w.rs = {}
        return ev

    def wait(self, eng, key, val):
        if self.dead:
            return
        kn = self.known[eng]
        if val > 0 and kn.get(key, 0) < val:
            self.q[eng].append(("w", key, val))
            kn[key] = val
            self.nwait += 1

    def barrier(self, engs=COMPUTE, chans=()):
        for e in engs:
            for x in engs:
                if x != e and x in self.cnt:
                    self.wait(e, "prog_" + x, self.cnt[x])
            for c in chans:
                self.wait(e, c.key, 16 * c.cnt)

    def emit(self):
        nc = self.nc
        with ExitStack() as st:
            sems = {k: st.enter_context(nc.semaphore(k)) for k in self.semkeys}
            block = st.enter_context(nc.Block())

            def run(e, items):
                for it in items:
                    if it[0] == "w":
                        e.wait_ge(sems[it[1]], it[2])
                    else:
                        it[1](e).then_inc(sems[it[2]], it[3])

            @block.tensor
            def _(e):
                run(e, self.q["pe"])

            @block.scalar
            def _(e):
                run(e, self.q["act"])

            @block.vector
            def _(e):
                run(e, self.q["dve"])

            @block.gpsimd
            def _(e):
                run(e, self.q["pool"])

            @block.sync
            def _(e):
                run(e, self.q["sp"])


class Tl:
    __slots__ = ("t", "r", "name")

    def __init__(self, t, name):
        self.t = t
        self.name = name
        self.r = Reg(name)


def vec_cols(depth):
    spec = [("ln1_g", 8), ("ln2_g", 8), ("mu", 11), ("w0", 3), ("a0", 3), ("k_k", 3), ("k_a", 3), ("r_k", 3),
            ("lnx_w", 3), ("lnx_b", 3), ("v0", 3), ("q_gain", 1), ("k_gain", 1), ("pool_b", 2),
            ("pool_scale", 2), ("conv_w", 132), ("conv_b", 44)]
    off, o = {}, 0
    for n, c in spec:
        off[n] = o
        o += c
    return off, o


SM_W2A2, SM_G2, SM_V1, SM_V2, SM_PW, SM_N = 0, 384, 768, 864, 1248, 1504
CB_ID, CB_ONES, CB_BONES, CB_MASKL, CB_NTRI, CB_MASK4, CB_AMASK, CB_N = 0, 128, 256, 384, 512, 640, 1152, 3200


def pack_consts():
    c = np.zeros((128, CB_N), np.float32)
    i = np.arange(128)
    c[:, CB_ID:CB_ID + 128] = np.eye(128)
    c[:, CB_ONES:CB_ONES + 128] = 1.0
    c[:, CB_BONES:CB_BONES + 128] = (i[:, None] // 64 == i[None, :] // 64)
    c[:, CB_MASKL:CB_MASKL + 128] = (i[None, :] < i[:, None])
    c[:, CB_NTRI:CB_NTRI + 128] = -(i[:, None] >= i[None, :]).astype(np.float32)
    su = (i[:, None] < i[None, :]).astype(np.float32)
    iu = (i[:, None] <= i[None, :]).astype(np.float32)
    c[:, CB_MASK4:CB_MASK4 + 512] = np.concatenate([su, iu, su, iu], 1)
    q = np.arange(512)
    for d in range(4):
        c[:, CB_AMASK + 512 * d:CB_AMASK + 512 * (d + 1)] = (128 * d + i[:, None] < q[None, :])
    pc = np.ones((128, 2, 16), np.float32)
    wins = (2, 4, 8, 16)
    for ti in range(2):
        for e in range(2):
            w = wins[2 * ti + e]
            t = np.arange(16)
            pc[e * 64:(e + 1) * 64, ti, :] = (w / np.minimum(t + 1, w))[None, :]
    return c.astype(ml_dtypes.bfloat16), pc.reshape(128, 32)


def pack_params(inp, depth):
    off, W = vec_cols(depth)
    vec = np.zeros((128, depth * W), np.float32)
    sm = np.zeros((depth, 128, SM_N), np.float32)

    def put(l, name, v):
        v = np.asarray(v, np.float32).reshape(-1, 128).T
        vec[:, l * W + off[name]: l * W + off[name] + v.shape[1]] = v

    for l in range(depth):
        put(l, "ln1_g", inp["ln1_g"][l]); put(l, "ln2_g", inp["ln2_g"][l]); put(l, "mu", inp["mu_shift"][l])
        for n in ("w0", "a0", "k_k", "k_a", "lnx_w", "lnx_b"):
            put(l, n, inp[n][l])
        put(l, "r_k", inp["r_k"][l].reshape(-1))
        if l > 0:
            put(l, "v0", inp["v0"][l - 1])
        put(l, "q_gain", np.tile(inp["q_gain"][l], 2)); put(l, "k_gain", np.tile(inp["k_gain"][l], 2))
        put(l, "pool_b", inp["pool_b"][l].reshape(-1)); put(l, "pool_scale", inp["pool_scale"][l])
        put(l, "conv_w", inp["conv_w"][l].reshape(-1)); put(l, "conv_b", inp["conv_b"][l])
        sm[l, 0:64, SM_W2A2:SM_W2A2 + 384] = inp["w2"][l]
        sm[l, 64:128, SM_W2A2:SM_W2A2 + 384] = inp["a2"][l]
        sm[l, :, SM_G2:SM_G2 + 384] = inp["g2"][l]
        if l > 0:
            v1 = np.asarray(inp["v1"][l - 1]).reshape(3, 128, 32).transpose(1, 0, 2).reshape(128, 96)
            sm[l, :, SM_V1:SM_V1 + 96] = v1
            sm[l, 0:32, SM_V2:SM_V2 + 384] = inp["v2"][l - 1]
        for ti in range(2):
            for e in range(2):
                g = 2 * ti + e
                sm[l, e * 64:(e + 1) * 64, SM_PW + ti * 128 + e * 64: SM_PW + ti * 128 + (e + 1) * 64] = inp["pool_w"][l, g]
    return vec, sm


class _Stop(Exception):
    pass


def build(T, depth, stop=0):
    nc = bass.Bass("TRN2", target_bir_lowering=False)
    NT = T // TT
    voff, VW = vec_cols(depth)

    def din(name, shape, dt=F32):
        return nc.dram_tensor(name, shape, dt, kind="ExternalInput").ap()

    xT_d = din("xT", [D, T])
    w_in_d = din("w_in", [depth, D, NIN]); w_out_d = din("w_out", [depth, D, D])
    w_up_d = din("w_up", [depth, D, NUP]); w_down_d = din("w_down", [depth, DFF, D])
    vec_d = din("vec", [128, depth * VW]); sm_d = din("sm", [depth, 128, SM_N])
    cb_d = din("cb", [128, CB_N], BF16); pc_d = din("pc", [128, 32])
    yT_d = nc.dram_tensor("yT", [D, T], F32, kind="ExternalOutput").ap()
    if stop:
        dbgf = nc.dram_tensor("dbgf", [16, 128, 512], F32, kind="ExternalOutput").ap()
        dbgb = nc.dram_tensor("dbgb", [16, 128, 512], BF16, kind="ExternalOutput").ap()
    wbi = nc.dram_tensor("wbi", [depth, D, NIN], BF16, kind="Internal").ap()
    wbo = nc.dram_tensor("wbo", [depth, D, D], BF16, kind="Internal").ap()
    wbu = nc.dram_tensor("wbu", [depth, D, NUP], BF16, kind="Internal").ap()
    wbd = nc.dram_tensor("wbd", [depth, DFF, D], BF16, kind="Internal").ap()
    x1_d = nc.dram_tensor("x1s", [D, T], F32, kind="Internal").ap()
    vf_d = nc.dram_tensor("vfs", [384, T], F32, kind="Internal").ap()
    r_wb = Reg("wbscratch"); r_x1 = Reg("x1s"); r_vf = Reg("vfs")

    P = Prog(nc)
    rr = {"i": 0}

    def R(ts):
        return [t.r if isinstance(t, Tl) else t for t in ts]

    def mm(out, lhsT, rhs, start, stop, rd, wr):
        P.op("pe", lambda e: e.matmul(out, lhsT=lhsT, rhs=rhs, start=start, stop=stop), R(rd), R(wr))

    def tr(out, in_, ident, rd, wr):
        P.op("pe", lambda e: e.transpose(out, in_, ident), R(rd), R(wr))

    def act(out, in_, func, rd, wr, bias=None, scale=None):
        kw = {}
        if bias is not None:
            kw["bias"] = bias
        if scale is not None:
            kw["scale"] = scale
        P.op("act", lambda e: e.activation(out=out, in_=in_, func=func, **kw), R(rd), R(wr))

    def tt(eng, out, a, b, op, rd, wr):
        P.op(eng, lambda e: e.tensor_tensor(out=out, in0=a, in1=b, op=op), R(rd), R(wr))

    def stt(eng, out, in0, scalar, in1, op0, op1, rd, wr):
        P.op(eng, lambda e: e.scalar_tensor_tensor(out=out, in0=in0, scalar=scalar, in1=in1, op0=op0, op1=op1), R(rd), R(wr))

    def ts(eng, out, in0, s1, s2, op0, op1, rd, wr):
        if s2 is None:
            P.op(eng, lambda e: e.tensor_scalar(out=out, in0=in0, scalar1=s1, scalar2=None, op0=op0), R(rd), R(wr))
        else:
            P.op(eng, lambda e: e.tensor_scalar(out=out, in0=in0, scalar1=s1, scalar2=s2, op0=op0, op1=op1), R(rd), R(wr))

    def cp(eng, out, in_, rd, wr):
        if eng == "act":
            P.op("act", lambda e: e.activation(out=out, in_=in_, func=AF.Copy), R(rd), R(wr))
        else:
            P.op(eng, lambda e: e.tensor_copy(out=out, in_=in_), R(rd), R(wr))

    def mset(eng, out, val, wr):
        P.op(eng, lambda e: e.memset(out, val), [], R(wr))

    def dma(q, out, in_, chan, rd, wr, nowaw=False):
        return P.op(q, lambda e: e.dma_start(out=out, in_=in_), R(rd), R(wr), chan=chan, nowaw=nowaw)

    dch = {}

    def dump(tl, ap, idx, bf=False, w=512, np_=128):
        if not stop:
            return
        if "c" not in dch:
            dch["c"] = P.chan("dbg")
        dst = (dbgb if bf else dbgf)[idx, 0:np_, 0:w]
        dma("pool", dst, ap, dch["c"], [tl], [])

    def stage(k):
        if stop == k:
            P.dead = True

    def alt():
        rr["i"] += 1
        return "act" if rr["i"] % 2 else "dve"

    with ExitStack() as G:
        uid = {"i": 0}

        def sb(st, name, shape, dt=F32):
            uid["i"] += 1
            nm = f"s{uid['i']}_{name}"
            return Tl(st.enter_context(nc.sbuf_tensor(nm, shape, dt)), nm)

        banks = [Tl(G.enter_context(nc.psum_tensor(f"pb{i}", [128, 512], F32)), f"pb{i}") for i in range(6)]
        obank = Tl(G.enter_context(nc.psum_tensor("pob", [128, 512], F32)), "pob")
        tbank = Tl(G.enter_context(nc.psum_tensor("ptb", [128, 1024], BF16)), "ptb")
        bk = {"i": 0}

        def bank():
            bk["i"] += 1
            return banks[bk["i"] % 6]

        cb = sb(G, "cb", [128, CB_N], BF16)
        pc = sb(G, "pc", [128, 32])
        vec = sb(G, "vec", [128, depth * VW])
        smb = sb(G, "smb", [128, SM_N], BF16)
        qg8 = sb(G, "qg8", [128, 1])
        ones_f = sb(G, "ones_f", [128, 128])
        wsl = [sb(G, f"wsl{i}", [128, 4096], BF16) for i in range(3)]
        wch = [P.chan(f"w{i}") for i in range(3)]
        kc_ = [[sb(G, f"kc{p}_{t}", [128, 512], BF16) for t in range(NT)] for p in range(3)]
        vc_ = [sb(G, f"vc{m}", [128, 384], BF16) for m in range(NT * 4)]
        hT = [sb(G, f"hT{c}", [128, 512], BF16) for c in range(8)]
        mixT = [sb(G, f"mixT{c}", [128, 512], BF16) for c in range(8)]
        ST = sb(G, "ST", [128, 384])
        STb = [sb(G, f"STb{i}", [128, 384], BF16) for i in range(2)]
        rhalo = sb(G, "rhalo", [128, 11])
        uhalo = [sb(G, f"uhalo{i}", [128, 16]) for i in range(2)]
        chalo = sb(G, "chalo", [128, 44, 2])
        c_misc = P.chan("misc")
        c_x = P.chan("x"); c_o = P.chan("o"); c_vf = P.chan("vf"); c_vfl = [P.chan("vfl0"), P.chan("vfl1")]

        ident = cb.t[:, CB_ID:CB_ID + 128]
        ones_b = cb.t[:, CB_ONES:CB_ONES + 128]
        bones_b = cb.t[:, CB_BONES:CB_BONES + 128]
        maskL = cb.t[:, CB_MASKL:CB_MASKL + 128]
        ntri = cb.t[:, CB_NTRI:CB_NTRI + 128]
        mask4 = cb.t[:, CB_MASK4:CB_MASK4 + 512]

        def amask(d):
            return cb.t[:, CB_AMASK + 512 * d: CB_AMASK + 512 * (d + 1)]

        dma("sp", cb.t[:], cb_d, c_misc, [], [cb])
        dma("sp", pc.t[:], pc_d, c_misc, [], [pc])
        dma("sp", vec.t[:], vec_d, c_misc, [], [vec])
        mset("dve", ones_f.t[:], 1.0, [ones_f])

        def vcol(l, name, j=0):
            o = l * VW + voff[name] + j
            return vec.t[:, o:o + 1]

        with ExitStack() as S:
            stg = [sb(S, f"stg{i}", [128, 2048]) for i in range(4)]
            stb = [sb(S, f"stb{i}", [128, 2048], BF16) for i in range(4)]
            cl = [P.chan(f"pl{i}") for i in range(4)]
            cs = [P.chan(f"ps{i}") for i in range(4)]
            n = 0
            pend = []
            for l in range(depth):
                for (src, dst, K, N) in ((w_in_d, wbi, D, NIN), (w_out_d, wbo, D, D), (w_up_d, wbu, D, NUP), (w_down_d, wbd, DFF, D)):
                    for r0 in range(0, K, 128):
                        for c0 in range(0, N, 2048):
                            w = min(2048, N - c0)
                            i = n % 4
                            n += 1
                            dma("sp", stg[i].t[:, 0:w], src[l, r0:r0 + 128, c0:c0 + w], cl[i], [], [stg[i]])
                            cp(("act", "dve", "pool")[n % 3], stb[i].t[:, 0:w], stg[i].t[:, 0:w], [stg[i]], [stb[i]])
                            pend.append((dst[l, r0:r0 + 128, c0:c0 + w], i, w))
                            if len(pend) > 2:
                                d_, i_, w_ = pend.pop(0)
                                dma("sp", d_, stb[i_].t[:, 0:w_], cs[i_], [stb[i_]], [r_wb], nowaw=True)
            for (d_, i_, w_) in pend:
                dma("sp", d_, stb[i_].t[:, 0:w_], cs[i_], [stb[i_]], [r_wb], nowaw=True)
            P.barrier(ALLQ, chans=cl + cs + [c_misc])

        wn = {"i": 0}

        def load_w(W2d, nk, k0, ranges):
            i = wn["i"] % 3
            wn["i"] += 1
            s = wsl[i]
            wc = sum(n_ for _, n_ in ranges)
            view = s.t[:, 0:nk * wc].rearrange("p (k c) -> p k c", c=wc)
            src = W2d[k0 * 128:(k0 + nk) * 128, :].rearrange("(k p) c -> p k c", p=128)
            o = 0
            first = True
            for (c0, n_) in ranges:
                dma("sp", view[:, :, o:o + n_], src[:, :, c0:c0 + n_], wch[i], [r_wb], [s], nowaw=not first)
                first = False
                o += n_
            return s, view

        try:
            stage(1)
            for l in range(depth):
              x_src = xT_d if l == 0 else x1_d
              x_dst = yT_d if l == depth - 1 else x1_d
              x_src_r = [] if l == 0 else [r_x1]
              x_dst_r = [] if l == depth - 1 else [r_x1]
              with ExitStack() as S:
                  smf = sb(S, "smf", [128, SM_N])
                  dma("pool", smf.t[:], sm_d[l], c_misc, [], [smf])
                  cp("dve", smb.t[:], smf.t[:], [smf], [smb])
                  P.op("act", lambda e, l=l: e.mul(out=qg8.t[:], in_=vcol(l, "q_gain"), mul=0.125), R([vec]), R([qg8]))
                  mset("dve", ST.t[:], 0.0, [ST]); mset("dve", STb[0].t[:], 0.0, [STb[0]])
                  mset("dve", rhalo.t[:], 0.0, [rhalo]); mset("dve", chalo.t[:], 0.0, [chalo])
                  for i in range(2):
                      mset("dve", uhalo[i].t[:], 0.0, [uhalo[i]])
                  P.barrier(COMPUTE, chans=[c_misc])
              stcur = 0
              w2a2 = smb.t[:, SM_W2A2:SM_W2A2 + 384]; g2b = smb.t[:, SM_G2:SM_G2 + 384]
              Win, Wout, Wup, Wdn = wbi[l], wbo[l], wbu[l], wbd[l]

              for it in range(NT):
                  t0 = it * TT

                  def rmsnorm(S, xt, gname):
                      sq = [sb(S, f"sq{i}", [128, 512], BF16) for i in range(2)]
                      lnv = sb(S, "lnv", [128, 512]); rstd = sb(S, "rstd", [128, 512])
                      b = bank()
                      for c in range(8):
                          act(sq[c % 2].t[:], xt[c].t[:], AF.Square, [xt[c]], [sq[c % 2]])
                          mm(b.t[:, :], ones_b, sq[c % 2].t[:], c == 0, c == 7, [cb, sq[c % 2]], [b])
                      act(lnv.t[:], b.t[:, :], AF.Ln, [b], [lnv], bias=1e-6, scale=1.0 / D)
                      act(rstd.t[:], lnv.t[:], AF.Exp, [lnv], [rstd], scale=-0.5)
                      for c in range(8):
                          stt("dve", hT[c].t[:], xt[c].t[:], vcol(l, gname, c), rstd.t[:], ALU.mult, ALU.mult,
                              [xt[c], vec, rstd], [hT[c]])

                  def win_tiles(view, j0, cts):
                      out = []
                      for i, ct in enumerate(cts):
                          b = bank()
                          for kc in range(8):
                              mm(b.t[:, :], view[:, kc, (j0 + i) * 128:(j0 + i + 1) * 128], hT[kc].t[:], kc == 0, kc == 7, [view_s[0], hT[kc]], [b])
                          out.append(b)
                      return out

                  view_s = [None]

                  SA = ExitStack()
                  with ExitStack() as S:
                      S = SA
                      xt = [sb(S, f"xa{c}", [128, 512]) for c in range(8)]
                      for c in range(8):
                          ev_ = dma("pool", xt[c].t[:], x_src[c * 128:(c + 1) * 128, t0:t0 + TT], c_x, x_src_r, [xt[c]])
                          if c == 7 and ev_ is not None:
                              for c2 in range(8):
                                  xt[c2].r.w = ev_
                      rmsnorm(S, xt, "ln1_g")

                  dump(hT[0], hT[0].t[:], 0, bf=True); dump(hT[7], hT[7].t[:], 1, bf=True)
                  stage(2)
                  with ExitStack() as S:
                      S = SA
                      s_, view = load_w(Win, 8, 0, [(20 * 128, 256)])
                      view_s[0] = s_
                      bs = win_tiles(view, 0, [20, 21])
                      ur = [sb(S, f"ur{i}", [128, 528]) for i in range(2)]
                      sA = sb(S, "sA", [128, 528]); sB = sb(S, "sB", [128, 528])
                      pl = sb(S, "pl", [128, 512], BF16)
                      for i in range(2):
                          u = ur[i]
                          cp("act", u.t[:, 16:528], bs[i].t[:, :], [bs[i]], [u])
                          cp("dve", u.t[:, 0:16], uhalo[i].t[:], [uhalo[i]], [u])
                          cp("dve", uhalo[i].t[:], u.t[:, 512:528], [u], [uhalo[i]])
                          tt("dve", sA.t[:, 1:528], u.t[:, 1:528], u.t[:, 0:527], ALU.add, [u], [sA])
                          tt("dve", sB.t[:, 3:528], sA.t[:, 3:528], sA.t[:, 1:526], ALU.add, [sA], [sB])
                          if i == 0:
                              lo, hi, wl, wh = sA, sB, 2.0, 4.0
                          else:
                              tt("dve", sA.t[:, 7:528], sB.t[:, 7:528], sB.t[:, 3:524], ALU.add, [sB], [sA])
                              tt("dve", sB.t[:, 15:528], sA.t[:, 15:528], sA.t[:, 7:520], ALU.add, [sA], [sB])
                              lo, hi, wl, wh = sA, sB, 8.0, 16.0
                          if it == 0:
                              tt("dve", lo.t[0:64, 16:32], lo.t[0:64, 16:32], pc.t[0:64, i * 16:(i + 1) * 16], ALU.mult, [lo, pc], [lo])
                              tt("dve", hi.t[64:128, 16:32], hi.t[64:128, 16:32], pc.t[64:128, i * 16:(i + 1) * 16], ALU.mult, [hi, pc], [hi])
                          stt("dve", pl.t[0:64, :], lo.t[0:64, 16:528], 1.0 / wl, u.t[0:64, 16:528], ALU.mult, ALU.subtract, [lo, u], [pl])
                          stt("dve", pl.t[64:128, :], hi.t[64:128, 16:528], 1.0 / wh, u.t[64:128, 16:528], ALU.mult, ALU.subtract, [hi, u], [pl])
                          b = bank()
                          mm(b.t[:, :], smb.t[:, SM_PW + i * 128: SM_PW + (i + 1) * 128], pl.t[:], True, True, [smb, pl], [b])
                          ts("dve", mixT[6 + i].t[:], b.t[:, :], vcol(l, "pool_b", i), vcol(l, "pool_scale", i), ALU.add, ALU.mult,
                             [b, vec], [mixT[6 + i]])

                  dump(mixT[6], mixT[6].t[:], 2, bf=True); dump(mixT[7], mixT[7].t[:], 3, bf=True)
                  stage(3)
                  with ExitStack() as S:
                      S = SA
                      qT = [sb(S, f"qT{p}", [128, 512], BF16) for p in range(3)]
                      vT = [sb(S, f"vT{p}", [128, 512], BF16) for p in range(3)]
                      sqq = [sb(S, f"sqq{i}", [128, 512], BF16) for i in range(2)]
                      lnq2 = [sb(S, f"lnq{i}", [128, 512]) for i in range(2)]; rsq2 = [sb(S, f"rsq{i}", [128, 512]) for i in range(2)]
                      groups = [[(11 * 128, 512)], [(15 * 128, 512)], [(19 * 128, 128)]]
                      cts = [[11, 12, 13, 14], [15, 16, 17, 18], [19]]
                      nq = 0
                      for gi in range(3):
                          s_, view = load_w(Win, 8, 0, groups[gi])
                          view_s[0] = s_
                          bs = win_tiles(view, 0, cts[gi])
                          for b, ct in zip(bs, cts[gi]):
                              if ct < 17:
                                  isq = ct < 14
                                  p = ct - 11 if isq else ct - 14
                                  sq_ = sqq[nq % 2]; lnq = lnq2[nq % 2]; rsq = rsq2[nq % 2]; nq += 1
                                  act(sq_.t[:], b.t[:, :], AF.Square, [b], [sq_])
                                  b2 = bank()
                                  mm(b2.t[:, :], bones_b, sq_.t[:], True, True, [cb, sq_], [b2])
                                  act(lnq.t[:], b2.t[:, :], AF.Ln, [b2], [lnq], bias=1e-6, scale=1.0 / 64)
                                  act(rsq.t[:], lnq.t[:], AF.Exp, [lnq], [rsq], scale=-0.5)
                                  dst = qT[p] if isq else kc_[p][it]
                                  gcol = qg8.t[:, 0:1] if isq else vcol(l, "k_gain")
                                  stt("dve", dst.t[:], b.t[:, :], gcol, rsq.t[:], ALU.mult, ALU.mult, [b, rsq, qg8, vec], [dst])
                              else:
                                  cp("act", vT[ct - 17].t[:], b.t[:, :], [b], [vT[ct - 17]])
                      for s in range(4):
                          h = s % 2
                          for p in range(3):
                              tr(tbank.t[:, h * 512 + p * 128: h * 512 + (p + 1) * 128], vT[p].t[:, s * 128:(s + 1) * 128], ident, [vT[p], cb], [tbank])
                          cp(alt(), vc_[it * 4 + s].t[:], tbank.t[:, h * 512: h * 512 + 384], [tbank], [vc_[it * 4 + s]])
                      Eb = [sb(S, f"Eb{i}", [128, 512]) for i in range(4)]
                      SPb = [sb(S, f"SPb{i}", [128, 512], BF16) for i in range(4)]
                      Rbc2 = [sb(S, f"Rbc{i}", [128, 512]) for i in range(2)]
                      argb = [sb(S, f"argb{i}", [128, 512]) for i in range(4)]
                      Ab = [sb(S, f"Ab{i}", [128, 512], BF16) for i in range(4)]
                      last = 4 * it + 3
                      units = [(p, m, e_) for p in range(3) for m in range(last, -1, -1) for e_ in range(2)]
                      NU = len(units)
                      ust = [dict() for _ in range(NU)]

                      def c0of(m):
                          d = m - 4 * it
                          return 128 * d if d > 0 else 0

                      def s1(u):
                          p, m, e_ = units[u]; pb = 64 * e_; c0 = c0of(m)
                          kt = kc_[p][m // 4]
                          zb = bank(); ust[u]["zb"] = zb
                          mm(zb.t[:, c0:512], kt.t[pb:pb + 64, (m % 4) * 128:(m % 4 + 1) * 128], qT[p].t[pb:pb + 64, c0:512], True, True, [kt, qT[p]], [zb])

                      def s2(u):
                          p, m, e_ = units[u]; d = m - 4 * it; c0 = c0of(m)
                          zb = ust[u]["zb"]; E, SP = Eb[u % 4], SPb[u % 4]
                          act(E.t[:, c0:512], zb.t[:, c0:512], AF.Exp, [zb], [E])
                          act(SP.t[:, c0:512], E.t[:, c0:512], AF.Ln, [E], [SP], bias=1.0)
                          if d >= 0:
                              tt("pool", SP.t[:, c0:512], SP.t[:, c0:512], amask(d)[:, c0:512], ALU.mult, [SP, cb], [SP])

                      def s3(u):
                          p, m, e_ = units[u]; c0 = c0of(m)
                          zb = ust[u]["zb"]; SP = SPb[u % 4]
                          mm(zb.t[:, c0:512], ntri, SP.t[:, c0:512], False, True, [cb, SP], [zb])
                          if m > 0:
                              rb = bank(); ust[u]["rb"] = rb
                              mm(rb.t[:, c0:512], ones_b, SP.t[:, c0:512], True, True, [cb, SP], [rb])

                      def s4(u):
                          p, m, e_ = units[u]; c0 = c0of(m)
                          zb = ust[u]["zb"]; ar = argb[u % 4]; Rbc = Rbc2[e_]
                          if m != last:
                              tt("dve", ar.t[:, c0:512], zb.t[:, c0:512], Rbc.t[:, c0:512], ALU.subtract, [zb, Rbc], [ar])
                          if m > 0:
                              rb = ust[u]["rb"]
                              if m == last:
                                  if c0 > 0:
                                      mset("pool", Rbc.t[:, 0:c0], 0.0, [Rbc])
                                  cp("dve", Rbc.t[:, c0:512], rb.t[:, c0:512], [rb], [Rbc])
                              else:
                                  tt("dve", Rbc.t[:, c0:512], Rbc.t[:, c0:512], rb.t[:, c0:512], ALU.add, [Rbc, rb], [Rbc])

                      def s5(u):
                          p, m, e_ = units[u]; d = m - 4 * it; c0 = c0of(m)
                          zb = ust[u]["zb"]; ar = argb[u % 4]; A = Ab[u % 4]
                          if m == last:
                              act(A.t[:, c0:512], zb.t[:, c0:512], AF.Exp, [zb], [A])
                          else:
                              act(A.t[:, c0:512], ar.t[:, c0:512], AF.Exp, [ar], [A])
                          if d >= 0:
                              tt("pool", A.t[:, c0:512], A.t[:, c0:512], amask(d)[:, c0:512], ALU.mult, [A, cb], [A])

                      def s6(u):
                          p, m, e_ = units[u]; pb = 64 * e_; h = 2 * p + e_; d = m - 4 * it; c0 = c0of(m)
                          A = Ab[u % 4]
                          vv_ = vc_[m].t[:, h * 64:(h + 1) * 64]
                          mm(obank.t[pb:pb + 64, c0:512], vv_, A.t[:, c0:512], m == last, m == 0, [vc_[m], A], [obank])
                          if m == 0 and e_ == 1:
                              cp("act", mixT[3 + p].t[:], obank.t[:, :], [obank], [mixT[3 + p]])

                      for t in range(NU + 3):
                          if t < NU:
                              s1(t); s2(t)
                          if 0 <= t - 1 < NU:
                              s3(t - 1); s4(t - 1); s5(t - 1)
                          if 0 <= t - 3 < NU:
                              s6(t - 3)
                      P.barrier(COMPUTE, chans=[c_x])
                  SA.close()

                  dump(mixT[3], mixT[3].t[:], 4, bf=True); dump(kc_[0][it], kc_[0][it].t[:], 5, bf=True); dump(vc_[it * 4], vc_[it * 4].t[:], 6, bf=True, w=384); dump(mixT[5], mixT[5].t[:], 7, bf=True)
                  stage(4)
                  with ExitStack() as S:
                      lor = sb(S, "lor", [128, 512], BF16); sgx = sb(S, "sgx", [128, 512], BF16)
                      raw = [sb(S, f"raw{i}", [128, 513]) for i in range(2)]
                      dtmp = sb(S, "dtmp", [128, 512])
                      ART = [sb(S, f"ART{p}", [128, 4, 2, 128], BF16) for p in range(3)]
                      BT = [sb(S, f"BT{p}", [128, 512], BF16) for p in range(3)]
                      KT = [sb(S, f"KT{p}", [128, 512], BF16) for p in range(3)]
                      GC = [sb(S, f"GC{p}", [128, 4]) for p in range(3)]
                      gT = [sb(S, f"gT{p}", [128, 512], BF16) for p in range(3)]
                      bonT = [sb(S, f"bonT{p}", [128, 512], BF16) for p in range(3)]
                      TOK = [sb(S, f"TOK{c}", [128, 4, 384], BF16) for c in range(4)]
                      Lp = sb(S, "Lp", [128, 512]); a_t = sb(S, "a_t", [128, 512]); kkn = sb(S, "kkn", [128, 512])
                      bv = sb(S, "bv", [128, 512]); et = sb(S, "et", [128, 512]); t2 = sb(S, "t2", [128, 512])
                      BHT = sb(S, "BHT", [128, 512], BF16); KHT = sb(S, "KHT", [128, 512], BF16)
                      vb = [sb(S, f"vb{p}", [128, 512], BF16) for p in range(3)]
                      sqk = sb(S, "sqk", [128, 512], BF16); lob = sb(S, "lob", [32, 512], BF16)
                      cLC = sb(S, "cLC", [128, 4])

                      def evac_shift(b, ct, dst, dtmp=dtmp):
                          cp("act", dst.t[:, 1:513], b.t[:, :], [b], [dst])
                          cp("dve", dst.t[:, 0:1], rhalo.t[:, ct:ct + 1], [rhalo], [dst])
                          tt("dve", dtmp.t[:], dst.t[:, 0:512], dst.t[:, 1:513], ALU.subtract, [dst], [dtmp])
                          cp("dve", rhalo.t[:, ct:ct + 1], dst.t[:, 512:513], [dst], [rhalo])
                          stt("dve", dst.t[:, 1:513], dtmp.t[:], vcol(l, "mu", ct), dst.t[:, 1:513], ALU.mult, ALU.add, [dtmp, vec, dst], [dst])

                      s_, view = load_w(Win, 8, 0, [(9 * 128, 256)])
                      view_s[0] = s_
                      bs = win_tiles(view, 0, [9, 10])
                      evac_shift(bs[0], 9, raw[0])
                      act(lor.t[0:64, :], raw[0].t[0:64, 1:513], AF.Tanh, [raw[0]], [lor])
                      cp("dve", lor.t[64:128, :], raw[0].t[64:128, 1:513], [raw[0]], [lor])
                      evac_shift(bs[1], 10, raw[1])
                      act(sgx.t[:], raw[1].t[:, 1:513], AF.Sigmoid, [raw[1]], [sgx])
                      stage(41)
                      vr = [sb(S, f"vr{p}", [128, 513]) for p in range(3)]
                      s_, view = load_w(Win, 8, 0, [(6 * 128, 384)])
                      view_s[0] = s_
                      bs = win_tiles(view, 0, [6, 7, 8])
                      for p in range(3):
                          evac_shift(bs[p], 6 + p, vr[p])
                      if l > 0:
                          for p in range(3):
                              cp("pool", vb[p].t[:], vr[p].t[:, 1:513], [vr[p]], [vb[p]])
                          b = bank()
                          for pp in range(3):
                              mm(b.t[0:32, :], smb.t[:, SM_V1 + pp * 32: SM_V1 + (pp + 1) * 32], vb[pp].t[:], pp == 0, pp == 2, [smb, vb[pp]], [b])
                          cp("act", lob.t[:], b.t[0:32, :], [b], [lob])
                      rawS = [raw, [sb(S, f"rawb{i}", [128, 513]) for i in range(2)]]
                      dtmpS = [dtmp, sb(S, "dtmpb", [128, 512])]
                      t2S = [t2, sb(S, "t2b", [128, 512])]; etS = [et, sb(S, "etb", [128, 512])]
                      LpS = [Lp, sb(S, "Lpb", [128, 512])]; a_tS = [a_t, sb(S, "a_tb", [128, 512])]

                      def prep_early(p):
                          s2_ = p % 2
                          raw_, dtmp_, t2, et, Lp, a_t = rawS[s2_], dtmpS[s2_], t2S[s2_], etS[s2_], LpS[s2_], a_tS[s2_]
                          pc0 = p * 128
                          s_, view = load_w(Win, 8, 0, [(p * 128, 128), ((3 + p) * 128, 128)])
                          view_s[0] = s_
                          bs = win_tiles(view, 0, [p, 3 + p])
                          evac_shift(bs[0], p, raw_[0], dtmp_)
                          evac_shift(bs[1], 3 + p, raw_[1], dtmp_)
                          rw = [raw_[0], raw_[1], vr[p]]
                          r_, k_, v_ = rw[0], rw[1], rw[2]
                          rT, kTr, vTr = r_.t[:, 1:513], k_.t[:, 1:513], v_.t[:, 1:513]
                          if l == 0:
                              dma("pool", vf_d[pc0:pc0 + 128, t0:t0 + TT], vTr, c_vf, [v_], [r_vf], nowaw=True)
                          else:
                              b = bank()
                              mm(b.t[:, :], smb.t[0:32, SM_V2 + pc0: SM_V2 + pc0 + 128], lob.t[:], True, True, [smb, lob], [b])
                              act(et.t[:], b.t[:, :], AF.Sigmoid, [b], [et], bias=vcol(l, "v0", p))
                              dma("pool", t2.t[:], vf_d[pc0:pc0 + 128, t0:t0 + TT], c_vfl[s2_], [r_vf], [t2])
                              tt("dve", t2.t[:], t2.t[:], vTr, ALU.subtract, [t2, v_], [t2])
                              tt("dve", t2.t[:], t2.t[:], et.t[:], ALU.mult, [t2, et], [t2])
                              tt("dve", vTr, vTr, t2.t[:], ALU.add, [v_, t2], [v_])
                          cp("pool", vb[p].t[:], vTr, [v_], [vb[p]])
                          b = bank()
                          mm(b.t[:, :], w2a2[0:64, pc0:pc0 + 128], lor.t[0:64, :], True, True, [smb, lor], [b])
                          act(et.t[:], b.t[:, :], AF.Sigmoid, [b], [et], bias=vcol(l, "w0", p))
                          for c in range(4):
                              P.op("dve", (lambda o_, d0_, d1_: lambda e: e.tensor_tensor_scan(out=o_, data0=d0_, data1=d1_, initial=0.0, op0=ALU.mult, op1=ALU.add))(
                              Lp.t[:, c * 128:(c + 1) * 128], ones_f.t[:], et.t[:, c * 128:(c + 1) * 128]), R([ones_f, et]), R([Lp]))
                          b = bank()
                          mm(b.t[:, :], w2a2[64:128, pc0:pc0 + 128], lor.t[64:128, :], True, True, [smb, lor], [b])
                          act(a_t.t[:], b.t[:, :], AF.Sigmoid, [b], [a_t], bias=vcol(l, "a0", p))
                          b = bank()
                          mm(b.t[:, :], g2b[:, pc0:pc0 + 128], sgx.t[:], True, True, [smb, sgx], [b])
                          cp("act", gT[p].t[:], b.t[:, :], [b], [gT[p]])

                      def prep_late(p):
                          s2_ = p % 2
                          raw_, dtmp_, t2, et, Lp, a_t = rawS[s2_], dtmpS[s2_], t2S[s2_], etS[s2_], LpS[s2_], a_tS[s2_]
                          pc0 = p * 128
                          r_, k_, v_ = raw_[0], raw_[1], vr[p]
                          rT, kTr, vTr = r_.t[:, 1:513], k_.t[:, 1:513], v_.t[:, 1:513]
                          act(sqk.t[:], kTr, AF.Square, [k_, vec], [sqk], scale=vcol(l, "k_k", p))
                          b = bank()
                          mm(b.t[:, :], bones_b, sqk.t[:], True, True, [cb, sqk], [b])
                          act(t2.t[:], b.t[:, :], AF.Ln, [b], [t2], bias=1e-24)
                          act(t2.t[:], t2.t[:], AF.Exp, [t2], [t2], scale=-0.5)
                          stt("dve", kkn.t[:], kTr, vcol(l, "k_k", p), t2.t[:], ALU.mult, ALU.mult, [k_, vec, t2], [kkn])
                          tt("dve", bv.t[:], kkn.t[:], a_t.t[:], ALU.mult, [kkn, a_t], [bv])
                          ts("dve", t2.t[:], a_t.t[:], -1.0, vcol(l, "k_a", p), ALU.add, ALU.mult, [a_t, vec], [t2])
                          stt("dve", kTr, t2.t[:], 1.0, kTr, ALU.add, ALU.mult, [t2, k_], [k_])
                          stt("dve", sqk.t[:], rT, vcol(l, "r_k", p), kTr, ALU.mult, ALU.mult, [r_, k_, vec], [sqk])
                          b = bank()
                          mm(b.t[:, :], bones_b, sqk.t[:], True, True, [cb, sqk], [b])
                          tt("dve", bonT[p].t[:], b.t[:, :], vTr, ALU.mult, [b, v_], [bonT[p]])
                          A4 = ART[p].t[:, :, 0, :]; R4 = ART[p].t[:, :, 1, :]
                          v4 = lambda ap: ap.rearrange("p (c t) -> p c t", t=128)
                          tt("dve", t2.t[:], Lp.t[:], et.t[:], ALU.subtract, [Lp, et], [t2])
                          act(t2.t[:], t2.t[:], AF.Exp, [t2], [t2], scale=CDEC)
                          stt("dve", A4, v4(kkn.t[:]), -1.0, v4(t2.t[:]), ALU.mult, ALU.mult, [kkn, t2], [ART[p]])
                          act(t2.t[:], Lp.t[:], AF.Exp, [Lp], [t2], scale=CDEC)
                          tt("dve", R4, v4(rT), v4(t2.t[:]), ALU.mult, [r_, t2], [ART[p]])
                          for c in range(4):
                              cp("dve", GC[p].t[:, c:c + 1], t2.t[:, c * 128 + 127: c * 128 + 128], [t2], [GC[p]])
                          act(t2.t[:], Lp.t[:], AF.Exp, [Lp], [t2], scale=-CDEC)
                          tt("dve", BT[p].t[:], bv.t[:], t2.t[:], ALU.mult, [bv, t2], [BT[p]])
                          tt("dve", KT[p].t[:], kTr, t2.t[:], ALU.mult, [k_, t2], [KT[p]])
                          for c in range(4):
                              ts("dve", cLC.t[:, c:c + 1], Lp.t[:, c * 128 + 127: c * 128 + 128], CDEC, None, ALU.mult, None, [Lp], [cLC])
                          for c in range(4):
                              act(t2.t[:, c * 128:(c + 1) * 128], Lp.t[:, c * 128:(c + 1) * 128], AF.Exp, [Lp, cLC], [t2],
                                  scale=-CDEC, bias=cLC.t[:, c:c + 1])
                          tt("dve", BHT.t[:], bv.t[:], t2.t[:], ALU.mult, [bv, t2], [BHT])
                          tt("dve", KHT.t[:], kTr, t2.t[:], ALU.mult, [k_, t2], [KHT])
                          for c in range(4):
                              h = c % 2
                              srcs = [(ART[p], ART[p].t[:, c, 0, :]), (BHT, BHT.t[:, c * 128:(c + 1) * 128]),
                                      (KHT, KHT.t[:, c * 128:(c + 1) * 128]), (vb[p], vb[p].t[:, c * 128:(c + 1) * 128])]
                              for kd, (tl_, ap_) in enumerate(srcs):
                                  tr(tbank.t[:, h * 512 + kd * 128: h * 512 + (kd + 1) * 128], ap_, ident, [tl_, cb], [tbank])
                              cp(alt(), TOK[c].t[:, :, pc0:pc0 + 128],
                                 tbank.t[:, h * 512:(h + 1) * 512].rearrange("p (k c) -> p k c", c=128), [tbank], [TOK[c]])


                      prep_early(0)
                      for p in range(3):
                          if p + 1 < 3:
                              prep_early(p + 1)
                          prep_late(p)
                      stage(42)
                      SCb = [sb(S, f"SCb{h}", [128, 512], BF16) for h in range(6)]
                      XTb = [sb(S, f"XTb{h}", [128, 128], BF16) for h in range(6)]
                      PP = [[sb(S, f"PP{p}_{i}", [128, 512], BF16) for i in range(2)] for p in range(3)]
                      ACC = [[sb(S, f"ACC{p}_{i}", [128, 256], BF16) for i in range(2)] for p in range(3)]
                      T1b = sb(S, "T1b", [128, 384], BF16); UH = sb(S, "UH", [128, 384]); WhT = sb(S, "WhT", [128, 384], BF16)
                      Ub = sb(S, "Ub", [128, 384], BF16); ysq = sb(S, "ysq", [128, 384]); yn = sb(S, "yn", [128, 384], BF16)
                      st1 = sb(S, "st1", [128, 6]); st2 = sb(S, "st2", [128, 6]); st3 = sb(S, "st3", [128, 6])
                      YT = sb(S, "YT", [128, 3, 512], BF16)
                      for c in range(4):
                          tok = TOK[c]
                          csl = slice(c * 128, (c + 1) * 128)
                          for p in range(3):
                              for e_ in range(2):
                                  pb = 64 * e_; h = 2 * p + e_
                                  b = bank()
                                  ar2 = ART[p].t[pb:pb + 64, c, :, :].rearrange("p a t -> p (a t)")
                                  mm(b.t[:, 0:256], BT[p].t[pb:pb + 64, csl], ar2, True, True, [BT[p], ART[p]], [b])
                                  mm(b.t[:, 256:512], KT[p].t[pb:pb + 64, csl], ar2, True, True, [KT[p], ART[p]], [b])
                                  tt("dve", SCb[h].t[:], b.t[:, :], mask4, ALU.mult, [b, cb], [SCb[h]])
                                  b = bank()
                                  mm(b.t[:, 0:128], ART[p].t[pb:pb + 64, c, 0, :], BT[p].t[pb:pb + 64, csl], True, True, [BT[p], ART[p]], [b])
                                  tt("dve", XTb[h].t[:], b.t[:, 0:128], maskL, ALU.mult, [b, cb], [XTb[h]])
                          stage(43)
                          b = bank()
                          for h in range(6):
                              mm(b.t[:, h * 64:(h + 1) * 64], SCb[h].t[:, 256:384], tok.t[:, 3, h * 64:(h + 1) * 64], True, True, [SCb[h], tok], [b])
                          cp("act", T1b.t[:], b.t[:, 0:384], [b], [T1b])
                          cur = [0, 0, 0]
                          for p in range(3):
                              for e_ in range(2):
                                  h = 2 * p + e_
                                  tt("pool", ACC[p][0].t[:, e_ * 128:(e_ + 1) * 128], SCb[h].t[:, 0:128], ident, ALU.add, [SCb[h], cb], [ACC[p][0]])
                          for k in range(1, 7):
                              i0 = (k - 1) % 2; i1 = k % 2
                              bq = []
                              for p in range(3):
                                  b = bank(); bq.append(b)
                                  for e_ in range(2):
                                      h = 2 * p + e_
                                      if k == 1:
                                          Pm, PTm, rds = SCb[h].t[:, 0:128], XTb[h].t[:], [SCb[h], XTb[h]]
                                      else:
                                          Pm = PP[p][i0].t[:, e_ * 256: e_ * 256 + 128]; PTm = PP[p][i0].t[:, e_ * 256 + 128: e_ * 256 + 256]
                                          rds = [PP[p][i0]]
                                      if k < 6:
                                          mm(b.t[:, e_ * 256: e_ * 256 + 128], PTm, Pm, True, True, rds, [b])
                                      mm(b.t[:, e_ * 256 + 128: e_ * 256 + 256], Pm, PTm, True, True, rds, [b])
                              for p in range(3):
                                  b = bq[p]
                                  if k < 6:
                                      cp(alt(), PP[p][i1].t[:], b.t[:, :], [b], [PP[p][i1]])
                                  else:
                                      for e_ in range(2):
                                          cp(alt(), PP[p][i1].t[:, e_ * 256 + 128: e_ * 256 + 256], b.t[:, e_ * 256 + 128: e_ * 256 + 256], [b], [PP[p][i1]])
                              bq2 = []
                              for p in range(3):
                                  b2 = bank(); bq2.append(b2)
                                  for e_ in range(2):
                                      mm(b2.t[:, e_ * 128:(e_ + 1) * 128], PP[p][i1].t[:, e_ * 256 + 128: e_ * 256 + 256],
                                         ACC[p][i0].t[:, e_ * 128:(e_ + 1) * 128], True, True, [PP[p][i1], ACC[p][i0]], [b2])
                              for p in range(3):
                                  tt("dve", ACC[p][i1].t[:], ACC[p][i0].t[:], bq2[p].t[:, 0:256], ALU.add, [ACC[p][i0], bq2[p]], [ACC[p][i1]])
                          stage(44)
                          NTf = [ACC[p][0] for p in range(3)]
                          b = bank()
                          for h in range(6):
                              p, e_ = h // 2, h % 2
                              mm(b.t[64 * e_:64 * e_ + 64, p * 128:(p + 1) * 128], tok.t[:, 0, h * 64:(h + 1) * 64], NTf[p].t[:, e_ * 128:(e_ + 1) * 128],
                                 True, True, [tok, NTf[p]], [b])
                          cp("act", WhT.t[:], b.t[:, 0:384], [b], [WhT])
                          b = bank()
                          for h in range(6):
                              p, e_ = h // 2, h % 2
                              mm(b.t[:, h * 64:(h + 1) * 64], NTf[p].t[:, e_ * 128:(e_ + 1) * 128], T1b.t[:, h * 64:(h + 1) * 64], True, True, [NTf[p], T1b], [b])
                          cp("act", UH.t[:], b.t[:, 0:384], [b], [UH])
                          stage(45)
                          so = STb[stcur]; sn = STb[1 - stcur]
                          b = bank()
                          for p in range(3):
                              mm(b.t[:, p * 128:(p + 1) * 128], WhT.t[:, p * 128:(p + 1) * 128], so.t[:, p * 128:(p + 1) * 128], True, True, [WhT, so], [b])
                          tt("dve", Ub.t[:], b.t[:, 0:384], UH.t[:], ALU.add, [b, UH], [Ub])
                          stage(451)
                          yb = bank()
                          for h in range(6):
                              p, e_ = h // 2, h % 2; pb = 64 * e_
                              hs = slice(h * 64, (h + 1) * 64)
                              if e_ == 0:
                                  mm(yb.t[:, p * 128:(p + 1) * 128], ART[p].t[:, c, 1, :], so.t[:, p * 128:(p + 1) * 128], True, False, [ART[p], so], [yb])
                              mm(yb.t[:, hs], SCb[h].t[:, 128:256], Ub.t[:, hs], False, False, [SCb[h], Ub], [yb])
                              mm(yb.t[:, hs], SCb[h].t[:, 384:512], tok.t[:, 3, hs], False, True, [SCb[h], tok], [yb])
                          stage(452)
                          sbk = bank()
                          for h in range(6):
                              p, e_ = h // 2, h % 2; pb = 64 * e_
                              hs = slice(h * 64, (h + 1) * 64)
                              mm(sbk.t[pb:pb + 64, hs], tok.t[:, 1, hs], Ub.t[:, hs], True, False, [tok, Ub], [sbk])
                              mm(sbk.t[pb:pb + 64, hs], tok.t[:, 2, hs], tok.t[:, 3, hs], False, True, [tok], [sbk])
                          stage(453)
                          for h in range(6):
                              p, e_ = h // 2, h % 2; pb = 64 * e_
                              hs = slice(h * 64, (h + 1) * 64)
                              stt("dve", ST.t[pb:pb + 64, hs], ST.t[pb:pb + 64, hs], GC[p].t[pb:pb + 64, c:c + 1], sbk.t[pb:pb + 64, hs],
                                  ALU.mult, ALU.add, [ST, GC[p], sbk], [ST])
                          cp("act", sn.t[:], ST.t[:], [ST], [sn])
                          stcur = 1 - stcur
                          stage(46)
                          y3 = yb.t[:, 0:384].rearrange("p (h i) -> p h i", i=64)
                          P.op("dve", (lambda o_, i_: lambda e: e.tensor_reduce(out=o_, in_=i_, axis=AX.X, op=ALU.add))(st1.t[:], y3), R([yb]), R([st1]))
                          act(ysq.t[:], yb.t[:, 0:384], AF.Square, [yb], [ysq])
                          P.op("dve", (lambda o_, i_: lambda e: e.tensor_reduce(out=o_, in_=i_, axis=AX.X, op=ALU.add))(
                              st2.t[:], ysq.t[:].rearrange("p (h i) -> p h i", i=64)), R([ysq]), R([st2]))
                          ts("dve", st1.t[:], st1.t[:], 1.0 / 64, None, ALU.mult, None, [st1], [st1])
                          tt("dve", st3.t[:], st1.t[:], st1.t[:], ALU.mult, [st1], [st3])
                          stt("dve", st2.t[:], st2.t[:], 1.0 / 64, st3.t[:], ALU.mult, ALU.subtract, [st2, st3], [st2])
                          act(st2.t[:], st2.t[:], AF.Ln, [st2], [st2], bias=64e-5)
                          act(st2.t[:], st2.t[:], AF.Exp, [st2], [st2], scale=-0.5)
                          for h in range(6):
                              hs = slice(h * 64, (h + 1) * 64)
                              ts("dve", yn.t[:, hs], yb.t[:, hs], st1.t[:, h:h + 1], st2.t[:, h:h + 1], ALU.subtract, ALU.mult, [yb, st1, st2], [yn])
                          stage(47)
                          hh = c % 2
                          for p in range(3):
                              tr(tbank.t[:, hh * 512 + p * 128: hh * 512 + (p + 1) * 128], yn.t[:, p * 128:(p + 1) * 128], ident, [yn, cb], [tbank])
                          cp("act", YT.t[:, :, csl], tbank.t[:, hh * 512: hh * 512 + 384].rearrange("p (k c) -> p k c", c=128), [tbank], [YT])
                      for p in range(3):
                          act(t2.t[:], YT.t[:, p, :], AF.Identity, [YT, vec], [t2], scale=vcol(l, "lnx_w", p), bias=vcol(l, "lnx_b", p))
                          tt("dve", t2.t[:], t2.t[:], bonT[p].t[:], ALU.add, [t2, bonT[p]], [t2])
                          tt("dve", mixT[p].t[:], t2.t[:], gT[p].t[:], ALU.mult, [t2, gT[p]], [mixT[p]])
                      P.barrier(COMPUTE, chans=[c_vf] + c_vfl)

                  dump(mixT[0], mixT[0].t[:], 8, bf=True); dump(mixT[1], mixT[1].t[:], 9, bf=True); dump(mixT[2], mixT[2].t[:], 10, bf=True)
                  stage(5)
                  with ExitStack() as S:
                      xt = [sb(S, f"xa{c}", [128, 512]) for c in range(8)]
                      for c in range(8):
                          ev_ = dma("pool", xt[c].t[:], x_src[c * 128:(c + 1) * 128, t0:t0 + TT], c_x, x_src_r, [xt[c]])
                          if c == 7 and ev_ is not None:
                              for c2 in range(8):
                                  xt[c2].r.w = ev_
                      for g in range(2):
                          s_, view = load_w(Wout, 8, 0, [(g * 512, 512)])
                          for j in range(4):
                              dt_ = g * 4 + j
                              b = bank()
                              for kc in range(8):
                                  mm(b.t[:, :], view[:, kc, j * 128:(j + 1) * 128], mixT[kc].t[:], kc == 0, kc == 7, [s_, mixT[kc]], [b])
                              tt("dve", xt[dt_].t[:], xt[dt_].t[:], b.t[:, :], ALU.add, [xt[dt_], b], [xt[dt_]])
                      rmsnorm(S, xt, "ln2_g")
                      graw = [sb(S, f"graw{i}", [128, 514]) for i in range(2)]
                      vraw = [sb(S, f"vraw{i}", [128, 514]) for i in range(2)]
                      cg = [sb(S, f"cg{i}", [128, 512]) for i in range(2)]
                      cv = [sb(S, f"cv{i}", [128, 512]) for i in range(2)]
                      gv = [sb(S, f"gv{i}", [128, 512], BF16) for i in range(11)]
                      nf = 0
                      for half in range(2):
                          for (g0, ng) in ((0, 4), (4, 4), (8, 3)):
                              ctg = half * 11 + g0
                              sg_, vg = load_w(Wup, 8, 0, [(ctg * 128, ng * 128)])
                              sv_, vv = load_w(Wup, 8, 0, [((22 + ctg) * 128, ng * 128)])
                              for j in range(ng):
                                  i2 = nf % 2; nf += 1
                                  ct = ctg + j
                                  res = []
                                  for (sl_, vw_, cti, rawt, cout) in ((sg_, vg, ct, graw[i2], cg[i2]), (sv_, vv, 22 + ct, vraw[i2], cv[i2])):
                                      b = bank()
                                      for kc in range(8):
                                          mm(b.t[:, :], vw_[:, kc, j * 128:(j + 1) * 128], hT[kc].t[:], kc == 0, kc == 7, [sl_, hT[kc]], [b])
                                      cp("act", rawt.t[:, 2:514], b.t[:, :], [b], [rawt])
                                      cp("pool", rawt.t[:, 0:2], chalo.t[:, cti, :], [chalo], [rawt])
                                      cp("pool", chalo.t[:, cti, :], rawt.t[:, 512:514], [rawt], [chalo])
                                      act(cout.t[:], b.t[:, :], AF.Identity, [b, vec], [cout], scale=vcol(l, "conv_w", 2 * 44 + cti), bias=vcol(l, "conv_b", cti))
                                      stt("dve", cout.t[:], rawt.t[:, 1:513], vcol(l, "conv_w", 44 + cti), cout.t[:], ALU.mult, ALU.add, [rawt, vec, cout], [cout])
                                      stt("dve", cout.t[:], rawt.t[:, 0:512], vcol(l, "conv_w", cti), cout.t[:], ALU.mult, ALU.add, [rawt, vec, cout], [cout])
                                  act(cg[i2].t[:], cg[i2].t[:], AF.Silu, [cg[i2]], [cg[i2]])
                                  tt("pool", gv[g0 + j].t[:], cg[i2].t[:], cv[i2].t[:], ALU.mult, [cg[i2], cv[i2]], [gv[g0 + j]])
                          for dq in range(4):
                              s_, view = load_w(Wdn, 11, half * 11, [(dq * 256, 256)])
                              for j in range(2):
                                  dt_ = dq * 2 + j
                                  b = bank()
                                  for kc in range(11):
                                      mm(b.t[:, :], view[:, kc, j * 128:(j + 1) * 128], gv[kc].t[:], kc == 0, kc == 10, [s_, gv[kc]], [b])
                                  tt("dve", xt[dt_].t[:], xt[dt_].t[:], b.t[:, :], ALU.add, [xt[dt_], b], [xt[dt_]])
                      for c in range(8):
                          dma("pool", x_dst[c * 128:(c + 1) * 128, t0:t0 + TT], xt[c].t[:], c_o, [xt[c]], x_dst_r, nowaw=True)
                      P.barrier(COMPUTE, chans=[c_o, c_x])

        except _Stop:
            pass
        P.dead = False
        P.barrier(ALLQ, chans=P.chans)
        P.emit()
    return nc, P


_CACHE = {}


def run(inputs, T, depth, n_cores, stop=0):
    vec, sm = pack_params(inputs, depth)
    cbv, pcv = pack_consts()
    key = (T, depth, stop)
    if key not in _CACHE:
        _CACHE[key] = build(T, depth, stop)[0]
    nc = _CACHE[key]
    x = np.asarray(inputs["x"], np.float32)
    shared = {
        "w_in": np.ascontiguousarray(np.asarray(inputs["w_in"], np.float32)[:depth]),
        "w_out": np.ascontiguousarray(np.asarray(inputs["w_out"], np.float32)[:depth]),
        "w_up": np.ascontiguousarray(np.asarray(inputs["w_up"], np.float32)[:depth]),
        "w_down": np.ascontiguousarray(np.asarray(inputs["w_down"], np.float32)[:depth]),
        "vec": vec, "sm": sm, "cb": cbv, "pc": pcv,
    }
    in_maps = []
    for b in range(n_cores):
        m = dict(shared)
        m["xT"] = np.ascontiguousarray(x[b, :T].T)
        in_maps.append(m)
    res = run_bass_kernel_spmd(nc, in_maps, core_ids=list(range(n_cores)))
    if stop:
        return res.results[0]
    out = np.stack([np.ascontiguousarray(res.results[b]["yT"].T) for b in range(n_cores)], 0)
    return out.astype(np.float32)


def kernel(**inputs):
    inputs = {k: np.asarray(v) for k, v in inputs.items()}
    x = inputs["x"]
    return run(inputs, x.shape[1], 2, x.shape[0])
```

```python
from contextlib import ExitStack
import numpy as np
import ml_dtypes
import concourse.bass as bass
import concourse.mybir as mybir
from concourse.bass_utils import run_bass_kernel_spmd

F32 = mybir.dt.float32
BF16 = mybir.dt.bfloat16
AF = mybir.ActivationFunctionType
ALU = mybir.AluOpType
AX = mybir.AxisListType

D = 1024
NIN = 2816
DFF = 2816
NUP = 5632
TT = 512
CDEC = -float(np.exp(-0.5))

COMPUTE = ("pe", "act", "dve", "pool")
ALLQ = ("pe", "act", "dve", "pool", "sp")


class Reg:
    __slots__ = ("name", "w", "rs")

    def __init__(self, name):
        self.name = name
        self.w = None
        self.rs = {}


class Chan:
    def __init__(self, prog, name):
        self.key = "dma_" + name
        prog.semkeys.append(self.key)
        self.cnt = 0


class Prog:
    def __init__(self, nc):
        self.nc = nc
        self.q = {e: [] for e in ALLQ}
        self.cnt = {e: 0 for e in COMPUTE}
        self.known = {e: {} for e in ALLQ}
        self.semkeys = ["prog_" + e for e in COMPUTE]
        self.chans = []
        self.nins = 0
        self.nwait = 0
        self.dead = False

    def chan(self, name):
        c = Chan(self, name)
        self.chans.append(c)
        return c

    def op(self, eng, fn, reads=(), writes=(), chan=None, nowaw=False):
        if self.dead:
            return None
        need = {}

        def req(k, v):
            if need.get(k, 0) < v:
                need[k] = v

        me = None if chan is not None else eng
        for r in reads:
            if r.w is not None:
                req(r.w[0], r.w[1])
        for w in writes:
            if w.w is not None and (w.w[2] is None or w.w[2] != me) and not nowaw:
                req(w.w[0], w.w[1])
            for k, (v, e) in w.rs.items():
                if e is None or e != me:
                    req(k, v)
        kn = self.known[eng]
        for k, v in need.items():
            if kn.get(k, 0) < v:
                self.q[eng].append(("w", k, v))
                kn[k] = v
                self.nwait += 1
        if chan is not None:
            chan.cnt += 1
            ev = (chan.key, 16 * chan.cnt, None)
            self.q[eng].append(("i", fn, chan.key, 16))
        else:
            self.cnt[eng] += 1
            ev = ("prog_" + eng, self.cnt[eng], eng)
            self.q[eng].append(("i", fn, ev[0], 1))
        self.nins += 1
        for r in reads:
            old = r.rs.get(ev[0])
            if old is None or old[0] < ev[1]:
                r.rs[ev[0]] = (ev[1], ev[2])
        for w in writes:
            w.w = ev
            w.rs = {}
        return ev

    def wait(self, eng, key, val):
        if self.dead:
            return
        kn = self.known[eng]
        if val > 0 and kn.get(key, 0) < val:
            self.q[eng].append(("w", key, val))
            kn[key] = val
            self.nwait += 1

    def barrier(self, engs=COMPUTE, chans=()):
        for e in engs:
            for x in engs:
                if x != e and x in self.cnt:
                    self.wait(e, "prog_" + x, self.cnt[x])
            for c in chans:
                self.wait(e, c.key, 16 * c.cnt)

    def emit(self):
        nc = self.nc
        with ExitStack() as st:
            sems = {k: st.enter_context(nc.semaphore(k)) for k in self.semkeys}
            block = st.enter_context(nc.Block())

            def run(e, items):
                for it in items:
                    if it[0] == "w":
                        e.wait_ge(sems[it[1]], it[2])
                    else:
                        it[1](e).then_inc(sems[it[2]], it[3])

            @block.tensor
            def _(e):
                run(e, self.q["pe"])

            @block.scalar
            def _(e):
                run(e, self.q["act"])

            @block.vector
            def _(e):
                run(e, self.q["dve"])

            @block.gpsimd
            def _(e):
                run(e, self.q["pool"])

            @block.sync
            def _(e):
                run(e, self.q["sp"])


class Tl:
    __slots__ = ("t", "r", "name")

    def __init__(self, t, name):
        self.t = t
        self.name = name
        self.r = Reg(name)


def vec_cols(depth):
    spec = [("ln1_g", 8), ("ln2_g", 8), ("mu", 11), ("w0", 3), ("a0", 3), ("k_k", 3), ("k_a", 3), ("r_k", 3),
            ("lnx_w", 3), ("lnx_b", 3), ("v0", 3), ("q_gain", 1), ("k_gain", 1), ("pool_b", 2),
            ("pool_scale", 2), ("conv_w", 132), ("conv_b", 44)]
    off, o = {}, 0
    for n, c in spec:
        off[n] = o
        o += c
    return off, o


SM_W2A2, SM_G2, SM_V1, SM_V2, SM_PW, SM_N = 0, 384, 768, 864, 1248, 1504
CB_ID, CB_ONES, CB_BONES, CB_MASKL, CB_NTRI, CB_MASK4, CB_AMASK, CB_N = 0, 128, 256, 384, 512, 640, 1152, 3200


def pack_consts():
    c = np.zeros((128, CB_N), np.float32)
    i = np.arange(128)
    c[:, CB_ID:CB_ID + 128] = np.eye(128)
    c[:, CB_ONES:CB_ONES + 128] = 1.0
    c[:, CB_BONES:CB_BONES + 128] = (i[:, None] // 64 == i[None, :] // 64)
    c[:, CB_MASKL:CB_MASKL + 128] = (i[None, :] < i[:, None])
    c[:, CB_NTRI:CB_NTRI + 128] = -(i[:, None] >= i[None, :]).astype(np.float32)
    su = (i[:, None] < i[None, :]).astype(np.float32)
    iu = (i[:, None] <= i[None, :]).astype(np.float32)
    c[:, CB_MASK4:CB_MASK4 + 512] = np.concatenate([su, iu, su, iu], 1)
    q = np.arange(512)
    for d in range(4):
        c[:, CB_AMASK + 512 * d:CB_AMASK + 512 * (d + 1)] = (128 * d + i[:, None] < q[None, :])
    pc = np.ones((128, 2, 16), np.float32)
    wins = (2, 4, 8, 16)
    for ti in range(2):
        for e in range(2):
            w = wins[2 * ti + e]
            t = np.arange(16)
            pc[e * 64:(e + 1) * 64, ti, :] = (w / np.minimum(t + 1, w))[None, :]
    return c.astype(ml_dtypes.bfloat16), pc.reshape(128, 32)


def pack_params(inp, depth):
    off, W = vec_cols(depth)
    vec = np.zeros((128, depth * W), np.float32)
    sm = np.zeros((depth, 128, SM_N), np.float32)

    def put(l, name, v):
        v = np.asarray(v, np.float32).reshape(-1, 128).T
        vec[:, l * W + off[name]: l * W + off[name] + v.shape[1]] = v

    for l in range(depth):
        put(l, "ln1_g", inp["ln1_g"][l]); put(l, "ln2_g", inp["ln2_g"][l]); put(l, "mu", inp["mu_shift"][l])
        for n in ("w0", "a0", "k_k", "k_a", "lnx_w", "lnx_b"):
            put(l, n, inp[n][l])
        put(l, "r_k", inp["r_k"][l].reshape(-1))
        if l > 0:
            put(l, "v0", inp["v0"][l - 1])
        put(l, "q_gain", np.tile(inp["q_gain"][l], 2)); put(l, "k_gain", np.tile(inp["k_gain"][l], 2))
        put(l, "pool_b", inp["pool_b"][l].reshape(-1)); put(l, "pool_scale", inp["pool_scale"][l])
        put(l, "conv_w", inp["conv_w"][l].reshape(-1)); put(l, "conv_b", inp["conv_b"][l])
        sm[l, 0:64, SM_W2A2:SM_W2A2 + 384] = inp["w2"][l]
        sm[l, 64:128, SM_W2A2:SM_W2A2 + 384] = inp["a2"][l]
        sm[l, :, SM_G2:SM_G2 + 384] = inp["g2"][l]
        if l > 0:
            v1 = np.asarray(inp["v1"][l - 1]).reshape(3, 128, 32).transpose(1, 0, 2).reshape(128, 96)
            sm[l, :, SM_V1:SM_V1 + 96] = v1
            sm[l, 0:32, SM_V2:SM_V2 + 384] = inp["v2"][l - 1]
        for ti in range(2):
            for e in range(2):
                g = 2 * ti + e
                sm[l, e * 64:(e + 1) * 64, SM_PW + ti * 128 + e * 64: SM_PW + ti * 128 + (e + 1) * 64] = inp["pool_w"][l, g]
    return vec, sm


class _Stop(Exception):
    pass


def build(T, depth, stop=0):
    nc = bass.Bass("TRN2", target_bir_lowering=False)
    NT = T // TT
    voff, VW = vec_cols(depth)

    def din(name, shape, dt=F32):
        return nc.dram_tensor(name, shape, dt, kind="ExternalInput").ap()

    xT_d = din("xT", [D, T])
    w_in_d = din("w_in", [depth, D, NIN]); w_out_d = din("w_out", [depth, D, D])
    w_up_d = din("w_up", [depth, D, NUP]); w_down_d = din("w_down", [depth, DFF, D])
    vec_d = din("vec", [128, depth * VW]); sm_d = din("sm", [depth, 128, SM_N])
    cb_d = din("cb", [128, CB_N], BF16); pc_d = din("pc", [128, 32])
    yT_d = nc.dram_tensor("yT", [D, T], F32, kind="ExternalOutput").ap()
    if stop:
        dbgf = nc.dram_tensor("dbgf", [16, 128, 512], F32, kind="ExternalOutput").ap()
        dbgb = nc.dram_tensor("dbgb", [16, 128, 512], BF16, kind="ExternalOutput").ap()
    wbi = nc.dram_tensor("wbi", [depth, D, NIN], BF16, kind="Internal").ap()
    wbo = nc.dram_tensor("wbo", [depth, D, D], BF16, kind="Internal").ap()
    wbu = nc.dram_tensor("wbu", [depth, D, NUP], BF16, kind="Internal").ap()
    wbd = nc.dram_tensor("wbd", [depth, DFF, D], BF16, kind="Internal").ap()
    x1_d = nc.dram_tensor("x1s", [D, T], F32, kind="Internal").ap()
    vf_d = nc.dram_tensor("vfs", [384, T], F32, kind="Internal").ap()
    r_wb = Reg("wbscratch"); r_x1 = Reg("x1s"); r_vf = Reg("vfs")

    P = Prog(nc)
    rr = {"i": 0}

    def R(ts):
        return [t.r if isinstance(t, Tl) else t for t in ts]

    def mm(out, lhsT, rhs, start, stop, rd, wr):
        P.op("pe", lambda e: e.matmul(out, lhsT=lhsT, rhs=rhs, start=start, stop=stop), R(rd), R(wr))

    def tr(out, in_, ident, rd, wr):
        P.op("pe", lambda e: e.transpose(out, in_, ident), R(rd), R(wr))

    def act(out, in_, func, rd, wr, bias=None, scale=None):
        kw = {}
        if bias is not None:
            kw["bias"] = bias
        if scale is not None:
            kw["scale"] = scale
        P.op("act", lambda e: e.activation(out=out, in_=in_, func=func, **kw), R(rd), R(wr))

    def tt(eng, out, a, b, op, rd, wr):
        P.op(eng, lambda e: e.tensor_tensor(out=out, in0=a, in1=b, op=op), R(rd), R(wr))

    def stt(eng, out, in0, scalar, in1, op0, op1, rd, wr):
        P.op(eng, lambda e: e.scalar_tensor_tensor(out=out, in0=in0, scalar=scalar, in1=in1, op0=op0, op1=op1), R(rd), R(wr))

    def ts(eng, out, in0, s1, s2, op0, op1, rd, wr):
        if s2 is None:
            P.op(eng, lambda e: e.tensor_scalar(out=out, in0=in0, scalar1=s1, scalar2=None, op0=op0), R(rd), R(wr))
        else:
            P.op(eng, lambda e: e.tensor_scalar(out=out, in0=in0, scalar1=s1, scalar2=s2, op0=op0, op1=op1), R(rd), R(wr))

    def cp(eng, out, in_, rd, wr):
        if eng == "act":
            P.op("act", lambda e: e.activation(out=out, in_=in_, func=AF.Copy), R(rd), R(wr))
        else:
            P.op(eng, lambda e: e.tensor_copy(out=out, in_=in_), R(rd), R(wr))

    def mset(eng, out, val, wr):
        P.op(eng, lambda e: e.memset(out, val), [], R(wr))

    def dma(q, out, in_, chan, rd, wr, nowaw=False):
        return P.op(q, lambda e: e.dma_start(out=out, in_=in_), R(rd), R(wr), chan=chan, nowaw=nowaw)

    dch = {}

    def dump(tl, ap, idx, bf=False, w=512, np_=128):
        if not stop:
            return
        if "c" not in dch:
            dch["c"] = P.chan("dbg")
        dst = (dbgb if bf else dbgf)[idx, 0:np_, 0:w]
        dma("pool", dst, ap, dch["c"], [tl], [])

    def stage(k):
        if stop == k:
            P.dead = True

    def alt():
        rr["i"] += 1
        return "act" if rr["i"] % 2 else "dve"

    with ExitStack() as G:
        uid = {"i": 0}

        def sb(st, name, shape, dt=F32):
            uid["i"] += 1
            nm = f"s{uid['i']}_{name}"
            return Tl(st.enter_context(nc.sbuf_tensor(nm, shape, dt)), nm)

        banks = [Tl(G.enter_context(nc.psum_tensor(f"pb{i}", [128, 512], F32)), f"pb{i}") for i in range(6)]
        obank = Tl(G.enter_context(nc.psum_tensor("pob", [128, 512], F32)), "pob")
        tbank = Tl(G.enter_context(nc.psum_tensor("ptb", [128, 1024], BF16)), "ptb")
        bk = {"i": 0}

        def bank():
            bk["i"] += 1
            return banks[bk["i"] % 6]

        cb = sb(G, "cb", [128, CB_N], BF16)
        pc = sb(G, "pc", [128, 32])
        vec = sb(G, "vec", [128, depth * VW])
        smb = sb(G, "smb", [128, SM_N], BF16)
        qg8 = sb(G, "qg8", [128, 1])
        ones_f = sb(G, "ones_f", [128, 128])
        wsl = [sb(G, f"wsl{i}", [128, 4096], BF16) for i in range(3)]
        wch = [P.chan(f"w{i}") for i in range(3)]
        kc_ = [[sb(G, f"kc{p}_{t}", [128, 512], BF16) for t in range(NT)] for p in range(3)]
        vc_ = [sb(G, f"vc{m}", [128, 384], BF16) for m in range(NT * 4)]
        hT = [sb(G, f"hT{c}", [128, 512], BF16) for c in range(8)]
        mixT = [sb(G, f"mixT{c}", [128, 512], BF16) for c in range(8)]
        ST = sb(G, "ST", [128, 384])
        STb = [sb(G, f"STb{i}", [128, 384], BF16) for i in range(2)]
        rhalo = sb(G, "rhalo", [128, 11])
        uhalo = [sb(G, f"uhalo{i}", [128, 16]) for i in range(2)]
        chalo = sb(G, "chalo", [128, 44, 2])
        c_misc = P.chan("misc")
        c_x = P.chan("x"); c_o = P.chan("o"); c_vf = P.chan("vf"); c_vfl = [P.chan("vfl0"), P.chan("vfl1")]

        ident = cb.t[:, CB_ID:CB_ID + 128]
        ones_b = cb.t[:, CB_ONES:CB_ONES + 128]
        bones_b = cb.t[:, CB_BONES:CB_BONES + 128]
        maskL = cb.t[:, CB_MASKL:CB_MASKL + 128]
        ntri = cb.t[:, CB_NTRI:CB_NTRI + 128]
        mask4 = cb.t[:, CB_MASK4:CB_MASK4 + 512]

        def amask(d):
            return cb.t[:, CB_AMASK + 512 * d: CB_AMASK + 512 * (d + 1)]

        dma("sp", cb.t[:], cb_d, c_misc, [], [cb])
        dma("sp", pc.t[:], pc_d, c_misc, [], [pc])
        dma("sp", vec.t[:], vec_d, c_misc, [], [vec])
        mset("dve", ones_f.t[:], 1.0, [ones_f])

        def vcol(l, name, j=0):
            o = l * VW + voff[name] + j
            return vec.t[:, o:o + 1]

        with ExitStack() as S:
            stg = [sb(S, f"stg{i}", [128, 2048]) for i in range(4)]
            stb = [sb(S, f"stb{i}", [128, 2048], BF16) for i in range(4)]
            cl = [P.chan(f"pl{i}") for i in range(4)]
            cs = [P.chan(f"ps{i}") for i in range(4)]
            n = 0
            pend = []
            for l in range(depth):
                for (src, dst, K, N) in ((w_in_d, wbi, D, NIN), (w_out_d, wbo, D, D), (w_up_d, wbu, D, NUP), (w_down_d, wbd, DFF, D)):
                    for r0 in range(0, K, 128):
                        for c0 in range(0, N, 2048):
                            w = min(2048, N - c0)
                            i = n % 4
                            n += 1
                            dma("sp", stg[i].t[:, 0:w], src[l, r0:r0 + 128, c0:c0 + w], cl[i], [], [stg[i]])
                            cp(("act", "dve", "pool")[n % 3], stb[i].t[:, 0:w], stg[i].t[:, 0:w], [stg[i]], [stb[i]])
                            pend.append((dst[l, r0:r0 + 128, c0:c0 + w], i, w))
                            if len(pend) > 2:
                                d_, i_, w_ = pend.pop(0)
                                dma("sp", d_, stb[i_].t[:, 0:w_], cs[i_], [stb[i_]], [r_wb], nowaw=True)
            for (d_, i_, w_) in pend:
                dma("sp", d_, stb[i_].t[:, 0:w_], cs[i_], [stb[i_]], [r_wb], nowaw=True)
            P.barrier(ALLQ, chans=cl + cs + [c_misc])

        wn = {"i": 0}

        def load_w(W2d, nk, k0, ranges):
            i = wn["i"] % 3
            wn["i"] += 1
            s = wsl[i]
            wc = sum(n_ for _, n_ in ranges)
            view = s.t[:, 0:nk * wc].rearrange("p (k c) -> p k c", c=wc)
            src = W2d[k0 * 128:(k0 + nk) * 128, :].rearrange("(k p) c -> p k c", p=128)
            o = 0
            first = True
            for (c0, n_) in ranges:
                dma("sp", view[:, :, o:o + n_], src[:, :, c0:c0 + n_], wch[i], [r_wb], [s], nowaw=not first)
                first = False
                o += n_
            return s, view

        try:
            stage(1)
            for l in range(depth):
              x_src = xT_d if l == 0 else x1_d
              x_dst = yT_d if l == depth - 1 else x1_d
              x_src_r = [] if l == 0 else [r_x1]
              x_dst_r = [] if l == depth - 1 else [r_x1]
              with ExitStack() as S:
                  smf = sb(S, "smf", [128, SM_N])
                  dma("pool", smf.t[:], sm_d[l], c_misc, [], [smf])
                  cp("dve", smb.t[:], smf.t[:], [smf], [smb])
                  P.op("act", lambda e, l=l: e.mul(out=qg8.t[:], in_=vcol(l, "q_gain"), mul=0.125), R([vec]), R([qg8]))
                  mset("dve", ST.t[:], 0.0, [ST]); mset("dve", STb[0].t[:], 0.0, [STb[0]])
                  mset("dve", rhalo.t[:], 0.0, [rhalo]); mset("dve", chalo.t[:], 0.0, [chalo])
                  for i in range(2):
                      mset("dve", uhalo[i].t[:], 0.0, [uhalo[i]])
                  P.barrier(COMPUTE, chans=[c_misc])
              stcur = 0
              w2a2 = smb.t[:, SM_W2A2:SM_W2A2 + 384]; g2b = smb.t[:, SM_G2:SM_G2 + 384]
              Win, Wout, Wup, Wdn = wbi[l], wbo[l], wbu[l], wbd[l]

              for it in range(NT):
                  t0 = it * TT

                  def rmsnorm(S, xt, gname):
                      sq = [sb(S, f"sq{i}", [128, 512], BF16) for i in range(2)]
                      lnv = sb(S, "lnv", [128, 512]); rstd = sb(S, "rstd", [128, 512])
                      b = bank()
                      for c in range(8):
                          act(sq[c % 2].t[:], xt[c].t[:], AF.Square, [xt[c]], [sq[c % 2]])
                          mm(b.t[:, :], ones_b, sq[c % 2].t[:], c == 0, c == 7, [cb, sq[c % 2]], [b])
                      act(lnv.t[:], b.t[:, :], AF.Ln, [b], [lnv], bias=1e-6, scale=1.0 / D)
                      act(rstd.t[:], lnv.t[:], AF.Exp, [lnv], [rstd], scale=-0.5)
                      for c in range(8):
                          stt("dve", hT[c].t[:], xt[c].t[:], vcol(l, gname, c), rstd.t[:], ALU.mult, ALU.mult,
                              [xt[c], vec, rstd], [hT[c]])

                  def win_tiles(view, j0, cts):
                      out = []
                      for i, ct in enumerate(cts):
                          b = bank()
                          for kc in range(8):
                              mm(b.t[:, :], view[:, kc, (j0 + i) * 128:(j0 + i + 1) * 128], hT[kc].t[:], kc == 0, kc == 7, [view_s[0], hT[kc]], [b])
                          out.append(b)
                      return out

                  view_s = [None]

                  SA = ExitStack()
                  with ExitStack() as S:
                      S = SA
                      xt = [sb(S, f"xa{c}", [128, 512]) for c in range(8)]
                      for c in range(8):
                          ev_ = dma("pool", xt[c].t[:], x_src[c * 128:(c + 1) * 128, t0:t0 + TT], c_x, x_src_r, [xt[c]])
                          if c == 7 and ev_ is not None:
                              for c2 in range(8):
                                  xt[c2].r.w = ev_
                      rmsnorm(S, xt, "ln1_g")

                  dump(hT[0], hT[0].t[:], 0, bf=True); dump(hT[7], hT[7].t[:], 1, bf=True)
                  stage(2)
                  with ExitStack() as S:
                      S = SA
                      s_, view = load_w(Win, 8, 0, [(20 * 128, 256)])
                      view_s[0] = s_
                      bs = win_tiles(view, 0, [20, 21])
                      ur = [sb(S, f"ur{i}", [128, 528]) for i in range(2)]
                      sA = sb(S, "sA", [128, 528]); sB = sb(S, "sB", [128, 528])
                      pl = sb(S, "pl", [128, 512], BF16)
                      for i in range(2):
                          u = ur[i]
                          cp("act", u.t[:, 16:528], bs[i].t[:, :], [bs[i]], [u])
                          cp("dve", u.t[:, 0:16], uhalo[i].t[:], [uhalo[i]], [u])
                          cp("dve", uhalo[i].t[:], u.t[:, 512:528], [u], [uhalo[i]])
                          tt("dve", sA.t[:, 1:528], u.t[:, 1:528], u.t[:, 0:527], ALU.add, [u], [sA])
                          tt("dve", sB.t[:, 3:528], sA.t[:, 3:528], sA.t[:, 1:526], ALU.add, [sA], [sB])
                          if i == 0:
                              lo, hi, wl, wh = sA, sB, 2.0, 4.0
                          else:
                              tt("dve", sA.t[:, 7:528], sB.t[:, 7:528], sB.t[:, 3:524], ALU.add, [sB], [sA])
                              tt("dve", sB.t[:, 15:528], sA.t[:, 15:528], sA.t[:, 7:520], ALU.add, [sA], [sB])
                              lo, hi, wl, wh = sA, sB, 8.0, 16.0
                          if it == 0:
                              tt("dve", lo.t[0:64, 16:32], lo.t[0:64, 16:32], pc.t[0:64, i * 16:(i + 1) * 16], ALU.mult, [lo, pc], [lo])
                              tt("dve", hi.t[64:128, 16:32], hi.t[64:128, 16:32], pc.t[64:128, i * 16:(i + 1) * 16], ALU.mult, [hi, pc], [hi])
                          stt("dve", pl.t[0:64, :], lo.t[0:64, 16:528], 1.0 / wl, u.t[0:64, 16:528], ALU.mult, ALU.subtract, [lo, u], [pl])
                          stt("dve", pl.t[64:128, :], hi.t[64:128, 16:528], 1.0 / wh, u.t[64:128, 16:528], ALU.mult, ALU.subtract, [hi, u], [pl])
                          b = bank()
                          mm(b.t[:, :], smb.t[:, SM_PW + i * 128: SM_PW + (i + 1) * 128], pl.t[:], True, True, [smb, pl], [b])
                          ts("dve", mixT[6 + i].t[:], b.t[:, :], vcol(l, "pool_b", i), vcol(l, "pool_scale", i), ALU.add, ALU.mult,
                             [b, vec], [mixT[6 + i]])

                  dump(mixT[6], mixT[6].t[:], 2, bf=True); dump(mixT[7], mixT[7].t[:], 3, bf=True)
                  stage(3)
                  with ExitStack() as S:
                      S = SA
                      qT = [sb(S, f"qT{p}", [128, 512], BF16) for p in range(3)]
                      vT = [sb(S, f"vT{p}", [128, 512], BF16) for p in range(3)]
                      sqq = [sb(S, f"sqq{i}", [128, 512], BF16) for i in range(2)]
                      lnq2 = [sb(S, f"lnq{i}", [128, 512]) for i in range(2)]; rsq2 = [sb(S, f"rsq{i}", [128, 512]) for i in range(2)]
                      groups = [[(11 * 128, 512)], [(15 * 128, 512)], [(19 * 128, 128)]]
                      cts = [[11, 12, 13, 14], [15, 16, 17, 18], [19]]
                      nq = 0
                      for gi in range(3):
                          s_, view = load_w(Win, 8, 0, groups[gi])
                          view_s[0] = s_
                          bs = win_tiles(view, 0, cts[gi])
                          for b, ct in zip(bs, cts[gi]):
                              if ct < 17:
                                  isq = ct < 14
                                  p = ct - 11 if isq else ct - 14
                                  sq_ = sqq[nq % 2]; lnq = lnq2[nq % 2]; rsq = rsq2[nq % 2]; nq += 1
                                  act(sq_.t[:], b.t[:, :], AF.Square, [b], [sq_])
                                  b2 = bank()
                                  mm(b2.t[:, :], bones_b, sq_.t[:], True, True, [cb, sq_], [b2])
                                  act(lnq.t[:], b2.t[:, :], AF.Ln, [b2], [lnq], bias=1e-6, scale=1.0 / 64)
                                  act(rsq.t[:], lnq.t[:], AF.Exp, [lnq], [rsq], scale=-0.5)
                                  dst = qT[p] if isq else kc_[p][it]
                                  gcol = qg8.t[:, 0:1] if isq else vcol(l, "k_gain")
                                  stt("dve", dst.t[:], b.t[:, :], gcol, rsq.t[:], ALU.mult, ALU.mult, [b, rsq, qg8, vec], [dst])
                              else:
                                  cp("act", vT[ct - 17].t[:], b.t[:, :], [b], [vT[ct - 17]])
                      for s in range(4):
                          h = s % 2
                          for p in range(3):
                              tr(tbank.t[:, h * 512 + p * 128: h * 512 + (p + 1) * 128], vT[p].t[:, s * 128:(s + 1) * 128], ident, [vT[p], cb], [tbank])
                          cp(alt(), vc_[it * 4 + s].t[:], tbank.t[:, h * 512: h * 512 + 384], [tbank], [vc_[it * 4 + s]])
                      Eb = [sb(S, f"Eb{i}", [128, 512]) for i in range(4)]
                      SPb = [sb(S, f"SPb{i}", [128, 512], BF16) for i in range(4)]
                      Rbc2 = [sb(S, f"Rbc{i}", [128, 512]) for i in range(2)]
                      argb = [sb(S, f"argb{i}", [128, 512]) for i in range(4)]
                      Ab = [sb(S, f"Ab{i}", [128, 512], BF16) for i in range(4)]
                      last = 4 * it + 3
                      units = [(p, m, e_) for p in range(3) for m in range(last, -1, -1) for e_ in range(2)]
                      NU = len(units)
                      ust = [dict() for _ in range(NU)]

                      def c0of(m):
                          d = m - 4 * it
                          return 128 * d if d > 0 else 0

                      def s1(u):
                          p, m, e_ = units[u]; pb = 64 * e_; c0 = c0of(m)
                          kt = kc_[p][m // 4]
                          zb = bank(); ust[u]["zb"] = zb
                          mm(zb.t[:, c0:512], kt.t[pb:pb + 64, (m % 4) * 128:(m % 4 + 1) * 128], qT[p].t[pb:pb + 64, c0:512], True, True, [kt, qT[p]], [zb])

                      def s2(u):
                          p, m, e_ = units[u]; d = m - 4 * it; c0 = c0of(m)
                          zb = ust[u]["zb"]; E, SP = Eb[u % 4], SPb[u % 4]
                          act(E.t[:, c0:512], zb.t[:, c0:512], AF.Exp, [zb], [E])
                          act(SP.t[:, c0:512], E.t[:, c0:512], AF.Ln, [E], [SP], bias=1.0)
                          if d >= 0:
                              tt("pool", SP.t[:, c0:512], SP.t[:, c0:512], amask(d)[:, c0:512], ALU.mult, [SP, cb], [SP])

                      def s3(u):
                          p, m, e_ = units[u]; c0 = c0of(m)
                          zb = ust[u]["zb"]; SP = SPb[u % 4]
                          mm(zb.t[:, c0:512], ntri, SP.t[:, c0:512], False, True, [cb, SP], [zb])
                          if m > 0:
                              rb = bank(); ust[u]["rb"] = rb
                              mm(rb.t[:, c0:512], ones_b, SP.t[:, c0:512], True, True, [cb, SP], [rb])

                      def s4(u):
                          p, m, e_ = units[u]; c0 = c0of(m)
                          zb = ust[u]["zb"]; ar = argb[u % 4]; Rbc = Rbc2[e_]
                          if m != last:
                              tt("dve", ar.t[:, c0:512], zb.t[:, c0:512], Rbc.t[:, c0:512], ALU.subtract, [zb, Rbc], [ar])
                          if m > 0:
                              rb = ust[u]["rb"]
                              if m == last:
                                  if c0 > 0:
                                      mset("pool", Rbc.t[:, 0:c0], 0.0, [Rbc])
                                  cp("dve", Rbc.t[:, c0:512], rb.t[:, c0:512], [rb], [Rbc])
                              else:
                                  tt("dve", Rbc.t[:, c0:512], Rbc.t[:, c0:512], rb.t[:, c0:512], ALU.add, [Rbc, rb], [Rbc])

                      def s5(u):
                          p, m, e_ = units[u]; d = m - 4 * it; c0 = c0of(m)
                          zb = ust[u]["zb"]; ar = argb[u % 4]; A = Ab[u % 4]
                          if m == last:
                              act(A.t[:, c0:512], zb.t[:, c0:512], AF.Exp, [zb], [A])
                          else:
                              act(A.t[:, c0:512], ar.t[:, c0:512], AF.Exp, [ar], [A])
                          if d >= 0:
                              tt("pool", A.t[:, c0:512], A.t[:, c0:512], amask(d)[:, c0:512], ALU.mult, [A, cb], [A])

                      def s6(u):
                          p, m, e_ = units[u]; pb = 64 * e_; h = 2 * p + e_; d = m - 4 * it; c0 = c0of(m)
                          A = Ab[u % 4]
                          vv_ = vc_[m].t[:, h * 64:(h + 1) * 64]
                          mm(obank.t[pb:pb + 64, c0:512], vv_, A.t[:, c0:512], m == last, m == 0, [vc_[m], A], [obank])
                          if m == 0 and e_ == 1:
                              cp("act", mixT[3 + p].t[:], obank.t[:, :], [obank], [mixT[3 + p]])

                      for t in range(NU + 4):
                          if t < NU:
                              s1(t); s2(t)
                          if 0 <= t - 1 < NU:
                              s3(t - 1); s4(t - 1)
                          if 0 <= t - 2 < NU:
                              s5(t - 2)
                          if 0 <= t - 4 < NU:
                              s6(t - 4)
                      P.barrier(COMPUTE, chans=[c_x])
                  SA.close()

                  dump(mixT[3], mixT[3].t[:], 4, bf=True); dump(kc_[0][it], kc_[0][it].t[:], 5, bf=True); dump(vc_[it * 4], vc_[it * 4].t[:], 6, bf=True, w=384); dump(mixT[5], mixT[5].t[:], 7, bf=True)
                  stage(4)
                  with ExitStack() as S:
                      lor = sb(S, "lor", [128, 512], BF16); sgx = sb(S, "sgx", [128, 512], BF16)
                      raw = [sb(S, f"raw{i}", [128, 513]) for i in range(2)]
                      dtmp = sb(S, "dtmp", [128, 512])
                      ART = [sb(S, f"ART{p}", [128, 4, 2, 128], BF16) for p in range(3)]
                      BT = [sb(S, f"BT{p}", [128, 512], BF16) for p in range(3)]
                      KT = [sb(S, f"KT{p}", [128, 512], BF16) for p in range(3)]
                      GC = [sb(S, f"GC{p}", [128, 4]) for p in range(3)]
                      gT = [sb(S, f"gT{p}", [128, 512], BF16) for p in range(3)]
                      bonT = [sb(S, f"bonT{p}", [128, 512], BF16) for p in range(3)]
                      TOK = [sb(S, f"TOK{c}", [128, 4, 384], BF16) for c in range(4)]
                      Lp = sb(S, "Lp", [128, 512]); a_t = sb(S, "a_t", [128, 512]); kkn = sb(S, "kkn", [128, 512])
                      bv = sb(S, "bv", [128, 512]); et = sb(S, "et", [128, 512]); t2 = sb(S, "t2", [128, 512])
                      BHT = sb(S, "BHT", [128, 512], BF16); KHT = sb(S, "KHT", [128, 512], BF16)
                      vb = [sb(S, f"vb{p}", [128, 512], BF16) for p in range(3)]
                      sqk = sb(S, "sqk", [128, 512], BF16); lob = sb(S, "lob", [32, 512], BF16)
                      cLC = sb(S, "cLC", [128, 4])

                      def evac_shift(b, ct, dst, dtmp=dtmp):
                          cp("act", dst.t[:, 1:513], b.t[:, :], [b], [dst])
                          cp("dve", dst.t[:, 0:1], rhalo.t[:, ct:ct + 1], [rhalo], [dst])
                          tt("dve", dtmp.t[:], dst.t[:, 0:512], dst.t[:, 1:513], ALU.subtract, [dst], [dtmp])
                          cp("dve", rhalo.t[:, ct:ct + 1], dst.t[:, 512:513], [dst], [rhalo])
                          stt("dve", dst.t[:, 1:513], dtmp.t[:], vcol(l, "mu", ct), dst.t[:, 1:513], ALU.mult, ALU.add, [dtmp, vec, dst], [dst])

                      s_, view = load_w(Win, 8, 0, [(9 * 128, 256)])
                      view_s[0] = s_
                      bs = win_tiles(view, 0, [9, 10])
                      evac_shift(bs[0], 9, raw[0])
                      act(lor.t[0:64, :], raw[0].t[0:64, 1:513], AF.Tanh, [raw[0]], [lor])
                      cp("dve", lor.t[64:128, :], raw[0].t[64:128, 1:513], [raw[0]], [lor])
                      evac_shift(bs[1], 10, raw[1])
                      act(sgx.t[:], raw[1].t[:, 1:513], AF.Sigmoid, [raw[1]], [sgx])
                      stage(41)
                      vr = [sb(S, f"vr{p}", [128, 513]) for p in range(3)]
                      s_, view = load_w(Win, 8, 0, [(6 * 128, 384)])
                      view_s[0] = s_
                      bs = win_tiles(view, 0, [6, 7, 8])
                      for p in range(3):
                          evac_shift(bs[p], 6 + p, vr[p])
                      if l > 0:
                          for p in range(3):
                              cp("pool", vb[p].t[:], vr[p].t[:, 1:513], [vr[p]], [vb[p]])
                          b = bank()
                          for pp in range(3):
                              mm(b.t[0:32, :], smb.t[:, SM_V1 + pp * 32: SM_V1 + (pp + 1) * 32], vb[pp].t[:], pp == 0, pp == 2, [smb, vb[pp]], [b])
                          cp("act", lob.t[:], b.t[0:32, :], [b], [lob])
                      rawS = [raw, [sb(S, f"rawb{i}", [128, 513]) for i in range(2)]]
                      dtmpS = [dtmp, sb(S, "dtmpb", [128, 512])]
                      t2S = [t2, sb(S, "t2b", [128, 512])]; etS = [et, sb(S, "etb", [128, 512])]
                      LpS = [Lp, sb(S, "Lpb", [128, 512])]; a_tS = [a_t, sb(S, "a_tb", [128, 512])]

                      def prep_early(p):
                          s2_ = p % 2
                          raw_, dtmp_, t2, et, Lp, a_t = rawS[s2_], dtmpS[s2_], t2S[s2_], etS[s2_], LpS[s2_], a_tS[s2_]
                          pc0 = p * 128
                          s_, view = load_w(Win, 8, 0, [(p * 128, 128), ((3 + p) * 128, 128)])
                          view_s[0] = s_
                          bs = win_tiles(view, 0, [p, 3 + p])
                          evac_shift(bs[0], p, raw_[0], dtmp_)
                          evac_shift(bs[1], 3 + p, raw_[1], dtmp_)
                          rw = [raw_[0], raw_[1], vr[p]]
                          r_, k_, v_ = rw[0], rw[1], rw[2]
                          rT, kTr, vTr = r_.t[:, 1:513], k_.t[:, 1:513], v_.t[:, 1:513]
                          if l == 0:
                              dma("pool", vf_d[pc0:pc0 + 128, t0:t0 + TT], vTr, c_vf, [v_], [r_vf], nowaw=True)
                          else:
                              b = bank()
                              mm(b.t[:, :], smb.t[0:32, SM_V2 + pc0: SM_V2 + pc0 + 128], lob.t[:], True, True, [smb, lob], [b])
                              act(et.t[:], b.t[:, :], AF.Sigmoid, [b], [et], bias=vcol(l, "v0", p))
                              dma("pool", t2.t[:], vf_d[pc0:pc0 + 128, t0:t0 + TT], c_vfl[s2_], [r_vf], [t2])
                              tt("dve", t2.t[:], t2.t[:], vTr, ALU.subtract, [t2, v_], [t2])
                              tt("dve", t2.t[:], t2.t[:], et.t[:], ALU.mult, [t2, et], [t2])
                              tt("dve", vTr, vTr, t2.t[:], ALU.add, [v_, t2], [v_])
                          cp("pool", vb[p].t[:], vTr, [v_], [vb[p]])
                          b = bank()
                          mm(b.t[:, :], w2a2[0:64, pc0:pc0 + 128], lor.t[0:64, :], True, True, [smb, lor], [b])
                          act(et.t[:], b.t[:, :], AF.Sigmoid, [b], [et], bias=vcol(l, "w0", p))
                          for c in range(4):
                              P.op("dve", (lambda o_, d0_, d1_: lambda e: e.tensor_tensor_scan(out=o_, data0=d0_, data1=d1_, initial=0.0, op0=ALU.mult, op1=ALU.add))(
                              Lp.t[:, c * 128:(c + 1) * 128], ones_f.t[:], et.t[:, c * 128:(c + 1) * 128]), R([ones_f, et]), R([Lp]))
                          b = bank()
                          mm(b.t[:, :], w2a2[64:128, pc0:pc0 + 128], lor.t[64:128, :], True, True, [smb, lor], [b])
                          act(a_t.t[:], b.t[:, :], AF.Sigmoid, [b], [a_t], bias=vcol(l, "a0", p))
                          b = bank()
                          mm(b.t[:, :], g2b[:, pc0:pc0 + 128], sgx.t[:], True, True, [smb, sgx], [b])
                          cp("act", gT[p].t[:], b.t[:, :], [b], [gT[p]])

                      def prep_late(p):
                          s2_ = p % 2
                          raw_, dtmp_, t2, et, Lp, a_t = rawS[s2_], dtmpS[s2_], t2S[s2_], etS[s2_], LpS[s2_], a_tS[s2_]
                          pc0 = p * 128
                          r_, k_, v_ = raw_[0], raw_[1], vr[p]
                          rT, kTr, vTr = r_.t[:, 1:513], k_.t[:, 1:513], v_.t[:, 1:513]
                          act(sqk.t[:], kTr, AF.Square, [k_, vec], [sqk], scale=vcol(l, "k_k", p))
                          b = bank()
                          mm(b.t[:, :], bones_b, sqk.t[:], True, True, [cb, sqk], [b])
                          act(t2.t[:], b.t[:, :], AF.Ln, [b], [t2], bias=1e-24)
                          act(t2.t[:], t2.t[:], AF.Exp, [t2], [t2], scale=-0.5)
                          stt("dve", kkn.t[:], kTr, vcol(l, "k_k", p), t2.t[:], ALU.mult, ALU.mult, [k_, vec, t2], [kkn])
                          tt("dve", bv.t[:], kkn.t[:], a_t.t[:], ALU.mult, [kkn, a_t], [bv])
                          ts("dve", t2.t[:], a_t.t[:], -1.0, vcol(l, "k_a", p), ALU.add, ALU.mult, [a_t, vec], [t2])
                          stt("dve", kTr, t2.t[:], 1.0, kTr, ALU.add, ALU.mult, [t2, k_], [k_])
                          stt("dve", sqk.t[:], rT, vcol(l, "r_k", p), kTr, ALU.mult, ALU.mult, [r_, k_, vec], [sqk])
                          b = bank()
                          mm(b.t[:, :], bones_b, sqk.t[:], True, True, [cb, sqk], [b])
                          tt("dve", bonT[p].t[:], b.t[:, :], vTr, ALU.mult, [b, v_], [bonT[p]])
                          A4 = ART[p].t[:, :, 0, :]; R4 = ART[p].t[:, :, 1, :]
                          v4 = lambda ap: ap.rearrange("p (c t) -> p c t", t=128)
                          tt("dve", t2.t[:], Lp.t[:], et.t[:], ALU.subtract, [Lp, et], [t2])
                          act(t2.t[:], t2.t[:], AF.Exp, [t2], [t2], scale=CDEC)
                          stt("dve", A4, v4(kkn.t[:]), -1.0, v4(t2.t[:]), ALU.mult, ALU.mult, [kkn, t2], [ART[p]])
                          act(t2.t[:], Lp.t[:], AF.Exp, [Lp], [t2], scale=CDEC)
                          tt("dve", R4, v4(rT), v4(t2.t[:]), ALU.mult, [r_, t2], [ART[p]])
                          for c in range(4):
                              cp("dve", GC[p].t[:, c:c + 1], t2.t[:, c * 128 + 127: c * 128 + 128], [t2], [GC[p]])
                          act(t2.t[:], Lp.t[:], AF.Exp, [Lp], [t2], scale=-CDEC)
                          tt("dve", BT[p].t[:], bv.t[:], t2.t[:], ALU.mult, [bv, t2], [BT[p]])
                          tt("dve", KT[p].t[:], kTr, t2.t[:], ALU.mult, [k_, t2], [KT[p]])
                          for c in range(4):
                              ts("dve", cLC.t[:, c:c + 1], Lp.t[:, c * 128 + 127: c * 128 + 128], CDEC, None, ALU.mult, None, [Lp], [cLC])
                          for c in range(4):
                              act(t2.t[:, c * 128:(c + 1) * 128], Lp.t[:, c * 128:(c + 1) * 128], AF.Exp, [Lp, cLC], [t2],
                                  scale=-CDEC, bias=cLC.t[:, c:c + 1])
                          tt("dve", BHT.t[:], bv.t[:], t2.t[:], ALU.mult, [bv, t2], [BHT])
                          tt("dve", KHT.t[:], kTr, t2.t[:], ALU.mult, [k_, t2], [KHT])
                          for c in range(4):
                              h = c % 2
                              srcs = [(ART[p], ART[p].t[:, c, 0, :]), (BHT, BHT.t[:, c * 128:(c + 1) * 128]),
                                      (KHT, KHT.t[:, c * 128:(c + 1) * 128]), (vb[p], vb[p].t[:, c * 128:(c + 1) * 128])]
                              for kd, (tl_, ap_) in enumerate(srcs):
                                  tr(tbank.t[:, h * 512 + kd * 128: h * 512 + (kd + 1) * 128], ap_, ident, [tl_, cb], [tbank])
                              cp(alt(), TOK[c].t[:, :, pc0:pc0 + 128],
                                 tbank.t[:, h * 512:(h + 1) * 512].rearrange("p (k c) -> p k c", c=128), [tbank], [TOK[c]])


                      prep_early(0)
                      for p in range(3):
                          if p + 1 < 3:
                              prep_early(p + 1)
                          prep_late(p)
                      stage(42)
                      SCb = [sb(S, f"SCb{h}", [128, 512], BF16) for h in range(6)]
                      XTb = [sb(S, f"XTb{h}", [128, 128], BF16) for h in range(6)]
                      PP = [[sb(S, f"PP{p}_{i}", [128, 512], BF16) for i in range(2)] for p in range(3)]
                      ACC = [[sb(S, f"ACC{p}_{i}", [128, 256], BF16) for i in range(2)] for p in range(3)]
                      T1b = sb(S, "T1b", [128, 384], BF16); UH = sb(S, "UH", [128, 384]); WhT = sb(S, "WhT", [128, 384], BF16)
                      Ub = sb(S, "Ub", [128, 384], BF16); ysq = sb(S, "ysq", [128, 384]); yn = sb(S, "yn", [128, 384], BF16)
                      st1 = sb(S, "st1", [128, 6]); st2 = sb(S, "st2", [128, 6]); st3 = sb(S, "st3", [128, 6])
                      YT = sb(S, "YT", [128, 3, 512], BF16)
                      for c in range(4):
                          tok = TOK[c]
                          csl = slice(c * 128, (c + 1) * 128)
                          for p in range(3):
                              for e_ in range(2):
                                  pb = 64 * e_; h = 2 * p + e_
                                  b = bank()
                                  ar2 = ART[p].t[pb:pb + 64, c, :, :].rearrange("p a t -> p (a t)")
                                  mm(b.t[:, 0:256], BT[p].t[pb:pb + 64, csl], ar2, True, True, [BT[p], ART[p]], [b])
                                  mm(b.t[:, 256:512], KT[p].t[pb:pb + 64, csl], ar2, True, True, [KT[p], ART[p]], [b])
                                  tt("dve", SCb[h].t[:], b.t[:, :], mask4, ALU.mult, [b, cb], [SCb[h]])
                                  b = bank()
                                  mm(b.t[:, 0:128], ART[p].t[pb:pb + 64, c, 0, :], BT[p].t[pb:pb + 64, csl], True, True, [BT[p], ART[p]], [b])
                                  tt("dve", XTb[h].t[:], b.t[:, 0:128], maskL, ALU.mult, [b, cb], [XTb[h]])
                          stage(43)
                          b = bank()
                          for h in range(6):
                              mm(b.t[:, h * 64:(h + 1) * 64], SCb[h].t[:, 256:384], tok.t[:, 3, h * 64:(h + 1) * 64], True, True, [SCb[h], tok], [b])
                          cp("act", T1b.t[:], b.t[:, 0:384], [b], [T1b])
                          cur = [0, 0, 0]
                          for p in range(3):
                              for e_ in range(2):
                                  h = 2 * p + e_
                                  tt("pool", ACC[p][0].t[:, e_ * 128:(e_ + 1) * 128], SCb[h].t[:, 0:128], ident, ALU.add, [SCb[h], cb], [ACC[p][0]])
                          for k in range(1, 7):
                              i0 = (k - 1) % 2; i1 = k % 2
                              bq = []
                              for p in range(3):
                                  b = bank(); bq.append(b)
                                  for e_ in range(2):
                                      h = 2 * p + e_
                                      if k == 1:
                                          Pm, PTm, rds = SCb[h].t[:, 0:128], XTb[h].t[:], [SCb[h], XTb[h]]
                                      else:
                                          Pm = PP[p][i0].t[:, e_ * 256: e_ * 256 + 128]; PTm = PP[p][i0].t[:, e_ * 256 + 128: e_ * 256 + 256]
                                          rds = [PP[p][i0]]
                                      if k < 6:
                                          mm(b.t[:, e_ * 256: e_ * 256 + 128], PTm, Pm, True, True, rds, [b])
                                      mm(b.t[:, e_ * 256 + 128: e_ * 256 + 256], Pm, PTm, True, True, rds, [b])
                              for p in range(3):
                                  b = bq[p]
                                  if k < 6:
                                      cp(alt(), PP[p][i1].t[:], b.t[:, :], [b], [PP[p][i1]])
                                  else:
                                      for e_ in range(2):
                                          cp(alt(), PP[p][i1].t[:, e_ * 256 + 128: e_ * 256 + 256], b.t[:, e_ * 256 + 128: e_ * 256 + 256], [b], [PP[p][i1]])
                              bq2 = []
                              for p in range(3):
                                  b2 = bank(); bq2.append(b2)
                                  for e_ in range(2):
                                      mm(b2.t[:, e_ * 128:(e_ + 1) * 128], PP[p][i1].t[:, e_ * 256 + 128: e_ * 256 + 256],
                                         ACC[p][i0].t[:, e_ * 128:(e_ + 1) * 128], True, True, [PP[p][i1], ACC[p][i0]], [b2])
                              for p in range(3):
                                  tt("dve", ACC[p][i1].t[:], ACC[p][i0].t[:], bq2[p].t[:, 0:256], ALU.add, [ACC[p][i0], bq2[p]], [ACC[p][i1]])
                          stage(44)
                          NTf = [ACC[p][0] for p in range(3)]
                          b = bank()
                          for h in range(6):
                              p, e_ = h // 2, h % 2
                              mm(b.t[64 * e_:64 * e_ + 64, p * 128:(p + 1) * 128], tok.t[:, 0, h * 64:(h + 1) * 64], NTf[p].t[:, e_ * 128:(e_ + 1) * 128],
                                 True, True, [tok, NTf[p]], [b])
                          cp("act", WhT.t[:], b.t[:, 0:384], [b], [WhT])
                          b = bank()
                          for h in range(6):
                              p, e_ = h // 2, h % 2
                              mm(b.t[:, h * 64:(h + 1) * 64], NTf[p].t[:, e_ * 128:(e_ + 1) * 128], T1b.t[:, h * 64:(h + 1) * 64], True, True, [NTf[p], T1b], [b])
                          cp("act", UH.t[:], b.t[:, 0:384], [b], [UH])
                          stage(45)
                          so = STb[stcur]; sn = STb[1 - stcur]
                          b = bank()
                          for p in range(3):
                              mm(b.t[:, p * 128:(p + 1) * 128], WhT.t[:, p * 128:(p + 1) * 128], so.t[:, p * 128:(p + 1) * 128], True, True, [WhT, so], [b])
                          tt("dve", Ub.t[:], b.t[:, 0:384], UH.t[:], ALU.add, [b, UH], [Ub])
                          stage(451)
                          yb = bank()
                          for h in range(6):
                              p, e_ = h // 2, h % 2; pb = 64 * e_
                              hs = slice(h * 64, (h + 1) * 64)
                              if e_ == 0:
                                  mm(yb.t[:, p * 128:(p + 1) * 128], ART[p].t[:, c, 1, :], so.t[:, p * 128:(p + 1) * 128], True, False, [ART[p], so], [yb])
                              mm(yb.t[:, hs], SCb[h].t[:, 128:256], Ub.t[:, hs], False, False, [SCb[h], Ub], [yb])
                              mm(yb.t[:, hs], SCb[h].t[:, 384:512], tok.t[:, 3, hs], False, True, [SCb[h], tok], [yb])
                          stage(452)
                          sbk = bank()
                          for h in range(6):
                              p, e_ = h // 2, h % 2; pb = 64 * e_
                              hs = slice(h * 64, (h + 1) * 64)
                              mm(sbk.t[pb:pb + 64, hs], tok.t[:, 1, hs], Ub.t[:, hs], True, False, [tok, Ub], [sbk])
                              mm(sbk.t[pb:pb + 64, hs], tok.t[:, 2, hs], tok.t[:, 3, hs], False, True, [tok], [sbk])
                          stage(453)
                          for h in range(6):
                              p, e_ = h // 2, h % 2; pb = 64 * e_
                              hs = slice(h * 64, (h + 1) * 64)
                              stt("dve", ST.t[pb:pb + 64, hs], ST.t[pb:pb + 64, hs], GC[p].t[pb:pb + 64, c:c + 1], sbk.t[pb:pb + 64, hs],
                                  ALU.mult, ALU.add, [ST, GC[p], sbk], [ST])
                          cp("act", sn.t[:], ST.t[:], [ST], [sn])
                          stcur = 1 - stcur
                          stage(46)
                          y3 = yb.t[:, 0:384].rearrange("p (h i) -> p h i", i=64)
                          P.op("dve", (lambda o_, i_: lambda e: e.tensor_reduce(out=o_, in_=i_, axis=AX.X, op=ALU.add))(st1.t[:], y3), R([yb]), R([st1]))
                          act(ysq.t[:], yb.t[:, 0:384], AF.Square, [yb], [ysq])
                          P.op("dve", (lambda o_, i_: lambda e: e.tensor_reduce(out=o_, in_=i_, axis=AX.X, op=ALU.add))(
                              st2.t[:], ysq.t[:].rearrange("p (h i) -> p h i", i=64)), R([ysq]), R([st2]))
                          ts("dve", st1.t[:], st1.t[:], 1.0 / 64, None, ALU.mult, None, [st1], [st1])
                          tt("dve", st3.t[:], st1.t[:], st1.t[:], ALU.mult, [st1], [st3])
                          stt("dve", st2.t[:], st2.t[:], 1.0 / 64, st3.t[:], ALU.mult, ALU.subtract, [st2, st3], [st2])
                          act(st2.t[:], st2.t[:], AF.Ln, [st2], [st2], bias=64e-5)
                          act(st2.t[:], st2.t[:], AF.Exp, [st2], [st2], scale=-0.5)
                          for h in range(6):
                              hs = slice(h * 64, (h + 1) * 64)
                              ts("dve", yn.t[:, hs], yb.t[:, hs], st1.t[:, h:h + 1], st2.t[:, h:h + 1], ALU.subtract, ALU.mult, [yb, st1, st2], [yn])
                          stage(47)
                          hh = c % 2
                          for p in range(3):
                              tr(tbank.t[:, hh * 512 + p * 128: hh * 512 + (p + 1) * 128], yn.t[:, p * 128:(p + 1) * 128], ident, [yn, cb], [tbank])
                          cp("act", YT.t[:, :, csl], tbank.t[:, hh * 512: hh * 512 + 384].rearrange("p (k c) -> p k c", c=128), [tbank], [YT])
                      for p in range(3):
                          act(t2.t[:], YT.t[:, p, :], AF.Identity, [YT, vec], [t2], scale=vcol(l, "lnx_w", p), bias=vcol(l, "lnx_b", p))
                          tt("dve", t2.t[:], t2.t[:], bonT[p].t[:], ALU.add, [t2, bonT[p]], [t2])
                          tt("dve", mixT[p].t[:], t2.t[:], gT[p].t[:], ALU.mult, [t2, gT[p]], [mixT[p]])
                      P.barrier(COMPUTE, chans=[c_vf] + c_vfl)

                  dump(mixT[0], mixT[0].t[:], 8, bf=True); dump(mixT[1], mixT[1].t[:], 9, bf=True); dump(mixT[2], mixT[2].t[:], 10, bf=True)
                  stage(5)
                  with ExitStack() as S:
                      xt = [sb(S, f"xa{c}", [128, 512]) for c in range(8)]
                      for c in range(8):
                          ev_ = dma("pool", xt[c].t[:], x_src[c * 128:(c + 1) * 128, t0:t0 + TT], c_x, x_src_r, [xt[c]])
                          if c == 7 and ev_ is not None:
                              for c2 in range(8):
                                  xt[c2].r.w = ev_
                      for g in range(2):
                          s_, view = load_w(Wout, 8, 0, [(g * 512, 512)])
                          for j in range(4):
                              dt_ = g * 4 + j
                              b = bank()
                              for kc in range(8):
                                  mm(b.t[:, :], view[:, kc, j * 128:(j + 1) * 128], mixT[kc].t[:], kc == 0, kc == 7, [s_, mixT[kc]], [b])
                              tt("dve", xt[dt_].t[:], xt[dt_].t[:], b.t[:, :], ALU.add, [xt[dt_], b], [xt[dt_]])
                      rmsnorm(S, xt, "ln2_g")
                      graw = [sb(S, f"graw{i}", [128, 514]) for i in range(2)]
                      vraw = [sb(S, f"vraw{i}", [128, 514]) for i in range(2)]
                      cg = [sb(S, f"cg{i}", [128, 512]) for i in range(2)]
                      cv = [sb(S, f"cv{i}", [128, 512]) for i in range(2)]
                      gv = [sb(S, f"gv{i}", [128, 512], BF16) for i in range(11)]
                      nf = 0
                      for half in range(2):
                          for (g0, ng) in ((0, 4), (4, 4), (8, 3)):
                              ctg = half * 11 + g0
                              sg_, vg = load_w(Wup, 8, 0, [(ctg * 128, ng * 128)])
                              sv_, vv = load_w(Wup, 8, 0, [((22 + ctg) * 128, ng * 128)])
                              for j in range(ng):
                                  i2 = nf % 2; nf += 1
                                  ct = ctg + j
                                  res = []
                                  for (sl_, vw_, cti, rawt, cout) in ((sg_, vg, ct, graw[i2], cg[i2]), (sv_, vv, 22 + ct, vraw[i2], cv[i2])):
                                      b = bank()
                                      for kc in range(8):
                                          mm(b.t[:, :], vw_[:, kc, j * 128:(j + 1) * 128], hT[kc].t[:], kc == 0, kc == 7, [sl_, hT[kc]], [b])
                                      cp("act", rawt.t[:, 2:514], b.t[:, :], [b], [rawt])
                                      cp("pool", rawt.t[:, 0:2], chalo.t[:, cti, :], [chalo], [rawt])
                                      cp("pool", chalo.t[:, cti, :], rawt.t[:, 512:514], [rawt], [chalo])
                                      act(cout.t[:], b.t[:, :], AF.Identity, [b, vec], [cout], scale=vcol(l, "conv_w", 2 * 44 + cti), bias=vcol(l, "conv_b", cti))
                                      stt("dve", cout.t[:], rawt.t[:, 1:513], vcol(l, "conv_w", 44 + cti), cout.t[:], ALU.mult, ALU.add, [rawt, vec, cout], [cout])
                                      stt("dve", cout.t[:], rawt.t[:, 0:512], vcol(l, "conv_w", cti), cout.t[:], ALU.mult, ALU.add, [rawt, vec, cout], [cout])
                                  act(cg[i2].t[:], cg[i2].t[:], AF.Silu, [cg[i2]], [cg[i2]])
                                  tt("pool", gv[g0 + j].t[:], cg[i2].t[:], cv[i2].t[:], ALU.mult, [cg[i2], cv[i2]], [gv[g0 + j]])
                          for dq in range(4):
                              s_, view = load_w(Wdn, 11, half * 11, [(dq * 256, 256)])
                              for j in range(2):
                                  dt_ = dq * 2 + j
                                  b = bank()
                                  for kc in range(11):
                                      mm(b.t[:, :], view[:, kc, j * 128:(j + 1) * 128], gv[kc].t[:], kc == 0, kc == 10, [s_, gv[kc]], [b])
                                  tt("dve", xt[dt_].t[:], xt[dt_].t[:], b.t[:, :], ALU.add, [xt[dt_], b], [xt[dt_]])
                      for c in range(8):
                          dma("pool", x_dst[c * 128:(c + 1) * 128, t0:t0 + TT], xt[c].t[:], c_o, [xt[c]], x_dst_r, nowaw=True)
                      P.barrier(COMPUTE, chans=[c_o, c_x])

        except _Stop:
            pass
        P.dead = False
        P.barrier(ALLQ, chans=P.chans)
        P.emit()
    return nc, P


_CACHE = {}


def run(inputs, T, depth, n_cores, stop=0):
    vec, sm = pack_params(inputs, depth)
    cbv, pcv = pack_consts()
    key = (T, depth, stop)
    if key not in _CACHE:
        _CACHE[key] = build(T, depth, stop)[0]
    nc = _CACHE[key]
    x = np.asarray(inputs["x"], np.float32)
    shared = {
        "w_in": np.ascontiguousarray(np.asarray(inputs["w_in"], np.float32)[:depth]),
        "w_out": np.ascontiguousarray(np.asarray(inputs["w_out"], np.float32)[:depth]),
        "w_up": np.ascontiguousarray(np.asarray(inputs["w_up"], np.float32)[:depth]),
        "w_down": np.ascontiguousarray(np.asarray(inputs["w_down"], np.float32)[:depth]),
        "vec": vec, "sm": sm, "cb": cbv, "pc": pcv,
    }
    in_maps = []
    for b in range(n_cores):
        m = dict(shared)
        m["xT"] = np.ascontiguousarray(x[b, :T].T)
        in_maps.append(m)
    res = run_bass_kernel_spmd(nc, in_maps, core_ids=list(range(n_cores)))
    if stop:
        return res.results[0]
    out = np.stack([np.ascontiguousarray(res.results[b]["yT"].T) for b in range(n_cores)], 0)
    return out.astype(np.float32)


def kernel(**inputs):
    inputs = {k: np.asarray(v) for k, v in inputs.items()}
    x = inputs["x"]
    return run(inputs, x.shape[1], 2, x.shape[0])
```
